# Optimizing a Trainium2 kernel written in Bass

```python
import jax, jax.numpy as jnp
from jax import lax
import numpy as np


D_MODEL = 1024
BATCH = 8
SEQ = 2048
DEPTH = 2
DEC_BATCH = 128
DEC_SEQ = 1
PAST_LEN = 16384
PAGE_SIZE = 128

RW_HEAD = 64
RW_HEADS = D_MODEL // RW_HEAD
D_RW = RW_HEADS * RW_HEAD
R_DECAY = 64
R_AAA = 64
R_GATE = 128
GN_EPS = 64e-5
D_LRU = D_MODEL
LRU_BLOCKS = 16
LRU_BS = D_LRU // LRU_BLOCKS
CONV_W = 4
LRU_C = 8.0
N_MEM = 256
XA_HEADS = 4
XA_HEAD = D_MODEL // XA_HEADS
D_FF = 4 * D_MODEL
ALPHA = (2 * DEPTH) ** 0.25
BETA = (8 * DEPTH) ** -0.25
LN_EPS = 1e-5

C_R = 0
C_K = C_R + D_RW
C_V = C_K + D_RW
C_W = C_V + D_RW
C_A = C_W + R_DECAY
C_G = C_A + R_AAA
N_SHIFT = C_G + R_GATE
C_U = N_SHIFT
C_Y = C_U + D_LRU
C_GATE = C_Y + D_LRU
N_IN = C_GATE + 2 * D_MODEL

kernel_name = 'rwkv7_rglru_gated_hybrid_step'


def _layer_norm(x, g, b):
    xf = x.astype(jnp.float32)
    mu = jnp.mean(xf, -1, keepdims=True)
    var = jnp.mean(jnp.square(xf - mu), -1, keepdims=True)
    return ((xf - mu) * lax.rsqrt(var + LN_EPS)).astype(x.dtype) * g + b


def _rwkv7_scan(r, w, k, v, a, b, s0):
    def step(s, inp):
        r_t, w_t, k_t, v_t, a_t, b_t = inp
        sa = jnp.einsum('bhvk,bhk->bhv', s, a_t)
        s = s * w_t[:, :, None, :] + sa[..., None] * b_t[:, :, None, :] + v_t[..., None] * k_t[:, :, None, :]
        y = jnp.einsum('bhvk,bhk->bhv', s, r_t)
        return s, y
    xs = (jnp.moveaxis(r, 1, 0), jnp.moveaxis(w, 1, 0), jnp.moveaxis(k, 1, 0),
          jnp.moveaxis(v, 1, 0), jnp.moveaxis(a, 1, 0), jnp.moveaxis(b, 1, 0))
    s, ys = lax.scan(step, s0, xs)
    return jnp.moveaxis(ys, 0, 1), s


def _rwkv7_branch(p_rw, p_prev, s0, l, W):
    f32 = jnp.float32
    ps = p_rw + W['mu_shift'][l] * (p_prev - p_rw)
    r = ps[..., C_R:C_K]
    k = ps[..., C_K:C_V]
    v = ps[..., C_V:C_W]
    xw = ps[..., C_W:C_A]
    xa = ps[..., C_A:C_G]
    xg = ps[..., C_G:N_SHIFT]
    w = -jax.nn.softplus(-(W['rw_w0'][l] + jnp.tanh(xw) @ W['rw_w2'][l])) - 0.5
    a = jax.nn.sigmoid(W['rw_a0'][l] + xa @ W['rw_a2'][l])
    g = jax.nn.sigmoid(xg) @ W['rw_g2'][l]
    hs = p_rw.shape[:-1] + (RW_HEADS, RW_HEAD)
    r = r.reshape(hs).astype(f32)
    k = k.reshape(hs).astype(f32)
    v = v.reshape(hs).astype(f32)
    a = a.reshape(hs).astype(f32)
    decay = jnp.exp(-jnp.exp(w.astype(f32))).reshape(hs)
    kk = k * W['rw_k_k'][l]
    kk = kk * lax.rsqrt(jnp.maximum(jnp.sum(kk * kk, -1, keepdims=True), 1e-24))
    k = k * (1.0 + (a - 1.0) * W['rw_k_a'][l])
    y, s = _rwkv7_scan(r, decay, k, v, -kk, kk * a, s0.astype(f32))
    mu = jnp.mean(y, -1, keepdims=True)
    var = jnp.mean(jnp.square(y - mu), -1, keepdims=True)
    y = (y - mu) * lax.rsqrt(var + GN_EPS) * W['rw_lnx_g'][l] + W['rw_lnx_b'][l]
    y = y + jnp.sum(r * k * W['rw_r_k'][l], -1, keepdims=True) * v
    y = y.reshape(p_rw.shape[:-1] + (D_RW,)).astype(p_rw.dtype) * g
    return y @ W['rw_proj'][l], s.astype(s0.dtype)


def _linear_scan(a, b, h0):
    b = b.at[:, 0].add(a[:, 0] * h0)
    def comb(left, right):
        al, bl = left
        ar, br = right
        return al * ar, ar * bl + br
    _, h = lax.associative_scan(comb, (a, b), axis=1)
    return h


def _rglru_branch(u, y_in, buf, h0, l, W):
    f32 = jnp.float32
    T = u.shape[1]
    ext = jnp.concatenate([buf.astype(u.dtype), u], axis=1)
    cw = W['lru_conv_w'][l]
    xc = W['lru_conv_b'][l] + ext[:, 0:T] * cw[0]
    for j in range(1, CONV_W):
        xc = xc + ext[:, j:j + T] * cw[j]
    new_buf = ext[:, T:]
    blk = xc.reshape(xc.shape[:-1] + (LRU_BLOCKS, LRU_BS))
    gr = jax.nn.sigmoid(jnp.einsum('btnd,nde->btne', blk, W['lru_wa'][l]).reshape(xc.shape) + W['lru_ba'][l])
    gi = jax.nn.sigmoid(jnp.einsum('btnd,nde->btne', blk, W['lru_wx'][l]).reshape(xc.shape) + W['lru_bx'][l])
    log_a = (-LRU_C * jax.nn.softplus(-W['lru_lambda'][l]) * gr).astype(f32)
    a = jnp.exp(log_a)
    bterm = jnp.sqrt(-jnp.expm1(2.0 * log_a)) * (gi * xc).astype(f32)
    h = _linear_scan(a, bterm, h0.astype(f32))
    y = h.astype(u.dtype) * jax.nn.gelu(y_in)
    return y @ W['lru_proj'][l], h[:, -1].astype(h0.dtype), new_buf.astype(buf.dtype)


def _mem_attention(x, k, v, wq, wo):
    q = (x @ wq).reshape(x.shape[:-1] + (XA_HEADS, XA_HEAD))
    s = jnp.einsum('bthd,bmhd->bhtm', q, k.astype(q.dtype)).astype(jnp.float32) * (XA_HEAD ** -0.5)
    p = jax.nn.softmax(s, axis=-1).astype(x.dtype)
    o = jnp.einsum('bhtm,bmhd->bthd', p, v.astype(x.dtype)).reshape(x.shape)
    return o @ wo


def _trunk(x, mem_k, mem_v, s_rw, s_shift, s_h, s_conv, W):
    o_rw_st, o_shift_st, o_h_st, o_conv_st = [], [], [], []
    for l in range(DEPTH):
        p = x @ W['w_in'][l]
        p_rw = p[..., :N_SHIFT]
        p_prev = jnp.concatenate([s_shift[l][:, None].astype(p.dtype), p_rw[:, :-1]], axis=1)
        o_rw, s_rw_l = _rwkv7_branch(p_rw, p_prev, s_rw[l], l, W)
        o_lru, h_l, conv_l = _rglru_branch(p[..., C_U:C_Y], p[..., C_Y:C_GATE], s_conv[l], s_h[l], l, W)
        gates = jax.nn.sigmoid(p[..., C_GATE:].reshape(p.shape[:-1] + (2, D_MODEL)) + W['mix_gate_b'][l])
        mix = (gates[..., 0, :] * o_rw + gates[..., 1, :] * o_lru) @ W['w_out_mix'][l]
        x = _layer_norm(ALPHA * x + mix, W['ln1_g'][l], W['ln1_b'][l])
        xa = _mem_attention(x, mem_k[l], mem_v[l], W['xa_wq'][l], W['xa_wo'][l])
        x = _layer_norm(ALPHA * x + xa, W['ln2_g'][l], W['ln2_b'][l])
        hdn = jnp.square(jax.nn.relu(x @ W['mlp_up'][l]))
        x = _layer_norm(ALPHA * x + hdn @ W['mlp_down'][l], W['ln3_g'][l], W['ln3_b'][l])
        o_rw_st.append(s_rw_l)
        o_shift_st.append(p_rw[:, -1].astype(s_shift.dtype))
        o_h_st.append(h_l)
        o_conv_st.append(conv_l)
    return x, (jnp.stack(o_rw_st, 0), jnp.stack(o_shift_st, 0), jnp.stack(o_h_st, 0), jnp.stack(o_conv_st, 0))


def setup_inputs(seed: int = 0) -> dict:
    key = jax.random.key(seed)
    ks = iter(jax.random.split(key, 64))
    def nrm(shape, scale):
        return jax.random.normal(next(ks), shape, jnp.float32) * scale
    def unif(shape, lo, hi):
        return jax.random.uniform(next(ks), shape, jnp.float32, lo, hi)
    L = DEPTH
    lam_s = unif((L, D_LRU), 0.9, 0.999) ** (1.0 / LRU_C)
    lru_lambda = jnp.log(lam_s) - jnp.log1p(-lam_s)
    return {
        'x_prompt': nrm((BATCH, SEQ, D_MODEL), 1.0),
        'x_sample': nrm((DEC_BATCH, DEC_SEQ, D_MODEL), 1.0),
        'mem_prompt': nrm((BATCH, N_MEM, D_MODEL), 1.0),
        'cache_mem_k': nrm((L, DEC_BATCH, N_MEM, XA_HEADS, XA_HEAD), 1.0),
        'cache_mem_v': nrm((L, DEC_BATCH, N_MEM, XA_HEADS, XA_HEAD), BETA),
        'state_rwkv': nrm((L, DEC_BATCH, RW_HEADS, RW_HEAD, RW_HEAD), 0.3),
        'state_rwkv_shift': nrm((L, DEC_BATCH, N_SHIFT), 1.0),
        'state_lru_h': nrm((L, DEC_BATCH, D_LRU), 0.5),
        'state_lru_conv': nrm((L, DEC_BATCH, CONV_W - 1, D_LRU), 1.0),
        'w_in': nrm((L, D_MODEL, N_IN), D_MODEL ** -0.5),
        'mu_shift': unif((L, N_SHIFT), 0.0, 1.0),
        'rw_w0': unif((L, D_RW), -6.0, -1.0),
        'rw_w2': nrm((L, R_DECAY, D_RW), 0.5 * R_DECAY ** -0.5),
        'rw_a0': nrm((L, D_RW), 0.5),
        'rw_a2': nrm((L, R_AAA, D_RW), 0.5 * R_AAA ** -0.5),
        'rw_g2': nrm((L, R_GATE, D_RW), R_GATE ** -0.5),
        'rw_k_k': 0.85 + nrm((L, RW_HEADS, RW_HEAD), 0.05),
        'rw_k_a': 1.0 + nrm((L, RW_HEADS, RW_HEAD), 0.05),
        'rw_r_k': nrm((L, RW_HEADS, RW_HEAD), 0.1),
        'rw_lnx_g': 1.0 + nrm((L, RW_HEADS, RW_HEAD), 0.05),
        'rw_lnx_b': nrm((L, RW_HEADS, RW_HEAD), 0.01),
        'rw_proj': nrm((L, D_RW, D_MODEL), D_RW ** -0.5),
        'lru_conv_w': nrm((L, CONV_W, D_LRU), CONV_W ** -0.5),
        'lru_conv_b': nrm((L, D_LRU), 0.01),
        'lru_wa': nrm((L, LRU_BLOCKS, LRU_BS, LRU_BS), LRU_BS ** -0.5),
        'lru_ba': nrm((L, D_LRU), 0.01),
        'lru_wx': nrm((L, LRU_BLOCKS, LRU_BS, LRU_BS), LRU_BS ** -0.5),
        'lru_bx': nrm((L, D_LRU), 0.01),
        'lru_lambda': lru_lambda,
        'lru_proj': nrm((L, D_LRU, D_MODEL), D_LRU ** -0.5),
        'mix_gate_b': nrm((L, 2, D_MODEL), 0.01),
        'w_out_mix': nrm((L, D_MODEL, D_MODEL), BETA * D_MODEL ** -0.5),
        'ln1_g': 1.0 + nrm((L, D_MODEL), 0.05),
        'ln1_b': nrm((L, D_MODEL), 0.01),
        'xa_wq': nrm((L, D_MODEL, D_MODEL), D_MODEL ** -0.5),
        'xa_wk': nrm((L, D_MODEL, D_MODEL), D_MODEL ** -0.5),
        'xa_wv': nrm((L, D_MODEL, D_MODEL), BETA * D_MODEL ** -0.5),
        'xa_wo': nrm((L, D_MODEL, D_MODEL), BETA * D_MODEL ** -0.5),
        'ln2_g': 1.0 + nrm((L, D_MODEL), 0.05),
        'ln2_b': nrm((L, D_MODEL), 0.01),
        'mlp_up': nrm((L, D_MODEL, D_FF), D_MODEL ** -0.5),
        'mlp_down': nrm((L, D_FF, D_MODEL), BETA * D_FF ** -0.5),
        'ln3_g': 1.0 + nrm((L, D_MODEL), 0.05),
        'ln3_b': nrm((L, D_MODEL), 0.01),
    }


def reference(x_prompt, x_sample, mem_prompt, cache_mem_k, cache_mem_v, state_rwkv, state_rwkv_shift,
              state_lru_h, state_lru_conv, w_in, mu_shift, rw_w0, rw_w2, rw_a0, rw_a2, rw_g2, rw_k_k,
              rw_k_a, rw_r_k, rw_lnx_g, rw_lnx_b, rw_proj, lru_conv_w, lru_conv_b, lru_wa, lru_ba, lru_wx,
              lru_bx, lru_lambda, lru_proj, mix_gate_b, w_out_mix, ln1_g, ln1_b, xa_wq, xa_wk, xa_wv,
              xa_wo, ln2_g, ln2_b, mlp_up, mlp_down, ln3_g, ln3_b):
    W = dict(w_in=w_in, mu_shift=mu_shift, rw_w0=rw_w0, rw_w2=rw_w2, rw_a0=rw_a0, rw_a2=rw_a2,
             rw_g2=rw_g2, rw_k_k=rw_k_k, rw_k_a=rw_k_a, rw_r_k=rw_r_k, rw_lnx_g=rw_lnx_g,
             rw_lnx_b=rw_lnx_b, rw_proj=rw_proj, lru_conv_w=lru_conv_w, lru_conv_b=lru_conv_b,
             lru_wa=lru_wa, lru_ba=lru_ba, lru_wx=lru_wx, lru_bx=lru_bx, lru_lambda=lru_lambda,
             lru_proj=lru_proj, mix_gate_b=mix_gate_b, w_out_mix=w_out_mix, ln1_g=ln1_g, ln1_b=ln1_b,
             xa_wq=xa_wq, xa_wo=xa_wo, ln2_g=ln2_g, ln2_b=ln2_b, mlp_up=mlp_up, mlp_down=mlp_down,
             ln3_g=ln3_g, ln3_b=ln3_b)
    bp = x_prompt.shape[0]
    dt = x_prompt.dtype
    mem_k_p = jnp.einsum('bmd,lde->lbme', mem_prompt, xa_wk).reshape(DEPTH, bp, N_MEM, XA_HEADS, XA_HEAD)
    mem_v_p = jnp.einsum('bmd,lde->lbme', mem_prompt, xa_wv).reshape(DEPTH, bp, N_MEM, XA_HEADS, XA_HEAD)
    z_rw = jnp.zeros((DEPTH, bp, RW_HEADS, RW_HEAD, RW_HEAD), dt)
    z_shift = jnp.zeros((DEPTH, bp, N_SHIFT), dt)
    z_h = jnp.zeros((DEPTH, bp, D_LRU), dt)
    z_conv = jnp.zeros((DEPTH, bp, CONV_W - 1, D_LRU), dt)
    y_prompt, (p_rw, p_shift, p_h, p_conv) = _trunk(x_prompt, mem_k_p, mem_v_p, z_rw, z_shift, z_h, z_conv, W)
    y_sample, (s_rw, s_shift, s_h, s_conv) = _trunk(x_sample, cache_mem_k, cache_mem_v, state_rwkv,
                                                    state_rwkv_shift, state_lru_h, state_lru_conv, W)
    return (y_prompt, y_sample, p_rw, p_shift, p_h, p_conv, mem_k_p, mem_v_p, s_rw, s_shift, s_h, s_conv)
```

```python
import math
import itertools
import numpy as np
from contextlib import ExitStack
import concourse.bass as bass
import concourse.mybir as mybir
from concourse.bass_utils import run_bass_kernel_spmd

F32 = mybir.dt.float32
BF16 = mybir.dt.bfloat16
AF = mybir.ActivationFunctionType
ALU = mybir.AluOpType
AX = mybir.AxisListType

COMPUTE = ('pe', 'act', 'dve', 'pool')
ALLQ = ('pe', 'act', 'dve', 'pool', 'sp')

D = 1024
KC = 8
SEQ = 2048
NS = 16
NT = 512
NIN = 7424
C_R, C_K, C_V, C_W, C_A, C_G = 0, 1024, 2048, 3072, 3136, 3200
NSH = 3328
C_U, C_Y, C_GATE = 3328, 4352, 5376
DEPTH = 2
ALPHA = float((2 * DEPTH) ** 0.25)
LN_EPS = 1e-5
GN_EPS = 64e-5
C0 = float(math.exp(-0.5))
NRL = 27
NPC = 30
CH = 64


class Sched:
    def __init__(self, nc, stack, n_dma_sems=40):
        self.nc = nc
        self.streams = {e: [] for e in ALLQ}
        self.sem = {e: stack.enter_context(nc.semaphore('s_' + e)) for e in COMPUTE}
        self.dsem = [stack.enter_context(nc.semaphore('d%d' % i)) for i in range(n_dma_sems)]
        self.reset()

    def reset(self):
        self.streams = {e: [] for e in ALLQ}
        self.cnt = {e: 0 for e in COMPUTE}
        self.dcnt = [0] * len(self.dsem)
        self.drr = 0
        self.seen = {q: {} for q in ALLQ}
        self.snap = {}
        self.lastw = {}
        self.readers = {}
        self.lastacc = {}
        self.n_wait = 0
        self.dry = False

    def _semh(self, key):
        return self.sem[key] if isinstance(key, str) else self.dsem[key]

    def _need(self, q, ev, waits):
        key, val = ev
        if self.seen[q].get(key, 0) >= val:
            return
        if key == q and q == 'pe':
            return
        if waits.get(key, 0) < val:
            waits[key] = val

    def _deps(self, q, reads, writes):
        waits = {}
        for k in reads:
            ev = self.lastw.get(k)
            if ev is not None:
                self._need(q, ev, waits)
        for k in writes:
            ev = self.lastw.get(k)
            if ev is not None:
                self._need(q, ev, waits)
            for ev in self.readers.get(k, ()):
                self._need(q, ev, waits)
        return waits

    def _apply_waits(self, q, waits):
        out = []
        for key, val in waits.items():
            if self.seen[q].get(key, 0) >= val:
                continue
            out.append((key, val))
            self.seen[q][key] = val
            sn = self.snap.get((key, val))
            if sn is not None:
                for e, v in zip(COMPUTE, sn):
                    if e == q:
                        continue
                    if self.seen[q].get(e, 0) < v:
                        self.seen[q][e] = v
        self.n_wait += len(out)
        return [(self._semh(k), v) for k, v in out]

    def _record(self, ev, reads, writes):
        for k in writes:
            self.lastw[k] = ev
            self.readers[k] = []
        for k in reads:
            self.readers.setdefault(k, []).append(ev)

    def op(self, q, fn, reads=(), writes=()):
        if self.dry:
            return None
        waits = self._deps(q, reads, writes)
        banks = set()
        for kk_ in reads:
            if isinstance(kk_, tuple) and kk_[0] == 'ps':
                banks.add(kk_[1])
        for kk_ in writes:
            if isinstance(kk_, tuple) and kk_[0] == 'ps':
                banks.add(kk_[1])
        for b_ in banks:
            ev0 = self.lastacc.get(b_)
            if ev0 is not None and ev0[0] != q:
                self._need(q, ev0, waits)
        wl = self._apply_waits(q, waits)
        self.cnt[q] += 1
        ev = (q, self.cnt[q])
        for b_ in banks:
            self.lastacc[b_] = ev
        self.snap[ev] = tuple(self.seen[q].get(e, 0) for e in COMPUTE)
        sem = self.sem[q]

        def emit(e, fn=fn, wl=wl, sem=sem):
            for s, v in wl:
                e.wait_ge(s, v)
            fn(e).then_inc(sem, 1)
        self.streams[q].append(emit)
        self._record(ev, reads, writes)
        return ev

    def dma(self, q, out, in_, reads=(), writes=(), **kw):
        if self.dry:
            return None
        waits = self._deps(q, reads, writes)
        i = self.drr
        self.drr = (self.drr + 1) % len(self.dsem)
        if self.dcnt[i] > 0:
            self._need(q, (i, 16 * self.dcnt[i]), waits)
        wl = self._apply_waits(q, waits)
        self.dcnt[i] += 1
        ev = (i, 16 * self.dcnt[i])
        self.snap[ev] = tuple(self.seen[q].get(e, 0) for e in COMPUTE)
        sem = self.dsem[i]

        def emit(e, out=out, in_=in_, wl=wl, sem=sem, kw=kw):
            for s, v in wl:
                e.wait_ge(s, v)
            e.dma_start(out=out, in_=in_, **kw).then_inc(sem, 16)
        self.streams[q].append(emit)
        self._record(ev, reads, writes)
        return ev

    def mark(self, name):
        if self.dry:
            return
        if not hasattr(self, 'marks'):
            self.marks = []
        self.marks.append((name, dict(self.cnt)))

    def barrier(self):
        if self.dry:
            return
        for q in ALLQ:
            waits = {}
            for e in COMPUTE:
                if self.cnt[e] and e != q:
                    self._need(q, (e, self.cnt[e]), waits)
            if q == 'sp':
                for i, c in enumerate(self.dcnt):
                    if c:
                        self._need(q, (i, 16 * c), waits)
            wl = self._apply_waits(q, waits)

            def emit(e, wl=wl):
                for s, v in wl:
                    e.wait_ge(s, v)
            self.streams[q].append(emit)

    def finish(self, q='sp'):
        waits = {}
        for i, c in enumerate(self.dcnt):
            if c:
                self._need(q, (i, 16 * c), waits)
        for e in COMPUTE:
            if self.cnt[e] and e != q:
                self._need(q, (e, self.cnt[e]), waits)
        wl = self._apply_waits(q, waits)

        def emit(e, wl=wl):
            for s, v in wl:
                e.wait_ge(s, v)
        self.streams[q].append(emit)

    def emit_all(self):
        nc = self.nc
        with nc.Block() as block:
            @block.tensor
            def _(e):
                for f in self.streams['pe']:
                    f(e)

            @block.scalar
            def _(e):
                for f in self.streams['act']:
                    f(e)

            @block.vector
            def _(e):
                for f in self.streams['dve']:
                    f(e)

            @block.gpsimd
            def _(e):
                for f in self.streams['pool']:
                    f(e)

            @block.sync
            def _(e):
                for f in self.streams['sp']:
                    f(e)


class T:
    __slots__ = ('ap', 'keys')

    def __init__(self, ap, keys):
        self.ap = ap
        self.keys = tuple(keys)

    def __getitem__(self, idx):
        return T(self.ap[idx], self.keys)

    def v(self, fn):
        return T(fn(self.ap), self.keys)

    def re(self, s, **kw):
        return T(self.ap.rearrange(s, **kw), self.keys)


def _ap(x):
    return x.ap if isinstance(x, T) else x


def _keys(*xs):
    out = []
    for x in xs:
        if isinstance(x, T):
            out.extend(x.keys)
    return out


class KB:
    def __init__(self, S):
        self.S = S

    def mm(self, out, lhsT, rhs, start=True, stop=True):
        o, l, r = _ap(out), _ap(lhsT), _ap(rhs)
        self.S.op('pe', lambda e: e.matmul(o, lhsT=l, rhs=r, start=start, stop=stop),
                  reads=_keys(lhsT, rhs), writes=_keys(out))

    def tr(self, out, in_, ident):
        o, i, d = _ap(out), _ap(in_), _ap(ident)
        self.S.op('pe', lambda e: e.transpose(out=o, in_=i, identity=d), reads=_keys(in_, ident), writes=_keys(out))

    def act(self, out, in_, func, bias=None, scale=None, accum=None):
        o, i = _ap(out), _ap(in_)
        kw = {}
        if bias is not None:
            kw['bias'] = _ap(bias)
        if scale is not None:
            kw['scale'] = _ap(scale)
        if accum is not None:
            kw['accum_out'] = _ap(accum)
        self.S.op('act', lambda e: e.activation(out=o, in_=i, func=func, **kw),
                  reads=_keys(in_, bias, scale), writes=_keys(out, accum))

    def tt(self, out, a, b, op, eng='dve'):
        if eng == 'pool':
            eng = 'dve'
        if eng == 'gp':
            eng = 'pool'
        o, x, y = _ap(out), _ap(a), _ap(b)
        self.S.op(eng, lambda e: e.tensor_tensor(out=o, in0=x, in1=y, op=op), reads=_keys(a, b), writes=_keys(out))

    def ts(self, out, a, s1, op0, s2=None, op1=None, eng='dve'):
        if eng == 'pool':
            eng = 'dve'
        if eng == 'gp':
            eng = 'pool'
        o, x, p, q = _ap(out), _ap(a), _ap(s1), _ap(s2)
        if op1 is None:
            self.S.op(eng, lambda e: e.tensor_scalar(out=o, in0=x, scalar1=p, scalar2=None, op0=op0),
                      reads=_keys(a, s1), writes=_keys(out))
        else:
            self.S.op(eng, lambda e: e.tensor_scalar(out=o, in0=x, scalar1=p, scalar2=q, op0=op0, op1=op1),
                      reads=_keys(a, s1, s2), writes=_keys(out))

    def stt(self, out, a, scalar, b, op0, op1):
        o, x, s, y = _ap(out), _ap(a), _ap(scalar), _ap(b)
        self.S.op('dve', lambda e: e.scalar_tensor_tensor(out=o, in0=x, scalar=s, in1=y, op0=op0, op1=op1),
                  reads=_keys(a, scalar, b), writes=_keys(out))

    def cp(self, out, in_, eng='dve'):
        o, i = _ap(out), _ap(in_)
        if eng == 'pool':
            eng = 'act'
        if eng == 'poolcast':
            eng = 'pool'
        if eng == 'act':
            self.S.op('act', lambda e: e.activation(out=o, in_=i, func=AF.Copy), reads=_keys(in_), writes=_keys(out))
        else:
            self.S.op(eng, lambda e: e.tensor_copy(out=o, in_=i), reads=_keys(in_), writes=_keys(out))

    def scan(self, out, d0, d1, init):
        o, x, y, z = _ap(out), _ap(d0), _ap(d1), _ap(init)
        self.S.op('dve', lambda e: e.tensor_tensor_scan(out=o, data0=x, data1=y, initial=z, op0=ALU.mult, op1=ALU.add),
                  reads=_keys(d0, d1, init), writes=_keys(out))

    def red(self, out, in_, op, axis=AX.X):
        o, i = _ap(out), _ap(in_)
        self.S.op('dve', lambda e: e.tensor_reduce(out=o, in_=i, axis=axis, op=op), reads=_keys(in_), writes=_keys(out))

    def recip(self, out, in_):
        o, i = _ap(out), _ap(in_)
        self.S.op('dve', lambda e: e.reciprocal(out=o, in_=i), reads=_keys(in_), writes=_keys(out))

    def memset(self, t, val, eng='pool'):
        o = _ap(t)
        self.S.op(eng, lambda e: e.memset(o, val), writes=_keys(t))

    def dma(self, out, in_, q='sp', **kw):
        self.S.dma(q, _ap(out), _ap(in_), reads=_keys(in_), writes=_keys(out), **kw)


class WQ:
    def __init__(self, k, stg, ring, la=3):
        self.k = k
        self.stg = stg
        self.ring = ring
        self.la = la
        self.specs = []
        self.collect = True
        self.i = 0
        self.issued = 0

    def start_emit(self):
        self.collect = False
        self.i = 0
        self.issued = 0

    def _issue(self, j):
        src = self.specs[j]
        dst = self.ring[j % len(self.ring)]
        self.k.dma(dst, src, q='pool')

    def next(self, src):
        if self.collect:
            self.specs.append(src)
            return self.ring[0]
        i = self.i
        self.i += 1
        lim = min(i + self.la, len(self.specs) - 1)
        while self.issued <= lim:
            self._issue(self.issued)
            self.issued += 1
        return self.ring[i % len(self.ring)]


class Stop(Exception):
    pass


def build_nc(debug=False, stop_at=None, short=False):
    nc = bass.Bass("TRN2", target_bir_lowering=False)

    def din(name, shape):
        return nc.dram_tensor(name, list(shape), F32, kind="ExternalInput").ap()

    def dout(name, shape):
        return nc.dram_tensor(name, list(shape), F32, kind="ExternalOutput").ap()

    xp = din("xp", [SEQ, D])
    xs = din("xs", [NS, D])
    mem = din("mem", [256, D])
    ck = din("ck", [2, NS, 256, D])
    cv = din("cv", [2, NS, 256, D])
    srw = din("srw", [2, 128, 8192])
    ssh = din("ssh", [2, NS, NSH])
    shh = din("shh", [2, NS, D])
    scv = din("scv", [2, NS, 3, D])
    prow = din("prow", [2 * NRL, D])
    w_in = din("w_in", [2, D, NIN])
    rw_w2 = din("rw_w2", [2, 64, D])
    rw_a2 = din("rw_a2", [2, 64, D])
    rw_g2 = din("rw_g2", [2, 128, D])
    rw_proj = din("rw_proj", [2, D, D])
    lru_wa = din("lru_wa", [2, 16, 64, 64])
    lru_wx = din("lru_wx", [2, 16, 64, 64])
    lru_proj = din("lru_proj", [2, D, D])
    w_out_mix = din("w_out_mix", [2, D, D])
    xa_wq = din("xa_wq", [2, D, D])
    xa_wk = din("xa_wk", [2, D, D])
    xa_wv = din("xa_wv", [2, D, D])
    xa_wo = din("xa_wo", [2, D, D])
    mlp_up = din("mlp_up", [2, D, 4 * D])
    mlp_down = din("mlp_down", [2, 4 * D, D])

    o_yp = dout("o_yp", [SEQ, D])
    o_ys = dout("o_ys", [NS, D])
    o_prw = dout("o_prw", [2, 16, 64, 64])
    o_psh = dout("o_psh", [2, NSH])
    o_ph = dout("o_ph", [2, D])
    o_pcv = dout("o_pcv", [2, 3, D])
    o_mk = dout("o_mk", [2, 256, D])
    o_mv = dout("o_mv", [2, 256, D])
    o_srw = dout("o_srw", [2, 128, 8192])
    o_ssh = dout("o_ssh", [2, NS, NSH])
    o_sh = dout("o_sh", [2, NS, D])
    o_scv = dout("o_scv", [2, NS, 3, D])
    scr = nc.dram_tensor("scr", [8, NS * D], F32, kind="Internal").ap()
    dbg = {}
    if debug:
        dbg['x1'] = dout("dbg_x1", [128, 8, 1040])
        dbg['yb'] = dout("dbg_yb", [128, 8, 1040])

    with ExitStack() as st:
        S = Sched(nc, st)
        k = KB(S)

        def sb(name, shape, dt):
            return st.enter_context(nc.sbuf_tensor(name, list(shape), dt))

        PP = 1040
        X32t = sb("X32", [128, KC, PP], F32)
        XBt = sb("XB", [128, KC, PP], BF16)
        YBt = sb("YB", [128, KC, PP], BF16)
        EBt = sb("EB", [128, KC * PP], BF16)
        LWt = sb("LW", [128, 2, PP], BF16)
        KTt = sb("KT", [128, 2, KC, 256], BF16)
        VNt = sb("VN", [128, 2, 2, D], BF16)
        NB = 20
        WPt = sb("WP", [128, NB, 516], F32)
        RINGt = sb("RING", [128, 10, KC, 128], BF16)
        PARt = sb("PAR", [128, KC, 2, NPC], F32)
        CONt = sb("CON", [128, 8, 128], F32)
        CONBt = sb("CONB", [128, 128], BF16)
        RSTt = sb("RST", [128, NT], F32)
        W2A2t = sb("W2A2", [128, D], BF16)
        G2t = sb("G2", [128, D], BF16)
        WABt = sb("WAB", [128, 2, KC, 128], BF16)
        HSTt = sb("HST", [128, 2, 8, 128], F32)
        HBFt = sb("HBF", [128, 64], BF16)
        HSPt = sb("HSP", [128, 64], F32)
        ZUt = sb("ZU", [128, 2, 64], BF16)
        PCt = sb("PCT", [128, 2, 8], F32)
        HOUTt = sb("HOUT", [64, 128], F32)
        MSKt = sb("MSK", [128, 3, 64], F32)
        IDSt = sb("IDS", [128, 64], BF16)
        CARt = sb("CAR", [128, 2, 40], F32)
        CVCt = sb("CVC", [128, 2, 8, 3], F32)
        SHOt = sb("SHO", [128, 26, 1 + NS], F32)
        HOt = sb("HO", [128, 8, 1 + NS], F32)
        CVOt = sb("CVO", [128, 8, 3, 1 + NS], F32)
        SHSt = sb("SHS", [128, 26, NS], F32)
        HS0t = sb("HS0", [128, 8, NS], F32)
        CSt = sb("CS", [128, 8, 3, NS], F32)
        SVt = sb("SV", [128, 6, 8, NS], F32)
        SV2t = sb("SV2", [128, 6, 128], F32)
        YSt = sb("YS", [128, 8, NS], F32)
        SMALLt = sb("SMALL", [128, 64], F32)

        ps = [st.enter_context(nc.psum_tensor("ps%d" % i, [128, 512], F32)) for i in range(8)]

        def psbank(b):
            return T(ps[b][:], [('ps', b, s) for s in range(4)])

        def psslot(b, s):
            return T(ps[b][:, s * 128:(s + 1) * 128], [('ps', b, s)])

        rr = {'bank': 0, 'slot': 0}

        def nbank():
            m_ = rr.get('nbmod', 8)
            b = rr['bank'] % m_
            rr['bank'] = (b + 1) % m_
            return psbank(b)

        def nslot():
            s = rr['slot']
            rr['slot'] = (s + 1) % 16
            return psslot(4 + s % 4, s // 4)

        def wp(i, n=None):
            t = T(WPt[:, i, :], [('wp', i, s_) for s_ in range(4)])
            return t if n is None else t[:, :n]

        def wpspan(i, cnt):
            return T(WPt[:, i:i + cnt, :], [('wp', j, s_) for j in range(i, i + cnt) for s_ in range(4)])

        def slot(i, s0, ns=1):
            return T(WPt[:, i, :].bitcast(BF16)[:, s0 * 256:(s0 + ns) * 256], [('wp', i, s_) for s_ in range(s0, s0 + ns)])

        X32 = lambda c, lc, n: T(X32t[:, c, lc:lc + n], [('X32', c, lc)])
        XB = lambda c, lc, n: T(XBt[:, c, lc:lc + n], [('XB', c, lc)])
        YB = lambda c, lc, n: T(YBt[:, c, lc:lc + n], [('YB', c, lc)])
        EBv = EBt[:].rearrange("p (c n) -> p c n", c=KC)
        EB = lambda c, lc, n: T(EBv[:, c, lc:lc + n], [('EB', c, lc)])
        LW = lambda i, lc, n: T(LWt[:, i, lc:lc + n], [('LW', i, lc)])
        CON = lambda i: T(CONt[:, i, :], [('CON', i)])
        IDF, BONES, BONES64, ONESD, M_LE, M_LT, M_GT = [CON(i) for i in range(7)]
        IDB = T(CONBt[:], ['IDB'])
        RST = T(RSTt[:], ['RST'])
        PAR = T(PARt[:], ['PAR'])

        def par(l, j, c, lo=0, hi=128):
            return T(PARt[lo:hi, c, l, j:j + 1], ['PAR'])

        RING = [T(RINGt[:, i], [('ring', i)]) for i in range(10)]
        wq = WQ(k, None, RING, la=6)

        def wsrc(w, l, c0, r0=0):
            return w[l, r0:r0 + D, c0:c0 + 128].rearrange("(kc p) n -> p kc n", p=128)

        def arena(off, n):
            return EBt[:, off:off + n]
        BLK = {}
        for i, nm in enumerate(['R', 'K', 'B', 'A', 'V']):
            BLK[nm] = T(arena(i * 512, 512), [('blk', nm)])

        npass = [2]
        dbg_pi = 0 if short else 1

        def program():
            k.memset(T(CONt[:], [('CON', i) for i in range(8)]), 0.0)
            k.memset(IDB, 0.0)
            S.op('pool', lambda e: e.affine_select(out=CONt[:, 0, :], in_=CONt[:, 0, :], compare_op=ALU.not_equal, fill=1.0,
                                                   base=0, pattern=[[-1, 128]], channel_multiplier=1),
                 reads=IDF.keys, writes=IDF.keys)
            S.op('pool', lambda e: e.affine_select(out=CONBt[:], in_=CONBt[:], compare_op=ALU.not_equal, fill=1.0,
                                                   base=0, pattern=[[-1, 128]], channel_multiplier=1),
                 reads=IDB.keys, writes=IDB.keys)
            for (lo, hi) in ((0, 64), (64, 128)):
                k.memset(T(CONt[lo:hi, 1, lo:hi], BONES.keys), 1.0)
                k.memset(T(CONt[lo:hi, 2, lo:hi], BONES64.keys), 1.0 / 64.0)
                for m in (4, 5, 6):
                    k.memset(T(CONt[lo:hi, m, lo:hi], CON(m).keys), 1.0)
            k.memset(ONESD, 1.0 / 1024.0)
            S.op('pool', lambda e: e.affine_select(out=CONt[:, 4, :], in_=CONt[:, 4, :], compare_op=ALU.is_ge, fill=0.0,
                                                   base=0, pattern=[[1, 128]], channel_multiplier=-1),
                 reads=M_LE.keys, writes=M_LE.keys)
            S.op('pool', lambda e: e.affine_select(out=CONt[:, 5, :], in_=CONt[:, 5, :], compare_op=ALU.is_gt, fill=0.0,
                                                   base=0, pattern=[[1, 128]], channel_multiplier=-1),
                 reads=M_LT.keys, writes=M_LT.keys)
            S.op('pool', lambda e: e.affine_select(out=CONt[:, 6, :], in_=CONt[:, 6, :], compare_op=ALU.is_gt, fill=0.0,
                                                   base=0, pattern=[[-1, 128]], channel_multiplier=1),
                 reads=M_GT.keys, writes=M_GT.keys)
            for mi, src_ in enumerate((M_LE, M_LT, M_GT)):
                k.tt(T(MSKt[:, mi, :], ['MSK']), src_[:, 0:64], src_[:, 64:128], ALU.add)
            k.tt(T(IDSt[:], ['IDS']), IDB[:, 0:64], IDB[:, 64:128], ALU.add)
            k.memset(RST, 1.0)
            k.memset(T(RSTt[:, 0:NT:CH], RST.keys), 0.0)
            k.memset(T(HSTt[:], ['HST']), 0.0)
            k.memset(T(CARt[:], ['CAR']), 0.0)
            k.memset(T(CVCt[:], ['CVC']), 0.0)
            S.barrier()

            PR = T(WPt[0:2 * NRL, 0:2, :].rearrange("p a b -> p (a b)")[:, 0:D], [('wp', 0, s_) for s_ in range(4)] + [('wp', 1, s_) for s_ in range(4)])
            k.dma(PR, prow)
            for c in range(KC):
                pb = nbank()
                k.tr(pb[:, 0:2 * NRL], PR[:, c * 128:(c + 1) * 128], IDF[0:2 * NRL, 0:2 * NRL])
                for l in range(2):
                    k.cp(T(PARt[:, c, l, 0:NRL], ['PAR']), pb[:, l * NRL:(l + 1) * NRL], eng='act')
            for l in range(2):
                pv = T(PARt[:, :, l, :], ['PAR'])
                k.ts(pv[:, :, 27], pv[:, :, 7], -1.0, ALU.mult, 1.0, ALU.add)
                k.act(pv[:, :, 28], pv[:, :, 18], AF.Exp, scale=-1.0)
                k.act(pv[:, :, 28], pv[:, :, 28], AF.Ln, bias=1.0)
                k.ts(pv[:, :, 29], pv[:, :, 28], -16.0, ALU.mult)
                k.ts(pv[:, :, 28], pv[:, :, 28], -8.0, ALU.mult)

            if stop_at == 'const':
                raise Stop()
            MT = T(WPt[:, 16:18, :].rearrange("p a b -> p (a b)")[:, 0:1024].bitcast(BF16).rearrange("p (c m) -> p c m", c=8),
                   [('wp', 16, s_) for s_ in range(4)] + [('wp', 17, s_) for s_ in range(4)])
            for mc in range(2):
                mt = wpspan(0, 2).v(lambda a: a.rearrange("p a b -> p (a b)")[:, 0:D])
                k.dma(mt, mem[mc * 128:(mc + 1) * 128, :])
                for half in range(2):
                    pb = nbank()
                    for j in range(4):
                        c = half * 4 + j
                        k.tr(pb[:, j * 128:(j + 1) * 128], mt[:, c * 128:(c + 1) * 128], IDF)
                    k.cp(MT[:, half * 4:half * 4 + 4, mc * 128:(mc + 1) * 128],
                         pb.v(lambda a: a.rearrange("p (j m) -> p j m", j=4)), eng='act')
            if stop_at == 'memT':
                raise Stop()
            for l in range(2):
                for which, wsrc_t, odram in ((0, xa_wk, o_mk), (1, xa_wv, o_mv)):
                    if stop_at == 'kv0' and (l, which) == (0, 1):
                        raise Stop()
                    NAT = wpspan(2, 4).v(lambda a: a.rearrange("p a b -> p (a b)")[:, 0:2048].rearrange("p (m e) -> p m e", m=2))
                    for e in range(KC):
                        W = wq.next(wsrc(wsrc_t, l, e * 128))
                        if which == 0:
                            pb = nbank()
                            for kc in range(KC):
                                k.mm(pb[:, 0:256], W[:, kc, :], MT[:, kc, :], start=(kc == 0), stop=(kc == KC - 1))
                            k.cp(T(KTt[:, l, e, :], [('KT', l)]), pb[:, 0:256], eng='act')
                        pb = nbank()
                        for mc in range(2):
                            for kc in range(KC):
                                k.mm(pb[:, mc * 128:(mc + 1) * 128], MT[:, kc, mc * 128:(mc + 1) * 128], W[:, kc, :],
                                     start=(kc == 0), stop=(kc == KC - 1))
                        k.cp(NAT[:, :, e * 128:(e + 1) * 128], pb[:, 0:256].v(lambda a: a.rearrange("p (m n) -> p m n", m=2)), eng='dve')
                        if which == 1:
                            k.cp(T(VNt[:, l, :, e * 128:(e + 1) * 128], [('VN', l)]),
                                 pb[:, 0:256].v(lambda a: a.rearrange("p (m n) -> p m n", m=2)), eng='act')
                    k.dma(odram[l].rearrange("(m p) e -> p m e", p=128), NAT)

            if stop_at == 'memkv':
                raise Stop()
            passes = [
                [(0, 0, NT, 'p'), (512, 512, NT, 'p')],
                [(1024, 0, NT, 'p'), (1536, 512, NT, 'p'), (None, 1024, NS, 's')],
            ]
            if short:
                passes = [[(0, 0, NT, 'p'), (512, 512, NT, 'p'), (None, 1024, NS, 's')]]
            npass[0] = len(passes)
            for pi, tiles in enumerate(passes):
                S.mark('LOADX p%d' % pi)
                load_x(tiles)
                if stop_at != 'loadx':
                    for l in range(2):
                        layer_pass(l, pi, tiles)
                S.mark('STOREY p%d' % pi)
                store_y(tiles)
            S.mark('END')

        def load_x(tiles):
            for (t0, lc, n, kind) in tiles:
                nblk = (n + 127) // 128
                for bi in range(nblk):
                    nb_ = min(128, n - bi * 128)
                    xin = wpspan(0, 2).v(lambda a: a.rearrange("p a b -> p (a b)")[:, 0:D])
                    src = xp[t0 + bi * 128:t0 + bi * 128 + nb_, :] if kind == 'p' else xs[:, :]
                    k.dma(xin[0:nb_, :], src)
                    for half in range(2):
                        pb = nbank()
                        for j in range(4):
                            c = half * 4 + j
                            k.tr(pb[:, j * 128:j * 128 + nb_], xin[0:nb_, c * 128:(c + 1) * 128], IDF[0:nb_, 0:nb_])
                        pv = pb.v(lambda a: a.rearrange("p (j m) -> p j m", j=4)[:, :, 0:nb_])
                        c0_ = half * 4
                        dst32 = T(X32t[:, c0_:c0_ + 4, lc + bi * 128:lc + bi * 128 + nb_], [('X32', c, lc) for c in range(c0_, c0_ + 4)])
                        dstb = T(XBt[:, c0_:c0_ + 4, lc + bi * 128:lc + bi * 128 + nb_], [('XB', c, lc) for c in range(c0_, c0_ + 4)])
                        k.cp(dst32, pv, eng='act')
                        k.cp(dstb, pv, eng='dve')

        def store_y(tiles):
            for (t0, lc, n, kind) in tiles:
                nblk = (n + 127) // 128
                for bi in range(nblk):
                    nb_ = min(128, n - bi * 128)
                    yo = wpspan(2, 2).v(lambda a: a.rearrange("p a b -> p (a b)")[:, 0:D])
                    for half in range(2):
                        pb = nbank()
                        for j in range(4):
                            c = half * 4 + j
                            src = T(X32t[:, c, lc + bi * 128:lc + bi * 128 + nb_], [('X32', c, lc)])
                            k.tr(pb[0:nb_, j * 128:(j + 1) * 128], src, IDF)
                        k.cp(yo[0:nb_, half * 512:(half + 1) * 512], pb[0:nb_, :], eng='act' if half == 0 else 'dve')
                    dst = o_yp[t0 + bi * 128:t0 + bi * 128 + nb_, :] if kind == 'p' else o_ys[:, :]
                    k.dma(dst, yo[0:nb_, :])

        def proj(W, lc, n, pb=None, src=XB):
            if pb is None:
                pb = nbank()
            for kc in range(KC):
                k.mm(pb[:, 0:n], W[:, kc, :], src(kc, lc, n), start=(kc == 0), stop=(kc == KC - 1))
            return pb

        def proj_shift(l, cid, W, tile, raw, dtmp, last_p):
            (t0, lc, n, kind) = tile
            pb = proj(W, lc, n)
            k.cp(raw[:, 1:n + 1], pb[:, 0:n], eng='act')
            mu = par(l, cid // 8 if cid < 24 else 3, cid % 8 if cid < 24 else cid - 24)
            car = T(CARt[:, l, cid:cid + 1], [('CAR', l, cid)])
            if kind == 'p':
                k.cp(raw[:, 0:1], car, eng='pool')
                prev = raw[:, 0:n]
                k.cp(car, raw[:, n:n + 1], eng='pool')
                if last_p:
                    k.cp(T(SHOt[:, cid, 0:1], [('SHO', cid)]), raw[:, n:n + 1], eng='pool')
            else:
                prev = T(SHSt[:, cid, :], ['SHS'])
                k.cp(T(SHOt[:, cid, 1:1 + NS], [('SHO', cid)]), raw[:, 1:n + 1], eng='pool')
            k.tt(dtmp[:, 0:n], prev, raw[:, 1:n + 1], ALU.subtract)
            k.stt(raw[:, 1:n + 1], dtmp[:, 0:n], mu, raw[:, 1:n + 1], ALU.mult, ALU.add)
            return raw[:, 1:n + 1]

        def layernorm(l, jg, jb, tile):
            (t0, lc, n, kind) = tile
            pm = nbank()
            pq = nbank()
            for c in range(KC):
                sq = wp(c % 3, n)
                k.act(sq, X32(c, lc, n), AF.Square)
                k.mm(pm[:, 0:n], ONESD, X32(c, lc, n), start=(c == 0), stop=(c == KC - 1))
                k.mm(pq[:, 0:n], ONESD, sq, start=(c == 0), stop=(c == KC - 1))
            mean = wp(3, n)
            rstd = wp(4, n)
            k.cp(mean, pm[:, 0:n], eng='act')
            k.tt(rstd, mean, mean, ALU.mult)
            k.tt(rstd, pq[:, 0:n], rstd, ALU.subtract)
            k.act(rstd, rstd, AF.Sqrt, bias=T(SMALLt[:, 0:1], ['SMALL0']))
            k.recip(rstd, rstd)
            for c in range(KC):
                d = wp(5 + c % 4, n)
                e_ = 'gp' if c % 2 else 'dve'
                k.tt(d, X32(c, lc, n), mean, ALU.subtract, eng=e_)
                k.tt(d, d, rstd, ALU.mult, eng=e_)
                k.ts(X32(c, lc, n), d, par(l, jg, c), ALU.mult, par(l, jb, c), ALU.add, eng=e_)
                k.act(XB(c, lc, n), d, AF.Identity, bias=par(l, jb, c), scale=par(l, jg, c))

        def resid_stage(wsrc_fn, src, tiles, first=True):
            for e in range(KC):
                W = wq.next(wsrc_fn(e))
                for (t0, lc, n, kind) in tiles:
                    pb = proj(W, lc, n, src=src)
                    if first:
                        k.stt(X32(e, lc, n), X32(e, lc, n), ALPHA, pb[:, 0:n], ALU.mult, ALU.add)
                    else:
                        k.tt(X32(e, lc, n), X32(e, lc, n), pb[:, 0:n], ALU.add)

        def layer_pass(l, pi, tiles):
            last_p_tile = max((i for i, t in enumerate(tiles) if t[3] == 'p'))
            is_last_pass = (pi == npass[0] - 1)
            has_s = any(t[3] == 's' for t in tiles)
            k.dma(T(W2A2t[0:64, :], ['W2A2']), rw_w2[l], q='pool')
            k.dma(T(W2A2t[64:128, :], ['W2A2']), rw_a2[l], q='pool')
            k.dma(T(G2t[:], ['G2']), rw_g2[l], q='pool')
            k.memset(T(WABt[:], ['WAB']), 0.0)
            for c in range(KC):
                for hh in range(2):
                    lo = hh * 64
                    k.dma(T(WABt[lo:lo + 64, 0, c, lo:lo + 64], ['WAB']), lru_wa[l, 2 * c + hh], q='pool')
                    k.dma(T(WABt[lo:lo + 64, 1, c, lo:lo + 64], ['WAB']), lru_wx[l, 2 * c + hh], q='pool')
            k.memset(T(SMALLt[:, 0:1], ['SMALL0']), LN_EPS)
            k.memset(T(SMALLt[:, 1:2], ['SMALL1']), GN_EPS)
            if has_s:
                load_sample_states(l)

            S.mark('M0 l%d p%d' % (l, pi))
            for ci in range(2):
                W = wq.next(wsrc(w_in, l, C_W + ci * 128))
                for ti, tile in enumerate(tiles):
                    (t0, lc, n, kind) = tile
                    psf = proj_shift(l, 24 + ci, W, tile, wp(0), wp(1), is_last_pass and ti == last_p_tile)
                    if ci == 0:
                        k.act(LW(0, lc, n)[0:64], psf[0:64], AF.Tanh)
                        k.cp(LW(0, lc, n)[64:128], psf[64:128], eng='act')
                    else:
                        k.act(LW(1, lc, n), psf, AF.Sigmoid)

            if stop_at == 'm0':
                raise Stop()
            S.mark('M1 l%d p%d' % (l, pi))
            S.barrier()
            rr['nbmod'] = 3
            iters = [(hp, ti, tile) for hp in range(KC) for ti, tile in enumerate(tiles)]
            Wd = {}

            def getW(hp):
                if hp not in Wd:
                    Wd[hp] = (wq.next(wsrc(w_in, l, C_R + hp * 128)), wq.next(wsrc(w_in, l, C_K + hp * 128)),
                              wq.next(wsrc(w_in, l, C_V + hp * 128)))
                return Wd[hp]

            def mkAB(i):
                hp, ti, tile = iters[i]
                ctx = {}
                return ctx, rwkv_AB_gen(l, hp, tile, getW(hp), is_last_pass and ti == last_p_tile, i % 2, ctx)
            ctx, gAB = mkAB(0)
            for _ in gAB:
                pass
            pend_post = None
            for i, (hp, ti, tile) in enumerate(iters):
                nctx, ngen = mkAB(i + 1) if i + 1 < len(iters) else (None, None)
                chain_ = itertools.chain(pend_post if pend_post is not None else [], ngen if ngen is not None else [])
                pend_post = None
                if tile[3] == 'p':
                    pend_post = rwkv_C(l, hp, tile, ctx, chain_)
                for _ in chain_:
                    pass
                ctx = nctx
                if is_last_pass and ti == len(tiles) - 1:
                    Hs = T(HSTt[:, l, hp, 0:64], [('HST', l, hp)])
                    pb = nbank()
                    k.tr(pb[0:64, 0:128], Hs, IDF)
                    ho = T(HOUTt[:], ['HOUT'])
                    k.cp(ho[0:64], pb[0:64, 0:128], eng='act')
                    for hh in range(2):
                        lo = hh * 64
                        k.dma(o_prw[l, 2 * hp + hh], ho[0:64, lo:lo + 64])
            if pend_post is not None:
                for _ in pend_post:
                    pass
            rr['nbmod'] = 8
            if has_s:
                sample_rwkv_state(l)
                stile = [t for t in tiles if t[3] == 's'][0]
                for hp in range(KC):
                    rwkv_post(l, hp, stile, T(YSt[:, hp, :], ['YS']), T(SVt[:, 2, hp, :], ['SV']), None, None, sample=True)
            S.barrier()
            if debug and l == 0 and pi == dbg_pi:
                k.dma(dbg['yb'], T(YBt[:], [('YB', c, lc) for c in range(KC) for lc in (0, 512, 1024)]), q='pool')

            if stop_at == 'm1':
                raise Stop()
            S.mark('M2 l%d p%d' % (l, pi))
            for e in range(KC):
                Wp = wq.next(wsrc(rw_proj, l, e * 128))
                Wg = wq.next(wsrc(w_in, l, C_GATE + e * 128))
                for (t0, lc, n, kind) in tiles:
                    po = proj(Wp, lc, n, src=YB)
                    pg = proj(Wg, lc, n)
                    g0 = wp(0, n)
                    k.act(g0, pg[:, 0:n], AF.Sigmoid, bias=par(l, 19, e))
                    k.tt(EB(e, lc, n), g0, po[:, 0:n], ALU.mult)

            if stop_at == 'm2':
                raise Stop()
            S.mark('M3 l%d p%d' % (l, pi))
            for c in range(0, KC, 2):
                Ws = [(wq.next(wsrc(w_in, l, C_U + cc * 128)), wq.next(wsrc(w_in, l, C_Y + cc * 128))) for cc in (c, c + 1)]
                for ti, tile in enumerate(tiles):
                    alive = [lru_gen(l, c + d_, tile, Ws[d_][0], Ws[d_][1], is_last_pass and ti == last_p_tile, 10 * d_) for d_ in range(2)]
                    while alive:
                        for g_ in list(alive):
                            try:
                                next(g_)
                            except StopIteration:
                                alive.remove(g_)

            if stop_at == 'm3':
                raise Stop()
            S.mark('M4 l%d p%d' % (l, pi))
            for e in range(KC):
                Wp = wq.next(wsrc(lru_proj, l, e * 128))
                Wg = wq.next(wsrc(w_in, l, C_GATE + D + e * 128))
                for (t0, lc, n, kind) in tiles:
                    po = proj(Wp, lc, n, src=YB)
                    pg = proj(Wg, lc, n)
                    g1 = wp(0, n)
                    k.act(g1, pg[:, 0:n], AF.Sigmoid, bias=par(l, 20, e))
                    k.tt(g1, g1, po[:, 0:n], ALU.mult)
                    k.tt(EB(e, lc, n), EB(e, lc, n), g1, ALU.add)

            S.mark('M5 l%d p%d' % (l, pi))
            resid_stage(lambda e: wsrc(w_out_mix, l, e * 128), EB, tiles)
            for tile in tiles:
                layernorm(l, 21, 22, tile)
            if debug and l == 0 and pi == dbg_pi:
                k.dma(dbg['x1'], T(X32t[:], [('X32', c, lc) for c in range(KC) for lc in (0, 512, 1024)]))

            if stop_at == 'm5':
                raise Stop()
            S.mark('AT l%d p%d' % (l, pi))
            for e in range(KC):
                W = wq.next(wsrc(xa_wq, l, e * 128))
                for (t0, lc, n, kind) in tiles:
                    pb = proj(W, lc, n)
                    if kind == 'p':
                        k.act(YB(e, lc, n), pb[:, 0:n], AF.Copy, scale=1.0 / 16.0)
                    else:
                        k.act(T(YSt[:, e, :], ['YS']), pb[:, 0:n], AF.Copy, scale=1.0 / 16.0)
            for tile in tiles:
                if tile[3] == 'p':
                    attn_prompt(l, tile)
                else:
                    attn_sample(l, tile)
            resid_stage(lambda e: wsrc(xa_wo, l, e * 128), EB, tiles)
            for tile in tiles:
                layernorm(l, 23, 24, tile)

            if stop_at == 'attn':
                raise Stop()
            S.mark('MLP l%d p%d' % (l, pi))
            for g in range(4):
                for j in range(KC):
                    W = wq.next(wsrc(mlp_up, l, g * D + j * 128))
                    for (t0, lc, n, kind) in tiles:
                        pb = proj(W, lc, n)
                        r_ = wp(j % 2, n)
                        k.act(r_, pb[:, 0:n], AF.Relu)
                        k.tt(EB(j, lc, n), r_, r_, ALU.mult, eng='gp' if j % 2 else 'dve')
                resid_stage(lambda e: wsrc(mlp_down, l, e * 128, r0=g * D), EB, tiles, first=(g == 0))
            for tile in tiles:
                layernorm(l, 25, 26, tile)

            if stop_at == 'mlp':
                raise Stop()
            S.mark('OUT l%d p%d' % (l, pi))
            if is_last_pass:
                def store_T(get_src, nchk, dst, npart):
                    tm = wpspan(8, 7).v(lambda a_: a_.rearrange("p a b -> p (a b)"))[0:npart, 0:nchk * 128]
                    for c0_ in range(0, nchk, 4):
                        pb = nbank()
                        cn = min(4, nchk - c0_)
                        for c in range(cn):
                            k.tr(pb[0:npart, c * 128:(c + 1) * 128], get_src(c0_ + c), IDF)
                        k.cp(tm[:, c0_ * 128:(c0_ + cn) * 128], pb[0:npart, 0:cn * 128], eng='act')
                    k.dma(dst, tm)
                shk = [('SHO', c) for c in range(26)]
                store_T(lambda c: T(SHOt[:, :, 0], shk), 1, o_psh[l].rearrange("(c p) -> c p", p=128), 26)
                store_T(lambda c: T(HOt[:, :, 0], ['HO']), 1, o_ph[l].rearrange("(c p) -> c p", p=128), 8)
                for j in range(3):
                    store_T(lambda c, j=j: T(CVOt[:, :, j, 0], ['CVO']), 1, o_pcv[l, j].rearrange("(c p) -> c p", p=128), 8)
                store_T(lambda c: T(SHOt[:, c, 1:1 + NS], [('SHO', c)]), 26, o_ssh[l], NS)
                store_T(lambda c: T(HOt[:, c, 1:1 + NS], ['HO']), 8, o_sh[l], NS)
                for j in range(3):
                    store_T(lambda c, j=j: T(CVOt[:, c, j, 1:1 + NS], ['CVO']), 8, o_scv[l, :, j, :], NS)

        def load_sample_states(l):
            def tload(src2d, ncols, dstfn):
                tm = wpspan(8, 7).v(lambda a: a.rearrange("p a b -> p (a b)"))[0:NS, 0:ncols]
                k.dma(tm, src2d)
                nch = ncols // 128
                for c0_ in range(0, nch, 26):
                    pb = nbank()
                    cn = min(26, nch - c0_)
                    for c in range(cn):
                        k.tr(pb[:, c * NS:(c + 1) * NS], tm[:, (c0_ + c) * 128:(c0_ + c + 1) * 128], IDF[0:NS, 0:NS])
                    dstfn(c0_, cn, pb)
            tload(ssh[l], NSH, lambda c0_, cn, pb: k.cp(T(SHSt[:, c0_:c0_ + cn, :], ['SHS']),
                                                        pb[:, 0:cn * NS].v(lambda a: a.rearrange("p (c b) -> p c b", b=NS)), eng='act'))
            tload(shh[l], D, lambda c0_, cn, pb: k.cp(T(HS0t[:, c0_:c0_ + cn, :], ['HS0']),
                                                      pb[:, 0:cn * NS].v(lambda a: a.rearrange("p (c b) -> p c b", b=NS)), eng='act'))
            for j in range(3):
                tload(scv[l, :, j, :], D, lambda c0_, cn, pb, j=j: k.cp(T(CSt[:, c0_:c0_ + cn, j, :], ['CS']),
                                                                        pb[:, 0:cn * NS].v(lambda a: a.rearrange("p (c b) -> p c b", b=NS)), eng='act'))

        ARB = [T(EBt[:, 5120 + i_ * 1032:5120 + (i_ + 1) * 1032].bitcast(F32), [('ar', i_, s_) for s_ in range(4)]) for i_ in range(3)]
        BLKP = [{nm: T(arena((P_ * 5 + i_) * 512, 512), [('blk', P_, nm)]) for i_, nm in enumerate(['R', 'K', 'B', 'A', 'V'])}
                for P_ in range(2)]

        def rwkv_AB_gen(l, hp, tile, W3, last_p, P, ctx):
            (t0, lc, n, kind) = tile
            dt = ARB[0]
            r = proj_shift(l, hp, W3[0], tile, wp(0), dt, last_p)
            yield
            kx = proj_shift(l, 8 + hp, W3[1], tile, wp(1), dt, last_p)
            yield
            vx = proj_shift(l, 16 + hp, W3[2], tile, wp(2), dt, last_p)
            yield
            sw = wp(5, n)
            asg = wp(17, n)
            kk = wp(18, n)
            t8 = wp(19, n)
            km = ARB[1][:, 0:n]
            cl = ARB[0][:, 0:n]
            en = ARB[2][:, 0:n]
            gg = slot(6, 2 * P, 2)[:, 0:n]
            bon = slot(10, 2 * P, 2)[:, 0:n]
            pz = nbank()
            k.mm(pz[:, 0:n], T(W2A2t[0:64, hp * 128:(hp + 1) * 128], ['W2A2']), LW(0, lc, n)[0:64])
            k.act(sw, pz[:, 0:n], AF.Sigmoid, bias=par(l, 4, hp))
            yield
            pz = nbank()
            k.mm(pz[:, 0:n], T(W2A2t[64:128, hp * 128:(hp + 1) * 128], ['W2A2']), LW(0, lc, n)[64:128])
            k.act(asg, pz[:, 0:n], AF.Sigmoid, bias=par(l, 5, hp))
            yield
            pz = nbank()
            k.mm(pz[:, 0:n], T(G2t[:, hp * 128:(hp + 1) * 128], ['G2']), LW(1, lc, n))
            k.cp(gg, pz[:, 0:n], eng='act')
            yield
            k.ts(kk, kx, par(l, 6, hp), ALU.mult)
            k.act(t8, kk, AF.Square)
            yield
            pn = nbank()
            k.mm(pn[:, 0:n], BONES, t8)
            k.ts(t8, pn[:, 0:n], 1e-24, ALU.max)
            yield
            k.act(t8, t8, AF.Sqrt)
            k.recip(t8, t8)
            yield
            k.tt(kk, kk, t8, ALU.mult)
            yield
            k.ts(km, asg, par(l, 7, hp), ALU.mult, par(l, 27, hp), ALU.add)
            k.tt(km, km, kx, ALU.mult)
            yield
            k.stt(t8, r, par(l, 8, hp), km, ALU.mult, ALU.mult)
            pbon = nbank()
            k.mm(pbon[:, 0:n], BONES, t8)
            k.tt(bon, pbon[:, 0:n], vx, ALU.mult)
            yield
            ctx['gg'], ctx['bon'] = gg, bon
            if kind == 's':
                k.cp(T(SVt[:, 0, hp, :], ['SV']), r, eng='act')
                k.cp(T(SVt[:, 1, hp, :], ['SV']), km, eng='act')
                k.cp(T(SVt[:, 2, hp, :], ['SV']), vx, eng='act')
                k.act(T(SVt[:, 3, hp, :], ['SV']), sw, AF.Exp, scale=-C0)
                k.ts(T(SVt[:, 4, hp, :], ['SV']), kk, -1.0, ALU.mult)
                k.tt(T(SVt[:, 5, hp, :], ['SV']), kk, asg, ALU.mult)
                k.cp(T(SGt[:, 0, hp, :], ['SG']), gg, eng='act')
                k.cp(T(SGt[:, 1, hp, :], ['SG']), bon, eng='act')
                return
            BL = BLKP[P]
            k.scan(cl, RST, sw, 0.0)
            yield
            ep = t8
            k.act(ep, cl, AF.Exp, scale=-C0)
            k.act(en, cl, AF.Exp, scale=C0)
            yield
            k.tt(sw, cl, sw, ALU.subtract)
            k.act(sw, sw, AF.Exp, scale=-C0)
            yield
            k.tt(asg, kk, asg, ALU.mult)
            nch = n // CH
            pc = T(PCt[:, P, 0:nch], [('PC', P)])
            k.cp(pc, ep.v(lambda a_: a_[:, CH - 1:n:CH]), eng='act')
            yield
            k.tt(BL['R'], r, ep, ALU.mult)
            yield
            k.tt(BL['K'], km, en, ALU.mult)
            yield
            k.tt(BL['B'], asg, en, ALU.mult)
            yield
            k.stt(BL['A'], kk, -1.0, sw, ALU.mult, ALU.mult)
            k.cp(BL['V'], vx, eng='act')
            ctx['pc'] = pc
            ctx['BL'] = BL
            yield

        def rwkv_C(l, hp, tile, ctx, nxt):
            (t0, lc, n, kind) = tile
            nch = n // CH
            BL, pc = ctx['BL'], ctx['pc']

            def adv(cnt):
                if nxt is not None:
                    for _ in range(cnt):
                        next(nxt, None)
            Y = wp(16, n)
            Hs = T(HSTt[:, l, hp, 0:64], [('HST', l, hp)])
            HB = T(HBFt[:], ['HBF'])
            k.cp(HB, Hs, eng='act')
            blk = lambda nm, c: BL[nm][:, c * CH:(c + 1) * CH]
            nq = nch // 4
            assert nq == 2
            alive = [pre_quad_gen(q, blk, QS[q]) for q in range(nq)]
            while alive:
                for g_ in list(alive):
                    try:
                        next(g_)
                    except StopIteration:
                        alive.remove(g_)
                    adv(1)
            Ybs = [psbank(3), psbank(7)]
            pend = []
            for c in range(nch):
                seq_chunk(c, c % 4, blk, QS[c // 4], Hs, HB, pc[:, c:c + 1], Ybs, pend)
                adv(2)
            pend.pop()()
            if nxt is not None:
                for _ in nxt:
                    pass
            for par_ in range(2):
                k.cp(Y.v(lambda a_: a_.rearrange("p (c two s) -> p c two s", two=2, s=CH)[:, :, par_, :]),
                     Ybs[par_][:, 0:n].v(lambda a_: a_.rearrange("p (c two s) -> p c two s", two=2, s=CH)[:, :, par_, :]),
                     eng='act' if par_ else 'dve')
            return rwkv_post_gen(l, hp, tile, Y, ctx['gg'], ctx['bon'], wp(18))

        def mkqs(b0, b1, b2, b3, b4):
            return {'VT': slot(b0, 0), 'KT': slot(b0, 1), 'BT': slot(b0, 2), 'AT': slot(b0, 3), 'TRS': slot(b0, 0, 4),
                    'ATrk': slot(b1, 0), 'ATrb': slot(b1, 1), 'ATR': slot(b1, 0, 2), 'ATak': slot(b1, 2), 'TT': slot(b1, 3),
                    'X': [slot(b2, 0), slot(b2, 2)], 'XT': [slot(b2, 1), slot(b2, 3)], 'XX': [slot(b2, 0, 2), slot(b2, 2, 2)],
                    'TTt': [slot(b3, 0), slot(b3, 1)], 'ZV': slot(b3, 2), 'Wh': slot(b3, 3),
                    'Uv': slot(b4, 0), 'WhT': slot(b4, 1), 'UW': slot(b4, 0, 2), 'GT': slot(b4, 2), 'N': slot(b4, 3), 'GN': slot(b4, 2, 2)}
        QS = [mkqs(3, 4, 7, 8, 14), mkqs(9, 11, 12, 13, 15)]
        rr['qb'] = 0

        def nqbank():
            b = rr['qb']
            rr['qb'] = (b + 1) % 3
            return psbank(4 + b)

        def hmm(out_t, lt, rh, start=True, stop=True):
            for h_ in range(2):
                lo = h_ * 64
                k.mm(out_t[lo:lo + 64], lt[lo:lo + 64], rh[lo:lo + 64], start=start, stop=stop)

        def ch(t, j):
            return t[:, j * CH:(j + 1) * CH]

        def bcm(mi, nrep):
            return T(MSKt[:, mi, :], ['MSK']).v(lambda a_: a_.unsqueeze(1).to_broadcast([128, nrep, 64]))

        def r3(t, nrep):
            return t.v(lambda a_: a_.rearrange("p (c n) -> p c n", c=nrep))

        def pre_quad_gen(q, blk, qs):
            cs = [4 * q + j for j in range(4)]
            bk = nqbank()
            bkb = bk.v(lambda a_: a_.bitcast(BF16))
            for si, nm in enumerate(('V', 'K', 'B', 'A')):
                for j, c in enumerate(cs):
                    for h_ in range(2):
                        lo = h_ * 64
                        k.tr(bkb[lo:lo + 64, si * 256 + j * 64:si * 256 + (j + 1) * 64], blk(nm, c)[lo:lo + 64], IDB[lo:lo + 64, lo:lo + 64])
            k.cp(qs['TRS'], bkb[:, 0:1024], eng='act')
            yield
            X, XT = qs['X'][0], qs['XT'][0]
            for lt, rh, mi, dst in (('B', 'A', 1, XT), ('A', 'B', 2, X), ('K', 'A', 1, qs['ATak'])):
                bk = nqbank()
                for j, c in enumerate(cs):
                    hmm(ch(bk, j), blk(lt, c), blk(rh, c))
                k.tt(r3(dst, 4), r3(bk[:, 0:256], 4), bcm(mi, 4), ALU.mult)
                yield
            bk = nqbank()
            for si, lt in enumerate(('K', 'B')):
                for j, c in enumerate(cs):
                    hmm(bk[:, si * 256 + j * 64:si * 256 + (j + 1) * 64], blk(lt, c), blk('R', c))
            k.tt(r3(qs['ATR'], 8), r3(bk, 8), bcm(0, 8), ALU.mult)
            yield
            bk = nqbank()
            for j in range(4):
                hmm(ch(bk, j), ch(qs['ATak'], j), ch(qs['VT'], j))
            k.cp(qs['ZV'], bk[:, 0:256], eng='act')
            yield
            TT = qs['TTt'][0]
            ids4 = T(IDSt[:], ['IDS']).v(lambda a_: a_.unsqueeze(1).to_broadcast([128, 4, 64]))
            k.tt(r3(TT, 4), r3(XT, 4), ids4, ALU.add)
            tpar = 0

            def tt_update(TT, Xf, dst):
                bk3 = nqbank()
                for j in range(4):
                    hmm(ch(bk3, j), ch(Xf, j), ch(TT, j))
                k.tt(dst, bk3[:, 0:256], TT, ALU.add)
                return dst
            for L in range(1, 6):
                p_ = L % 2
                bk = nqbank()
                for j in range(4):
                    hmm(ch(bk, j), ch(XT, j), ch(X, j))
                if L < 5:
                    for j in range(4):
                        hmm(bk[:, 256 + j * 64:256 + (j + 1) * 64], ch(X, j), ch(XT, j))
                    k.cp(qs['XX'][p_], bk, eng='act')
                else:
                    k.cp(qs['X'][p_], bk[:, 0:256], eng='act')
                if L >= 2:
                    tpar = 1 - tpar
                    TT = tt_update(TT, X, qs['TTt'][tpar])
                X, XT = qs['X'][p_], qs['XT'][p_]
                yield
            tt_update(TT, X, qs['TT'])
            yield
            TTf = qs['TT']
            bk = nqbank()
            for j in range(4):
                hmm(ch(bk, j), ch(TTf, j), ch(qs['ZV'], j))
            for j in range(4):
                hmm(bk[:, 256 + j * 64:256 + (j + 1) * 64], ch(qs['AT'], j), ch(TTf, j))
            k.cp(qs['UW'], bk, eng='act')
            bk = nqbank()
            for j in range(4):
                hmm(ch(bk, j), ch(TTf, j), ch(qs['AT'], j))
            k.cp(qs['Wh'], bk[:, 0:256], eng='dve')
            yield
            bk = nqbank()
            for j in range(4):
                hmm(ch(bk, j), ch(qs['Wh'], j), ch(qs['BT'], j))
            for j in range(4):
                o_ = bk[:, 256 + j * 64:256 + (j + 1) * 64]
                for h_ in range(2):
                    lo = h_ * 64
                    k.mm(o_[lo:lo + 64], ch(qs['KT'], j)[lo:lo + 64], ch(qs['VT'], j)[lo:lo + 64], start=True, stop=False)
                    k.mm(o_[lo:lo + 64], ch(qs['BT'], j)[lo:lo + 64], ch(qs['Uv'], j)[lo:lo + 64], start=False, stop=True)
            k.cp(qs['GN'], bk, eng='act')
            yield

        def seq_chunk(c, j, blk, qs, Hs, HB, pcc, Ybs, pend):
            R = blk('R', c)
            g = lambda nm: ch(qs[nm], j)
            hsp = T(HSPt[:], ['HSP'])
            k.tt(hsp, Hs, g('N'), ALU.add)
            k.ts(hsp, hsp, pcc, ALU.mult)
            bku = nqbank()
            hmm(bku[:, 0:64], g('WhT'), HB)
            ys = Ybs[c % 2][:, c * CH:(c + 1) * CH]
            for h_ in range(2):
                lo = h_ * 64
                k.mm(ys[lo:lo + 64], HB[lo:lo + 64], R[lo:lo + 64], start=True, stop=False)
                k.mm(ys[lo:lo + 64], g('VT')[lo:lo + 64], g('ATrk')[lo:lo + 64], start=False, stop=False)
            bk = nqbank()
            hmm(bk[:, 0:64], g('GT'), HB)
            if pend:
                pend.pop()()
            k.stt(HB, bk[:, 0:64], pcc, hsp, ALU.mult, ALU.add)
            k.stt(Hs, bk[:, 0:64], pcc, hsp, ALU.mult, ALU.add)
            U = T(ZUt[:, c % 2, :], [('U', c % 2)])
            k.tt(U, bku[:, 0:64], g('Uv'), ALU.add)

            def fin(ys=ys, U=U, atrb=g('ATrb')):
                for h_ in range(2):
                    lo = h_ * 64
                    k.mm(ys[lo:lo + 64], U[lo:lo + 64], atrb[lo:lo + 64], start=False, stop=True)
            pend.append(fin)

        def evac_b(dst, src, eng):
            if eng == 'act':
                k.cp(dst, src, eng='act')
            else:
                k.cp(dst, src, eng='dve')

        def chunk_pre(c, blk):
            R, K_, B_, A_, V_ = blk('R', c), blk('K', c), blk('B', c), blk('A', c), blk('V', c)
            pers = SMB[(c % 2) * NPERS:(c % 2 + 1) * NPERS]
            out = {}
            for i, (nm, src) in enumerate((('VT', V_), ('KT', K_), ('BT', B_))):
                sl = nslot()
                slb = sl.v(lambda a: a.bitcast(BF16)[:, 0:128])
                k.tr(slb, src, IDB)
                d = pers[i]
                k.cp(d, slb, eng='act')
                out[nm] = d
            for i, (nm, lt, rh, mask) in enumerate((('ATrk', K_, R, M_LE), ('ATrb', B_, R, M_LE), ('ATak', K_, A_, M_LT),
                                                    ('XT', B_, A_, M_LT), ('X', A_, B_, M_GT))):
                sl = nslot()
                k.mm(sl, lt, rh)
                d = pers[3 + i] if i < 3 else nsmb()
                k.tt(d, sl, mask, ALU.mult)
                out[nm] = d
            X, XT = out['X'], out['XT']
            TT = nsmb()
            k.tt(TT, XT, IDB, ALU.add, eng='pool')

            def tt_update(TT, Xf, final):
                sl3 = nslot()
                k.mm(sl3, Xf, TT)
                TTn = pers[6] if final else nsmb()
                k.tt(TTn, sl3, TT, ALU.add)
                return TTn
            for L in range(1, 6):
                sl = nslot()
                k.mm(sl, XT, X)
                Xn = nsmb()
                k.cp(Xn, sl, eng='act')
                XTn = None
                if L < 5:
                    sl2 = nslot()
                    k.mm(sl2, X, XT)
                    XTn = nsmb()
                    k.cp(XTn, sl2, eng='act')
                if L >= 2:
                    TT = tt_update(TT, X, False)
                X, XT = Xn, XTn
            TT = tt_update(TT, X, True)
            out['TT'] = TT
            return out

        def chunk_seq(c, blk, pre, Hs, HB, pcc, Y):
            R, A_ = blk('R', c), blk('A', c)
            sl = nslot()
            k.mm(sl, A_, HB, start=True, stop=False)
            k.mm(sl, pre['ATak'], pre['VT'], start=False, stop=True)
            Z = nsmb()
            k.cp(Z, sl, eng='act')
            sl = nslot()
            k.mm(sl, pre['TT'], Z)
            U = nsmb()
            k.cp(U, sl, eng='dve')
            sl = nslot()
            k.mm(sl, HB, R, start=True, stop=False)
            k.mm(sl, pre['VT'], pre['ATrk'], start=False, stop=False)
            k.mm(sl, U, pre['ATrb'], start=False, stop=True)
            for hh in range(2):
                lo = hh * 64
                k.cp(Y[lo:lo + 64, c * CH:(c + 1) * CH], sl[lo:lo + 64, lo:lo + 64], eng='act' if hh else 'dve')
            sl = nslot()
            k.mm(sl, pre['KT'], pre['VT'], start=True, stop=False)
            k.mm(sl, pre['BT'], U, start=False, stop=True)
            k.tt(Hs, Hs, sl, ALU.add)
            k.ts(Hs, Hs, pcc, ALU.mult)
            k.cp(HB, Hs, eng='act')

        def rwkv_post_gen(l, hp, tile, Y, gg, bon, tmp, sample=False):
            (t0, lc, n, kind) = tile
            if sample:
                gg = T(SGt[:, 0, hp, :], ['SG'])
                bon = T(SGt[:, 1, hp, :], ['SG'])
            ysq = tmp[:, 0:n]
            k.act(ysq, Y, AF.Square)
            yield
            pm = nbank()
            pq = nbank()
            k.mm(pm[:, 0:n], BONES64, Y)
            k.mm(pq[:, 0:n], BONES64, ysq)
            k.act(ysq, pm[:, 0:n], AF.Square)
            k.tt(ysq, pq[:, 0:n], ysq, ALU.subtract)
            k.tt(Y, Y, pm[:, 0:n], ALU.subtract)
            yield
            k.act(ysq, ysq, AF.Sqrt, bias=T(SMALLt[:, 1:2], ['SMALL1']))
            yield
            k.recip(ysq, ysq)
            yield
            k.tt(Y, Y, ysq, ALU.mult)
            yield
            k.ts(Y, Y, par(l, 9, hp), ALU.mult, par(l, 10, hp), ALU.add)
            k.tt(Y, Y, bon, ALU.add)
            yield
            k.tt(YB(hp, lc, n), Y, gg, ALU.mult)
            yield

        def rwkv_post(l, hp, tile, Y, vx, gg, bon, sample=False):
            for _ in rwkv_post_gen(l, hp, tile, Y, gg, bon, wp(18), sample=sample):
                pass

        SGt = sb("SG", [128, 2, 8, NS], F32)

        def sample_rwkv_state(l):
            for j in range(6):
                tm = wpspan(0, 2).v(lambda a: a.rearrange("p a b -> p (a b)"))[0:NS, 0:D]
                for half in range(2):
                    pb = nbank()
                    for h4 in range(4):
                        hp = half * 4 + h4
                        k.tr(pb[0:NS, h4 * 128:(h4 + 1) * 128], T(SVt[:, j, hp, :], ['SV']), IDF)
                    k.cp(tm[:, half * 512:(half + 1) * 512], pb[0:NS, :], eng='act')
                k.dma(T(scr[j].rearrange("(b f) -> b f", b=NS), [('scr', j)]), tm)
                k.dma(T(SV2t[:, j, :], ['SV2']), T(scr[j].rearrange("(p f) -> p f", p=128), [('scr', j)]))
            sv = lambda j: T(SV2t[:, j, :], ['SV2'])
            ysm = T(WPt[:, 19, 0:128], [('wp', 19, s_) for s_ in range(4)])
            for hh in range(2):
                for vh in range(2):
                    ST = wpspan(0, 4).v(lambda a: a.rearrange("p a b -> p (a b)")[:, 0:2048].rearrange("p (v k) -> p v k", k=64))
                    TM = wpspan(4, 4).v(lambda a: a.rearrange("p a b -> p (a b)")[:, 0:2048].rearrange("p (v k) -> p v k", k=64))
                    off = hh * 4096 + vh * 2048
                    k.dma(ST, srw[l, :, off:off + 2048].rearrange("p (v k) -> p v k", k=64))
                    kb = lambda j: sv(j).v(lambda a: a[:, hh * 64:(hh + 1) * 64].unsqueeze(1).to_broadcast([128, 32, 64]))
                    vs = sv(2).v(lambda a: a[:, hh * 64 + vh * 32:hh * 64 + vh * 32 + 32])
                    k.tt(TM, ST, kb(4), ALU.mult)
                    sa = T(SMALLt[:, 24:56], ['SMALLsa'])
                    k.red(sa, TM, ALU.add)
                    k.tt(ST, ST, kb(3), ALU.mult)
                    k.tt(TM, sa.v(lambda a: a.unsqueeze(2).to_broadcast([128, 32, 64])), kb(5), ALU.mult)
                    k.tt(ST, ST, TM, ALU.add)
                    k.tt(TM, vs.v(lambda a: a.unsqueeze(2).to_broadcast([128, 32, 64])), kb(1), ALU.mult)
                    k.tt(ST, ST, TM, ALU.add)
                    k.dma(o_srw[l, :, off:off + 2048].rearrange("p (v k) -> p v k", k=64), ST)
                    k.tt(TM, ST, kb(0), ALU.mult)
                    k.red(ysm[:, hh * 64 + vh * 32:hh * 64 + vh * 32 + 32], TM, ALU.add)
            k.dma(T(scr[6].rearrange("(p f) -> p f", p=128), [('scr', 6)]), ysm)
            tm = wpspan(0, 2).v(lambda a: a.rearrange("p a b -> p (a b)"))[0:NS, 0:D]
            k.dma(tm, T(scr[6].rearrange("(b f) -> b f", b=NS), [('scr', 6)]))
            pb = nbank()
            for hp in range(KC):
                k.tr(pb[:, hp * NS:(hp + 1) * NS], tm[:, hp * 128:(hp + 1) * 128], IDF[0:NS, 0:NS])
            k.cp(T(YSt[:], ['YS']), pb[:, 0:KC * NS].v(lambda a: a.rearrange("p (c b) -> p c b", b=NS)), eng='act')

        def lru_gen(l, c, tile, Wu, Wy, last_p, base):
            (t0, lc, n, kind) = tile
            pu = proj(Wu, lc, n)
            ue = wp(base + 0)
            yv = wp(base + 8, n)
            cw = lambda j: par(l, 11 + j, c)
            xc = wp(base + 1, n)
            if kind == 'p':
                k.cp(ue[:, 3:3 + n], pu[:, 0:n], eng='act')
                cvc = T(CVCt[:, l, c, :], [('CVC', l, c)])
                k.cp(ue[:, 0:3], cvc, eng='pool')
                k.cp(cvc, ue[:, n:n + 3], eng='pool')
                if last_p:
                    k.cp(T(CVOt[:, c, :, 0], ['CVO']), ue[:, n:n + 3], eng='pool')
                k.ts(xc, ue[:, 0:n], cw(0), ALU.mult, par(l, 15, c), ALU.add)
                for j in (1, 2, 3):
                    k.stt(xc, ue[:, j:j + n], cw(j), xc, ALU.mult, ALU.add)
            else:
                k.cp(ue[:, 0:n], pu[:, 0:n], eng='act')
                cs = lambda j: T(CSt[:, c, j, :], ['CS'])
                k.ts(xc, cs(0), cw(0), ALU.mult, par(l, 15, c), ALU.add)
                k.stt(xc, cs(1), cw(1), xc, ALU.mult, ALU.add)
                k.stt(xc, cs(2), cw(2), xc, ALU.mult, ALU.add)
                k.stt(xc, ue[:, 0:n], cw(3), xc, ALU.mult, ALU.add)
                k.cp(T(CVOt[:, c, 0, 1:1 + NS], ['CVO']), cs(1), eng='pool')
                k.cp(T(CVOt[:, c, 1, 1:1 + NS], ['CVO']), cs(2), eng='pool')
                k.cp(T(CVOt[:, c, 2, 1:1 + NS], ['CVO']), ue[:, 0:n], eng='pool')
            yield
            py = proj(Wy, lc, n)
            k.cp(yv, py[:, 0:n], eng='act')
            yield
            xcb = wp(base + 2, n).v(lambda a: a.bitcast(BF16)[:, 0:n])
            k.cp(xcb, xc, eng='act')
            pa = nbank()
            k.mm(pa[:, 0:n], T(WABt[:, 0, c, :], ['WAB']), xcb)
            px = nbank()
            k.mm(px[:, 0:n], T(WABt[:, 1, c, :], ['WAB']), xcb)
            gr = wp(base + 3, n)
            gi = wp(base + 4, n)
            k.act(gr, pa[:, 0:n], AF.Sigmoid, bias=par(l, 16, c))
            k.act(gi, px[:, 0:n], AF.Sigmoid, bias=par(l, 17, c))
            yield
            av = wp(base + 5, n)
            e2 = wp(base + 6, n)
            k.act(av, gr, AF.Exp, scale=par(l, 28, c))
            k.act(e2, gr, AF.Exp, scale=par(l, 29, c))
            k.ts(e2, e2, -1.0, ALU.mult, 1.0, ALU.add)
            k.ts(e2, e2, 0.0, ALU.max)
            yield
            k.act(e2, e2, AF.Sqrt)
            k.tt(gi, gi, xc, ALU.mult)
            k.tt(e2, e2, gi, ALU.mult)
            yield
            hh_ = wp(base + 7, n)
            hc = T(CARt[:, l, 26 + c:27 + c], [('CAR', l, 26 + c)])
            if kind == 'p':
                k.scan(hh_, av, e2, hc)
                k.cp(hc, hh_[:, n - 1:n], eng='pool')
                if last_p:
                    k.cp(T(HOt[:, c, 0:1], ['HO']), hh_[:, n - 1:n], eng='pool')
            else:
                k.tt(hh_, av, T(HS0t[:, c, :], ['HS0']), ALU.mult)
                k.tt(hh_, hh_, e2, ALU.add)
                k.cp(T(HOt[:, c, 1:1 + NS], ['HO']), hh_, eng='pool')
            yield
            t9 = wp(base + 9, n)
            k.tt(t9, yv, yv, ALU.mult, eng='pool')
            k.ts(t9, t9, 0.044715, ALU.mult, 1.0, ALU.add)
            yield
            k.tt(t9, t9, yv, ALU.mult)
            k.act(t9, t9, AF.Sigmoid, scale=1.5957691216057308)
            k.tt(t9, t9, yv, ALU.mult, eng='pool')
            yield
            k.tt(YB(c, lc, n), hh_, t9, ALU.mult)
            yield

        rr['b8'] = 0

        def nbank8():
            b_ = rr['b8']
            rr['b8'] = (b_ + 1) % 8
            return psbank(b_)

        def attn_prompt(l, tile):
            (t0, lc, n, kind) = tile
            KT = lambda e: T(KTt[:, l, e, :], [('KT', l)])
            nsub = n // 128

            def bufs(P):
                ex = wpspan(4 * P, 2).v(lambda a: a.rearrange("p a b -> p (a b)")[:, 0:1024].rearrange("p (h m) -> p h m", h=4))
                pbf = wp(4 * P + 2).v(lambda a: a.bitcast(BF16)[:, 0:1024].rearrange("p (h m) -> p h m", h=4))
                PTs = wp(4 * P + 3).v(lambda a: a.bitcast(BF16)[:, 0:1024])
                mx = T(SMALLt[:, 2:6], ['SMALLmx']) if P == 0 else T(SMALLt[:, 56:60], ['SMALLmxb'])
                sm = T(SMALLt[:, 6:10], ['SMALLsm']) if P == 0 else T(SMALLt[:, 60:64], ['SMALLsmb'])
                return ex, pbf, PTs, mx, sm

            def stageA(sub):
                P = sub % 2
                c0_ = lc + sub * 128
                ex, pbf, PTs, mx, sm = bufs(P)
                pbs = []
                for hpair in range(2):
                    pb = nbank8()
                    pbs.append(pb)
                    for h2 in range(2):
                        h = hpair * 2 + h2
                        for dc in range(2):
                            q = T(YBt[:, 2 * h + dc, c0_:c0_ + 128], [('YB', 2 * h + dc, lc)])
                            k.mm(pb[:, h2 * 256:(h2 + 1) * 256], q, KT(2 * h + dc), start=(dc == 0), stop=(dc == 1))
                for hpair in range(2):
                    k.red(mx[:, hpair * 2:hpair * 2 + 2], pbs[hpair].v(lambda a: a.rearrange("p (h m) -> p h m", h=2)), ALU.max)
                k.ts(mx, mx, -1.0, ALU.mult)
                for h in range(4):
                    k.act(ex[:, h, :], pbs[h // 2][:, (h % 2) * 256:(h % 2 + 1) * 256], AF.Exp, bias=mx[:, h:h + 1], accum=sm[:, h:h + 1])

            def stageB(sub):
                P = sub % 2
                c0_ = lc + sub * 128
                ex, pbf, PTs, mx, sm = bufs(P)
                k.recip(sm, sm)
                for h in range(4):
                    k.ts(pbf[:, h, :], ex[:, h, :], sm[:, h:h + 1], ALU.mult)
                ptb = nbank8()
                ptv = ptb.v(lambda a: a.bitcast(BF16))
                for h in range(4):
                    for mc in range(2):
                        j = h * 2 + mc
                        k.tr(ptv[:, j * 128:(j + 1) * 128], pbf[:, h, mc * 128:(mc + 1) * 128], IDB)
                k.cp(PTs, ptv[:, 0:1024], eng='act')
                for half in range(2):
                    po = nbank8()
                    for j in range(4):
                        ec = half * 4 + j
                        h = ec // 2
                        for mc in range(2):
                            k.mm(po[:, j * 128:(j + 1) * 128], T(VNt[:, l, mc, ec * 128:(ec + 1) * 128], [('VN', l)]),
                                 PTs[:, (h * 2 + mc) * 128:(h * 2 + mc + 1) * 128], start=(mc == 0), stop=(mc == 1))
                    dst = T(EBv[:, half * 4:half * 4 + 4, c0_:c0_ + 128], [('EB', c, lc) for c in range(half * 4, half * 4 + 4)])
                    k.cp(dst, po.v(lambda a: a.rearrange("p (j t) -> p j t", j=4)), eng='act' if half else 'dve')
            stageA(0)
            for sub in range(nsub):
                if sub + 1 < nsub:
                    stageA(sub + 1)
                stageB(sub)

        def attn_sample(l, tile):
            (t0, lc, n, kind) = tile
            pb = nbank()
            pb2 = nbank()
            for c in range(KC):
                tgt = pb if c < 4 else pb2
                k.tr(tgt[0:NS, (c % 4) * 128:(c % 4 + 1) * 128], T(YSt[:, c, :], ['YS']), IDF)
            tm = wpspan(0, 2).v(lambda a: a.rearrange("p a b -> p (a b)"))[0:NS, 0:D]
            k.cp(tm[:, 0:512], pb[0:NS, :], eng='act')
            k.cp(tm[:, 512:1024], pb2[0:NS, :], eng='act')
            k.dma(T(scr[7].rearrange("(b f) -> b f", b=NS), [('scr', 7)]), tm)
            SCT = wp(19, 128).v(lambda a: a.rearrange("p (m b h) -> p m b h", m=2, b=NS))
            for b in range(NS):
                qb = wpspan(2 + 2 * (b % 2), 2).v(lambda a: a.rearrange("p a b -> p (a b)")[:, 0:D])
                k.dma(qb, T(scr[7][b * D:(b + 1) * D].partition_broadcast(128), [('scr', 7)]))
                kb = wpspan(6 + 4 * (b % 2), 4).v(lambda a: a.rearrange("p a b -> p (a b)")[:, 0:2048].rearrange("p (m f) -> p m f", m=2))
                k.dma(kb, ck[l, b].rearrange("(m p) f -> p m f", p=128))
                for mc in range(2):
                    pr = wpspan(14, 2).v(lambda a: a.rearrange("p a b -> p (a b)")[:, 0:D])
                    k.tt(pr, kb[:, mc, :], qb, ALU.mult)
                    k.red(SCT[:, mc, b, :], pr.v(lambda a: a.rearrange("p (h d) -> p h d", h=4)), ALU.add)
            pbT = nbank()
            for mc in range(2):
                k.tr(pbT[0:64, mc * 128:(mc + 1) * 128], SCT[:, mc].v(lambda a: a.rearrange("p b h -> p (b h)")), IDF)
            sc = wp(16, 256)[0:64]
            k.cp(sc, pbT[0:64, 0:256], eng='act')
            mx = T(SMALLt[0:64, 10:11], ['SMALLmx2'])
            sm = T(SMALLt[0:64, 11:12], ['SMALLsm2'])
            k.red(mx, sc, ALU.max)
            k.ts(mx, mx, -1.0, ALU.mult)
            k.act(sc, sc, AF.Exp, bias=mx, accum=sm)
            k.recip(sm, sm)
            k.ts(sc, sc, sm, ALU.mult)
            pbP = nbank()
            for mc in range(2):
                k.tr(pbP[:, mc * 64:(mc + 1) * 64], sc[:, mc * 128:(mc + 1) * 128], IDF[0:64, 0:64])
            PTS = wp(17, 128).v(lambda a: a.rearrange("p (m j) -> p m j", m=2))
            k.cp(PTS, pbP[:, 0:128].v(lambda a: a.rearrange("p (m j) -> p m j", m=2)), eng='act')
            po = nbank()
            for b in range(NS):
                vb = wpspan(6 + 4 * (b % 2), 4).v(lambda a: a.rearrange("p a b -> p (a b)")[:, 0:2048].rearrange("p (m f) -> p m f", m=2))
                k.dma(vb, cv[l, b].rearrange("(m p) f -> p m f", p=128))
                for ec in range(KC):
                    h = ec // 2
                    for mc in range(2):
                        k.mm(po[:, ec * NS + b:ec * NS + b + 1], vb[:, mc, ec * 128:(ec + 1) * 128],
                             PTS[:, mc, b * 4 + h:b * 4 + h + 1], start=(mc == 0), stop=(mc == 1))
            dst = T(EBv[:, :, lc:lc + NS], [('EB', c, lc) for c in range(KC)])
            k.cp(dst, po[:, 0:KC * NS].v(lambda a: a.rearrange("p (c b) -> p c b", b=NS)), eng='act')

        S.dry = True
        try:
            program()
        except Stop:
            pass
        S.reset()
        rr['bank'] = 0
        rr['slot'] = 0
        rr['smb'] = 0
        rr['qb'] = 0
        rr['b8'] = 0
        rr['nbmod'] = 8
        wq.start_emit()
        try:
            program()
        except Stop:
            pass
        S.finish('sp')
        S.emit_all()
        nc._sched_stats = {q: len(v) for q, v in S.streams.items()}
        nc._sched_stats['waits'] = S.n_wait
        nc._marks = getattr(S, 'marks', [])
    return nc


_NC_CACHE = {}


def _get_nc(debug=False):
    if debug not in _NC_CACHE:
        _NC_CACHE[debug] = build_nc(debug)
    return _NC_CACHE[debug]


def make_in_maps(inp):
    f = lambda a: np.ascontiguousarray(np.asarray(a, dtype=np.float32))
    prow = np.zeros((2 * NRL, D), np.float32)
    for l in range(2):
        rows = []
        mu = f(inp['mu_shift'])[l]
        rows += [mu[0:1024], mu[1024:2048], mu[2048:3072], np.concatenate([mu[3072:3328], np.zeros(768, np.float32)])]
        rows += [f(inp['rw_w0'])[l], f(inp['rw_a0'])[l], f(inp['rw_k_k'])[l].reshape(-1), f(inp['rw_k_a'])[l].reshape(-1),
                 f(inp['rw_r_k'])[l].reshape(-1), f(inp['rw_lnx_g'])[l].reshape(-1), f(inp['rw_lnx_b'])[l].reshape(-1)]
        cw = f(inp['lru_conv_w'])[l]
        rows += [cw[0], cw[1], cw[2], cw[3], f(inp['lru_conv_b'])[l], f(inp['lru_ba'])[l], f(inp['lru_bx'])[l], f(inp['lru_lambda'])[l]]
        gb = f(inp['mix_gate_b'])[l]
        rows += [gb[0], gb[1]]
        rows += [f(inp[n])[l] for n in ('ln1_g', 'ln1_b', 'ln2_g', 'ln2_b', 'ln3_g', 'ln3_b')]
        assert len(rows) == NRL
        for j, r in enumerate(rows):
            prow[l * NRL + j] = r
    shared = {n: f(inp[n]) for n in ('w_in', 'rw_w2', 'rw_a2', 'rw_g2', 'rw_proj', 'lru_wa', 'lru_wx', 'lru_proj', 'w_out_mix',
                                     'xa_wq', 'xa_wk', 'xa_wv', 'xa_wo', 'mlp_up', 'mlp_down')}
    shared['prow'] = prow
    xp = f(inp['x_prompt'])
    xs = f(inp['x_sample']).reshape(128, D)
    mem = f(inp['mem_prompt'])
    ck = f(inp['cache_mem_k']).reshape(2, 128, 256, D)
    cv = f(inp['cache_mem_v']).reshape(2, 128, 256, D)
    srw = f(inp['state_rwkv']).reshape(2, 128, 16 * 64 * 64)
    ssh = f(inp['state_rwkv_shift'])
    shh = f(inp['state_lru_h'])
    scv = f(inp['state_lru_conv'])
    maps = []
    for b in range(8):
        sl = slice(b * NS, (b + 1) * NS)
        m = dict(shared)
        m['xp'] = xp[b]
        m['xs'] = np.ascontiguousarray(xs[sl])
        m['mem'] = mem[b]
        m['ck'] = np.ascontiguousarray(ck[:, sl])
        m['cv'] = np.ascontiguousarray(cv[:, sl])
        m['srw'] = np.ascontiguousarray(srw[:, sl]).reshape(2, 128, 8192)
        m['ssh'] = np.ascontiguousarray(ssh[:, sl])
        m['shh'] = np.ascontiguousarray(shh[:, sl])
        m['scv'] = np.ascontiguousarray(scv[:, sl])
        maps.append(m)
    return maps


def gather(results):
    R = results
    cat = lambda name, ax: np.concatenate([r[name] for r in R], axis=ax)
    y_p = np.stack([r['o_yp'] for r in R], 0)
    y_s = cat('o_ys', 0).reshape(128, 1, D)
    p_rw = np.stack([r['o_prw'] for r in R], 1)
    p_sh = np.stack([r['o_psh'] for r in R], 1)
    p_h = np.stack([r['o_ph'] for r in R], 1)
    p_cv = np.stack([r['o_pcv'] for r in R], 1)
    mk = np.stack([r['o_mk'] for r in R], 1).reshape(2, 8, 256, 4, 256)
    mv = np.stack([r['o_mv'] for r in R], 1).reshape(2, 8, 256, 4, 256)
    s_rw = np.concatenate([r['o_srw'].reshape(2, NS, 16, 64, 64) for r in R], axis=1)
    s_sh = cat('o_ssh', 1)
    s_h = cat('o_sh', 1)
    s_cv = cat('o_scv', 1)
    outs = (y_p, y_s, p_rw, p_sh, p_h, p_cv, mk, mv, s_rw, s_sh, s_h, s_cv)
    return tuple(np.ascontiguousarray(o, dtype=np.float32) for o in outs)


def kernel(**inputs):
    nc = _get_nc(False)
    maps = make_in_maps(inputs)
    res = run_bass_kernel_spmd(nc, maps, core_ids=list(range(8)))
    return gather(res.results)
```

```python
import math
import itertools
import numpy as np
from contextlib import ExitStack
import concourse.bass as bass
import concourse.mybir as mybir
from concourse.bass_utils import run_bass_kernel_spmd

F32 = mybir.dt.float32
BF16 = mybir.dt.bfloat16
AF = mybir.ActivationFunctionType
ALU = mybir.AluOpType
AX = mybir.AxisListType

COMPUTE = ('pe', 'act', 'dve', 'pool')
ALLQ = ('pe', 'act', 'dve', 'pool', 'sp')

D = 1024
KC = 8
SEQ = 2048
NS = 16
NT = 512
NIN = 7424
C_R, C_K, C_V, C_W, C_A, C_G = 0, 1024, 2048, 3072, 3136, 3200
NSH = 3328
C_U, C_Y, C_GATE = 3328, 4352, 5376
DEPTH = 2
ALPHA = float((2 * DEPTH) ** 0.25)
LN_EPS = 1e-5
GN_EPS = 64e-5
C0 = float(math.exp(-0.5))
NRL = 27
NPC = 30
CH = 64


class Sched:
    def __init__(self, nc, stack, n_dma_sems=40):
        self.nc = nc
        self.streams = {e: [] for e in ALLQ}
        self.sem = {e: stack.enter_context(nc.semaphore('s_' + e)) for e in COMPUTE}
        self.dsem = [stack.enter_context(nc.semaphore('d%d' % i)) for i in range(n_dma_sems)]
        self.reset()

    def reset(self):
        self.streams = {e: [] for e in ALLQ}
        self.cnt = {e: 0 for e in COMPUTE}
        self.dcnt = [0] * len(self.dsem)
        self.drr = 0
        self.seen = {q: {} for q in ALLQ}
        self.snap = {}
        self.lastw = {}
        self.readers = {}
        self.lastacc = {}
        self.n_wait = 0
        self.dry = False

    def _semh(self, key):
        return self.sem[key] if isinstance(key, str) else self.dsem[key]

    def _need(self, q, ev, waits):
        key, val = ev
        if self.seen[q].get(key, 0) >= val:
            return
        if key == q and q == 'pe':
            return
        if waits.get(key, 0) < val:
            waits[key] = val

    def _deps(self, q, reads, writes):
        waits = {}
        for k in reads:
            ev = self.lastw.get(k)
            if ev is not None:
                self._need(q, ev, waits)
        for k in writes:
            ev = self.lastw.get(k)
            if ev is not None:
                self._need(q, ev, waits)
            for ev in self.readers.get(k, ()):
                self._need(q, ev, waits)
        return waits

    def _apply_waits(self, q, waits):
        out = []
        for key, val in waits.items():
            if self.seen[q].get(key, 0) >= val:
                continue
            out.append((key, val))
            self.seen[q][key] = val
            sn = self.snap.get((key, val))
            if sn is not None:
                for e, v in zip(COMPUTE, sn):
                    if e == q:
                        continue
                    if self.seen[q].get(e, 0) < v:
                        self.seen[q][e] = v
        self.n_wait += len(out)
        return [(self._semh(k), v) for k, v in out]

    def _record(self, ev, reads, writes):
        for k in writes:
            self.lastw[k] = ev
            self.readers[k] = []
        for k in reads:
            self.readers.setdefault(k, []).append(ev)

    def op(self, q, fn, reads=(), writes=()):
        if self.dry:
            return None
        waits = self._deps(q, reads, writes)
        banks = set()
        for kk_ in reads:
            if isinstance(kk_, tuple) and kk_[0] == 'ps':
                banks.add(kk_[1])
        for kk_ in writes:
            if isinstance(kk_, tuple) and kk_[0] == 'ps':
                banks.add(kk_[1])
        for b_ in banks:
            ev0 = self.lastacc.get(b_)
            if ev0 is not None and ev0[0] != q:
                self._need(q, ev0, waits)
        wl = self._apply_waits(q, waits)
        self.cnt[q] += 1
        ev = (q, self.cnt[q])
        for b_ in banks:
            self.lastacc[b_] = ev
        self.snap[ev] = tuple(self.seen[q].get(e, 0) for e in COMPUTE)
        sem = self.sem[q]

        def emit(e, fn=fn, wl=wl, sem=sem):
            for s, v in wl:
                e.wait_ge(s, v)
            fn(e).then_inc(sem, 1)
        self.streams[q].append(emit)
        self._record(ev, reads, writes)
        return ev

    def dma(self, q, out, in_, reads=(), writes=(), **kw):
        if self.dry:
            return None
        waits = self._deps(q, reads, writes)
        i = self.drr
        self.drr = (self.drr + 1) % len(self.dsem)
        if self.dcnt[i] > 0:
            self._need(q, (i, 16 * self.dcnt[i]), waits)
        wl = self._apply_waits(q, waits)
        self.dcnt[i] += 1
        ev = (i, 16 * self.dcnt[i])
        self.snap[ev] = tuple(self.seen[q].get(e, 0) for e in COMPUTE)
        sem = self.dsem[i]

        def emit(e, out=out, in_=in_, wl=wl, sem=sem, kw=kw):
            for s, v in wl:
                e.wait_ge(s, v)
            e.dma_start(out=out, in_=in_, **kw).then_inc(sem, 16)
        self.streams[q].append(emit)
        self._record(ev, reads, writes)
        return ev

    def mark(self, name):
        if self.dry:
            return
        if not hasattr(self, 'marks'):
            self.marks = []
        self.marks.append((name, dict(self.cnt)))

    def barrier(self):
        if self.dry:
            return
        for q in ALLQ:
            waits = {}
            for e in COMPUTE:
                if self.cnt[e] and e != q:
                    self._need(q, (e, self.cnt[e]), waits)
            if q == 'sp':
                for i, c in enumerate(self.dcnt):
                    if c:
                        self._need(q, (i, 16 * c), waits)
            wl = self._apply_waits(q, waits)

            def emit(e, wl=wl):
                for s, v in wl:
                    e.wait_ge(s, v)
            self.streams[q].append(emit)

    def finish(self, q='sp'):
        waits = {}
        for i, c in enumerate(self.dcnt):
            if c:
                self._need(q, (i, 16 * c), waits)
        for e in COMPUTE:
            if self.cnt[e] and e != q:
                self._need(q, (e, self.cnt[e]), waits)
        wl = self._apply_waits(q, waits)

        def emit(e, wl=wl):
            for s, v in wl:
                e.wait_ge(s, v)
        self.streams[q].append(emit)

    def emit_all(self):
        nc = self.nc
        with nc.Block() as block:
            @block.tensor
            def _(e):
                for f in self.streams['pe']:
                    f(e)

            @block.scalar
            def _(e):
                for f in self.streams['act']:
                    f(e)

            @block.vector
            def _(e):
                for f in self.streams['dve']:
                    f(e)

            @block.gpsimd
            def _(e):
                for f in self.streams['pool']:
                    f(e)

            @block.sync
            def _(e):
                for f in self.streams['sp']:
                    f(e)


class T:
    __slots__ = ('ap', 'keys')

    def __init__(self, ap, keys):
        self.ap = ap
        self.keys = tuple(keys)

    def __getitem__(self, idx):
        return T(self.ap[idx], self.keys)

    def v(self, fn):
        return T(fn(self.ap), self.keys)

    def re(self, s, **kw):
        return T(self.ap.rearrange(s, **kw), self.keys)


def _ap(x):
    return x.ap if isinstance(x, T) else x


def _keys(*xs):
    out = []
    for x in xs:
        if isinstance(x, T):
            out.extend(x.keys)
    return out


class KB:
    def __init__(self, S):
        self.S = S

    def mm(self, out, lhsT, rhs, start=True, stop=True):
        o, l, r = _ap(out), _ap(lhsT), _ap(rhs)
        self.S.op('pe', lambda e: e.matmul(o, lhsT=l, rhs=r, start=start, stop=stop),
                  reads=_keys(lhsT, rhs), writes=_keys(out))

    def tr(self, out, in_, ident):
        o, i, d = _ap(out), _ap(in_), _ap(ident)
        self.S.op('pe', lambda e: e.transpose(out=o, in_=i, identity=d), reads=_keys(in_, ident), writes=_keys(out))

    def act(self, out, in_, func, bias=None, scale=None, accum=None):
        o, i = _ap(out), _ap(in_)
        kw = {}
        if bias is not None:
            kw['bias'] = _ap(bias)
        if scale is not None:
            kw['scale'] = _ap(scale)
        if accum is not None:
            kw['accum_out'] = _ap(accum)
        self.S.op('act', lambda e: e.activation(out=o, in_=i, func=func, **kw),
                  reads=_keys(in_, bias, scale), writes=_keys(out, accum))

    def tt(self, out, a, b, op, eng='dve'):
        if eng == 'pool':
            eng = 'dve'
        if eng == 'gp':
            eng = 'pool'
        o, x, y = _ap(out), _ap(a), _ap(b)
        self.S.op(eng, lambda e: e.tensor_tensor(out=o, in0=x, in1=y, op=op), reads=_keys(a, b), writes=_keys(out))

    def ts(self, out, a, s1, op0, s2=None, op1=None, eng='dve'):
        if eng == 'pool':
            eng = 'dve'
        if eng == 'gp':
            eng = 'pool'
        o, x, p, q = _ap(out), _ap(a), _ap(s1), _ap(s2)
        if op1 is None:
            self.S.op(eng, lambda e: e.tensor_scalar(out=o, in0=x, scalar1=p, scalar2=None, op0=op0),
                      reads=_keys(a, s1), writes=_keys(out))
        else:
            self.S.op(eng, lambda e: e.tensor_scalar(out=o, in0=x, scalar1=p, scalar2=q, op0=op0, op1=op1),
                      reads=_keys(a, s1, s2), writes=_keys(out))

    def stt(self, out, a, scalar, b, op0, op1):
        o, x, s, y = _ap(out), _ap(a), _ap(scalar), _ap(b)
        self.S.op('dve', lambda e: e.scalar_tensor_tensor(out=o, in0=x, scalar=s, in1=y, op0=op0, op1=op1),
                  reads=_keys(a, scalar, b), writes=_keys(out))

    def cp(self, out, in_, eng='dve'):
        o, i = _ap(out), _ap(in_)
        if eng == 'pool':
            eng = 'act'
        if eng == 'poolcast':
            eng = 'pool'
        if eng == 'act':
            self.S.op('act', lambda e: e.activation(out=o, in_=i, func=AF.Copy), reads=_keys(in_), writes=_keys(out))
        else:
            self.S.op(eng, lambda e: e.tensor_copy(out=o, in_=i), reads=_keys(in_), writes=_keys(out))

    def scan(self, out, d0, d1, init):
        o, x, y, z = _ap(out), _ap(d0), _ap(d1), _ap(init)
        self.S.op('dve', lambda e: e.tensor_tensor_scan(out=o, data0=x, data1=y, initial=z, op0=ALU.mult, op1=ALU.add),
                  reads=_keys(d0, d1, init), writes=_keys(out))

    def red(self, out, in_, op, axis=AX.X):
        o, i = _ap(out), _ap(in_)
        self.S.op('dve', lambda e: e.tensor_reduce(out=o, in_=i, axis=axis, op=op), reads=_keys(in_), writes=_keys(out))

    def recip(self, out, in_):
        o, i = _ap(out), _ap(in_)
        self.S.op('dve', lambda e: e.reciprocal(out=o, in_=i), reads=_keys(in_), writes=_keys(out))

    def memset(self, t, val, eng='pool'):
        o = _ap(t)
        self.S.op(eng, lambda e: e.memset(o, val), writes=_keys(t))

    def dma(self, out, in_, q='sp', **kw):
        self.S.dma(q, _ap(out), _ap(in_), reads=_keys(in_), writes=_keys(out), **kw)


class WQ:
    def __init__(self, k, stg, ring, la=3):
        self.k = k
        self.stg = stg
        self.ring = ring
        self.la = la
        self.specs = []
        self.collect = True
        self.i = 0
        self.issued = 0

    def start_emit(self):
        self.collect = False
        self.i = 0
        self.issued = 0

    def _issue(self, j):
        src = self.specs[j]
        dst = self.ring[j % len(self.ring)]
        self.k.dma(dst, src, q='pool')

    def next(self, src):
        if self.collect:
            self.specs.append(src)
            return self.ring[0]
        i = self.i
        self.i += 1
        lim = min(i + self.la, len(self.specs) - 1)
        while self.issued <= lim:
            self._issue(self.issued)
            self.issued += 1
        return self.ring[i % len(self.ring)]


class Stop(Exception):
    pass


def build_nc(debug=False, stop_at=None, short=False):
    nc = bass.Bass("TRN2", target_bir_lowering=False)

    def din(name, shape):
        return nc.dram_tensor(name, list(shape), F32, kind="ExternalInput").ap()

    def dout(name, shape):
        return nc.dram_tensor(name, list(shape), F32, kind="ExternalOutput").ap()

    xp = din("xp", [SEQ, D])
    xs = din("xs", [NS, D])
    mem = din("mem", [256, D])
    ck = din("ck", [2, NS, 256, D])
    cv = din("cv", [2, NS, 256, D])
    srw = din("srw", [2, 128, 8192])
    ssh = din("ssh", [2, NS, NSH])
    shh = din("shh", [2, NS, D])
    scv = din("scv", [2, NS, 3, D])
    prow = din("prow", [2 * NRL, D])
    w_in = din("w_in", [2, D, NIN])
    rw_w2 = din("rw_w2", [2, 64, D])
    rw_a2 = din("rw_a2", [2, 64, D])
    rw_g2 = din("rw_g2", [2, 128, D])
    rw_proj = din("rw_proj", [2, D, D])
    lru_wa = din("lru_wa", [2, 16, 64, 64])
    lru_wx = din("lru_wx", [2, 16, 64, 64])
    lru_proj = din("lru_proj", [2, D, D])
    w_out_mix = din("w_out_mix", [2, D, D])
    xa_wq = din("xa_wq", [2, D, D])
    xa_wk = din("xa_wk", [2, D, D])
    xa_wv = din("xa_wv", [2, D, D])
    xa_wo = din("xa_wo", [2, D, D])
    mlp_up = din("mlp_up", [2, D, 4 * D])
    mlp_down = din("mlp_down", [2, 4 * D, D])

    o_yp = dout("o_yp", [SEQ, D])
    o_ys = dout("o_ys", [NS, D])
    o_prw = dout("o_prw", [2, 16, 64, 64])
    o_psh = dout("o_psh", [2, NSH])
    o_ph = dout("o_ph", [2, D])
    o_pcv = dout("o_pcv", [2, 3, D])
    o_mk = dout("o_mk", [2, 256, D])
    o_mv = dout("o_mv", [2, 256, D])
    o_srw = dout("o_srw", [2, 128, 8192])
    o_ssh = dout("o_ssh", [2, NS, NSH])
    o_sh = dout("o_sh", [2, NS, D])
    o_scv = dout("o_scv", [2, NS, 3, D])
    scr = nc.dram_tensor("scr", [8, NS * D], F32, kind="Internal").ap()
    dbg = {}
    if debug:
        dbg['x1'] = dout("dbg_x1", [128, 8, 1040])
        dbg['yb'] = dout("dbg_yb", [128, 8, 1040])

    with ExitStack() as st:
        S = Sched(nc, st)
        k = KB(S)

        def sb(name, shape, dt):
            return st.enter_context(nc.sbuf_tensor(name, list(shape), dt))

        PP = 1040
        X32t = sb("X32", [128, KC, PP], F32)
        XBt = sb("XB", [128, KC, PP], BF16)
        YBt = sb("YB", [128, KC, PP], BF16)
        EBt = sb("EB", [128, KC * PP], BF16)
        LWt = sb("LW", [128, 2, PP], BF16)
        KTt = sb("KT", [128, 2, KC, 256], BF16)
        VNt = sb("VN", [128, 2, 2, D], BF16)
        NB = 20
        WPt = sb("WP", [128, NB, 516], F32)
        RINGt = sb("RING", [128, 10, KC, 128], BF16)
        PARt = sb("PAR", [128, KC, 2, NPC], F32)
        CONt = sb("CON", [128, 8, 128], F32)
        CONBt = sb("CONB", [128, 128], BF16)
        RSTt = sb("RST", [128, NT], F32)
        W2A2t = sb("W2A2", [128, D], BF16)
        G2t = sb("G2", [128, D], BF16)
        WABt = sb("WAB", [128, 2, KC, 128], BF16)
        HSTt = sb("HST", [128, 2, 8, 128], F32)
        HBFt = sb("HBF", [128, 64], BF16)
        HSPt = sb("HSP", [128, 64], F32)
        ZUt = sb("ZU", [128, 2, 64], BF16)
        PCt = sb("PCT", [128, 2, 8], F32)
        HOUTt = sb("HOUT", [64, 128], F32)
        MSKt = sb("MSK", [128, 3, 64], F32)
        IDSt = sb("IDS", [128, 64], BF16)
        CARt = sb("CAR", [128, 2, 40], F32)
        CVCt = sb("CVC", [128, 2, 8, 3], F32)
        SHOt = sb("SHO", [128, 26, 1 + NS], F32)
        HOt = sb("HO", [128, 8, 1 + NS], F32)
        CVOt = sb("CVO", [128, 8, 3, 1 + NS], F32)
        SHSt = sb("SHS", [128, 26, NS], F32)
        HS0t = sb("HS0", [128, 8, NS], F32)
        CSt = sb("CS", [128, 8, 3, NS], F32)
        SVt = sb("SV", [128, 6, 8, NS], F32)
        SV2t = sb("SV2", [128, 6, 128], F32)
        YSt = sb("YS", [128, 8, NS], F32)
        SMALLt = sb("SMALL", [128, 64], F32)

        ps = [st.enter_context(nc.psum_tensor("ps%d" % i, [128, 512], F32)) for i in range(8)]

        def psbank(b):
            return T(ps[b][:], [('ps', b, s) for s in range(4)])

        def psslot(b, s):
            return T(ps[b][:, s * 128:(s + 1) * 128], [('ps', b, s)])

        rr = {'bank': 0, 'slot': 0}

        def nbank():
            m_ = rr.get('nbmod', 8)
            b = rr['bank'] % m_
            rr['bank'] = (b + 1) % m_
            return psbank(b)

        def nslot():
            s = rr['slot']
            rr['slot'] = (s + 1) % 16
            return psslot(4 + s % 4, s // 4)

        def wp(i, n=None):
            t = T(WPt[:, i, :], [('wp', i, s_) for s_ in range(4)])
            return t if n is None else t[:, :n]

        def wpspan(i, cnt):
            return T(WPt[:, i:i + cnt, :], [('wp', j, s_) for j in range(i, i + cnt) for s_ in range(4)])

        def slot(i, s0, ns=1):
            return T(WPt[:, i, :].bitcast(BF16)[:, s0 * 256:(s0 + ns) * 256], [('wp', i, s_) for s_ in range(s0, s0 + ns)])

        X32 = lambda c, lc, n: T(X32t[:, c, lc:lc + n], [('X32', c, lc)])
        XB = lambda c, lc, n: T(XBt[:, c, lc:lc + n], [('XB', c, lc)])
        YB = lambda c, lc, n: T(YBt[:, c, lc:lc + n], [('YB', c, lc)])
        EBv = EBt[:].rearrange("p (c n) -> p c n", c=KC)
        EB = lambda c, lc, n: T(EBv[:, c, lc:lc + n], [('EB', c, lc)])
        LW = lambda i, lc, n: T(LWt[:, i, lc:lc + n], [('LW', i, lc)])
        CON = lambda i: T(CONt[:, i, :], [('CON', i)])
        IDF, BONES, BONES64, ONESD, M_LE, M_LT, M_GT = [CON(i) for i in range(7)]
        IDB = T(CONBt[:], ['IDB'])
        RST = T(RSTt[:], ['RST'])
        PAR = T(PARt[:], ['PAR'])

        def par(l, j, c, lo=0, hi=128):
            return T(PARt[lo:hi, c, l, j:j + 1], ['PAR'])

        RING = [T(RINGt[:, i], [('ring', i)]) for i in range(10)]
        wq = WQ(k, None, RING, la=6)

        def wsrc(w, l, c0, r0=0):
            return w[l, r0:r0 + D, c0:c0 + 128].rearrange("(kc p) n -> p kc n", p=128)

        def arena(off, n):
            return EBt[:, off:off + n]
        BLK = {}
        for i, nm in enumerate(['R', 'K', 'B', 'A', 'V']):
            BLK[nm] = T(arena(i * 512, 512), [('blk', nm)])

        npass = [2]
        dbg_pi = 0 if short else 1

        def program():
            k.memset(T(CONt[:], [('CON', i) for i in range(8)]), 0.0)
            k.memset(IDB, 0.0)
            S.op('pool', lambda e: e.affine_select(out=CONt[:, 0, :], in_=CONt[:, 0, :], compare_op=ALU.not_equal, fill=1.0,
                                                   base=0, pattern=[[-1, 128]], channel_multiplier=1),
                 reads=IDF.keys, writes=IDF.keys)
            S.op('pool', lambda e: e.affine_select(out=CONBt[:], in_=CONBt[:], compare_op=ALU.not_equal, fill=1.0,
                                                   base=0, pattern=[[-1, 128]], channel_multiplier=1),
                 reads=IDB.keys, writes=IDB.keys)
            for (lo, hi) in ((0, 64), (64, 128)):
                k.memset(T(CONt[lo:hi, 1, lo:hi], BONES.keys), 1.0)
                k.memset(T(CONt[lo:hi, 2, lo:hi], BONES64.keys), 1.0 / 64.0)
                for m in (4, 5, 6):
                    k.memset(T(CONt[lo:hi, m, lo:hi], CON(m).keys), 1.0)
            k.memset(ONESD, 1.0 / 1024.0)
            S.op('pool', lambda e: e.affine_select(out=CONt[:, 4, :], in_=CONt[:, 4, :], compare_op=ALU.is_ge, fill=0.0,
                                                   base=0, pattern=[[1, 128]], channel_multiplier=-1),
                 reads=M_LE.keys, writes=M_LE.keys)
            S.op('pool', lambda e: e.affine_select(out=CONt[:, 5, :], in_=CONt[:, 5, :], compare_op=ALU.is_gt, fill=0.0,
                                                   base=0, pattern=[[1, 128]], channel_multiplier=-1),
                 reads=M_LT.keys, writes=M_LT.keys)
            S.op('pool', lambda e: e.affine_select(out=CONt[:, 6, :], in_=CONt[:, 6, :], compare_op=ALU.is_gt, fill=0.0,
                                                   base=0, pattern=[[-1, 128]], channel_multiplier=1),
                 reads=M_GT.keys, writes=M_GT.keys)
            for mi, src_ in enumerate((M_LE, M_LT, M_GT)):
                k.tt(T(MSKt[:, mi, :], ['MSK']), src_[:, 0:64], src_[:, 64:128], ALU.add)
            k.tt(T(IDSt[:], ['IDS']), IDB[:, 0:64], IDB[:, 64:128], ALU.add)
            k.memset(RST, 1.0)
            k.memset(T(RSTt[:, 0:NT:CH], RST.keys), 0.0)
            k.memset(T(HSTt[:], ['HST']), 0.0)
            k.memset(T(CARt[:], ['CAR']), 0.0)
            k.memset(T(CVCt[:], ['CVC']), 0.0)
            S.barrier()

            PR = T(WPt[0:2 * NRL, 0:2, :].rearrange("p a b -> p (a b)")[:, 0:D], [('wp', 0, s_) for s_ in range(4)] + [('wp', 1, s_) for s_ in range(4)])
            k.dma(PR, prow)
            for c in range(KC):
                pb = nbank()
                k.tr(pb[:, 0:2 * NRL], PR[:, c * 128:(c + 1) * 128], IDF[0:2 * NRL, 0:2 * NRL])
                for l in range(2):
                    k.cp(T(PARt[:, c, l, 0:NRL], ['PAR']), pb[:, l * NRL:(l + 1) * NRL], eng='act')
            for l in range(2):
                pv = T(PARt[:, :, l, :], ['PAR'])
                k.ts(pv[:, :, 27], pv[:, :, 7], -1.0, ALU.mult, 1.0, ALU.add)
                k.act(pv[:, :, 28], pv[:, :, 18], AF.Exp, scale=-1.0)
                k.act(pv[:, :, 28], pv[:, :, 28], AF.Ln, bias=1.0)
                k.ts(pv[:, :, 29], pv[:, :, 28], -16.0, ALU.mult)
                k.ts(pv[:, :, 28], pv[:, :, 28], -8.0, ALU.mult)

            if stop_at == 'const':
                raise Stop()
            MT = T(WPt[:, 16:18, :].rearrange("p a b -> p (a b)")[:, 0:1024].bitcast(BF16).rearrange("p (c m) -> p c m", c=8),
                   [('wp', 16, s_) for s_ in range(4)] + [('wp', 17, s_) for s_ in range(4)])
            for mc in range(2):
                mt = wpspan(0, 2).v(lambda a: a.rearrange("p a b -> p (a b)")[:, 0:D])
                k.dma(mt, mem[mc * 128:(mc + 1) * 128, :])
                for half in range(2):
                    pb = nbank()
                    for j in range(4):
                        c = half * 4 + j
                        k.tr(pb[:, j * 128:(j + 1) * 128], mt[:, c * 128:(c + 1) * 128], IDF)
                    k.cp(MT[:, half * 4:half * 4 + 4, mc * 128:(mc + 1) * 128],
                         pb.v(lambda a: a.rearrange("p (j m) -> p j m", j=4)), eng='act')
            if stop_at == 'memT':
                raise Stop()
            for l in range(2):
                for which, wsrc_t, odram in ((0, xa_wk, o_mk), (1, xa_wv, o_mv)):
                    if stop_at == 'kv0' and (l, which) == (0, 1):
                        raise Stop()
                    NAT = wpspan(2, 4).v(lambda a: a.rearrange("p a b -> p (a b)")[:, 0:2048].rearrange("p (m e) -> p m e", m=2))
                    for e in range(KC):
                        W = wq.next(wsrc(wsrc_t, l, e * 128))
                        if which == 0:
                            pb = nbank()
                            for kc in range(KC):
                                k.mm(pb[:, 0:256], W[:, kc, :], MT[:, kc, :], start=(kc == 0), stop=(kc == KC - 1))
                            k.cp(T(KTt[:, l, e, :], [('KT', l)]), pb[:, 0:256], eng='act')
                        pb = nbank()
                        for mc in range(2):
                            for kc in range(KC):
                                k.mm(pb[:, mc * 128:(mc + 1) * 128], MT[:, kc, mc * 128:(mc + 1) * 128], W[:, kc, :],
                                     start=(kc == 0), stop=(kc == KC - 1))
                        k.cp(NAT[:, :, e * 128:(e + 1) * 128], pb[:, 0:256].v(lambda a: a.rearrange("p (m n) -> p m n", m=2)), eng='dve')
                        if which == 1:
                            k.cp(T(VNt[:, l, :, e * 128:(e + 1) * 128], [('VN', l)]),
                                 pb[:, 0:256].v(lambda a: a.rearrange("p (m n) -> p m n", m=2)), eng='act')
                    k.dma(odram[l].rearrange("(m p) e -> p m e", p=128), NAT)

            if stop_at == 'memkv':
                raise Stop()
            passes = [
                [(0, 0, NT, 'p'), (512, 512, NT, 'p')],
                [(1024, 0, NT, 'p'), (1536, 512, NT, 'p'), (None, 1024, NS, 's')],
            ]
            if short:
                passes = [[(0, 0, NT, 'p'), (512, 512, NT, 'p'), (None, 1024, NS, 's')]]
            npass[0] = len(passes)
            for pi, tiles in enumerate(passes):
                S.mark('LOADX p%d' % pi)
                load_x(tiles)
                if stop_at != 'loadx':
                    for l in range(2):
                        layer_pass(l, pi, tiles)
                S.mark('STOREY p%d' % pi)
                store_y(tiles)
            S.mark('END')

        def load_x(tiles):
            for (t0, lc, n, kind) in tiles:
                nblk = (n + 127) // 128
                for bi in range(nblk):
                    nb_ = min(128, n - bi * 128)
                    xin = wpspan(0, 2).v(lambda a: a.rearrange("p a b -> p (a b)")[:, 0:D])
                    src = xp[t0 + bi * 128:t0 + bi * 128 + nb_, :] if kind == 'p' else xs[:, :]
                    k.dma(xin[0:nb_, :], src)
                    for half in range(2):
                        pb = nbank()
                        for j in range(4):
                            c = half * 4 + j
                            k.tr(pb[:, j * 128:j * 128 + nb_], xin[0:nb_, c * 128:(c + 1) * 128], IDF[0:nb_, 0:nb_])
                        pv = pb.v(lambda a: a.rearrange("p (j m) -> p j m", j=4)[:, :, 0:nb_])
                        c0_ = half * 4
                        dst32 = T(X32t[:, c0_:c0_ + 4, lc + bi * 128:lc + bi * 128 + nb_], [('X32', c, lc) for c in range(c0_, c0_ + 4)])
                        dstb = T(XBt[:, c0_:c0_ + 4, lc + bi * 128:lc + bi * 128 + nb_], [('XB', c, lc) for c in range(c0_, c0_ + 4)])
                        k.cp(dst32, pv, eng='act')
                        k.cp(dstb, pv, eng='dve')

        def store_y(tiles):
            for (t0, lc, n, kind) in tiles:
                nblk = (n + 127) // 128
                for bi in range(nblk):
                    nb_ = min(128, n - bi * 128)
                    yo = wpspan(2, 2).v(lambda a: a.rearrange("p a b -> p (a b)")[:, 0:D])
                    for half in range(2):
                        pb = nbank()
                        for j in range(4):
                            c = half * 4 + j
                            src = T(X32t[:, c, lc + bi * 128:lc + bi * 128 + nb_], [('X32', c, lc)])
                            k.tr(pb[0:nb_, j * 128:(j + 1) * 128], src, IDF)
                        k.cp(yo[0:nb_, half * 512:(half + 1) * 512], pb[0:nb_, :], eng='act' if half == 0 else 'dve')
                    dst = o_yp[t0 + bi * 128:t0 + bi * 128 + nb_, :] if kind == 'p' else o_ys[:, :]
                    k.dma(dst, yo[0:nb_, :])

        def proj(W, lc, n, pb=None, src=XB):
            if pb is None:
                pb = nbank()
            for kc in range(KC):
                k.mm(pb[:, 0:n], W[:, kc, :], src(kc, lc, n), start=(kc == 0), stop=(kc == KC - 1))
            return pb

        def proj_shift(l, cid, W, tile, raw, dtmp, last_p):
            (t0, lc, n, kind) = tile
            pb = proj(W, lc, n)
            k.cp(raw[:, 1:n + 1], pb[:, 0:n], eng='act')
            mu = par(l, cid // 8 if cid < 24 else 3, cid % 8 if cid < 24 else cid - 24)
            car = T(CARt[:, l, cid:cid + 1], [('CAR', l, cid)])
            if kind == 'p':
                k.cp(raw[:, 0:1], car, eng='pool')
                prev = raw[:, 0:n]
                k.cp(car, raw[:, n:n + 1], eng='pool')
                if last_p:
                    k.cp(T(SHOt[:, cid, 0:1], [('SHO', cid)]), raw[:, n:n + 1], eng='pool')
            else:
                prev = T(SHSt[:, cid, :], ['SHS'])
                k.cp(T(SHOt[:, cid, 1:1 + NS], [('SHO', cid)]), raw[:, 1:n + 1], eng='pool')
            k.tt(dtmp[:, 0:n], prev, raw[:, 1:n + 1], ALU.subtract)
            k.stt(raw[:, 1:n + 1], dtmp[:, 0:n], mu, raw[:, 1:n + 1], ALU.mult, ALU.add)
            return raw[:, 1:n + 1]

        def layernorm(l, jg, jb, tile):
            (t0, lc, n, kind) = tile
            pm = nbank()
            pq = nbank()
            for c in range(KC):
                sq = wp(c % 3, n)
                k.act(sq, X32(c, lc, n), AF.Square)
                k.mm(pm[:, 0:n], ONESD, X32(c, lc, n), start=(c == 0), stop=(c == KC - 1))
                k.mm(pq[:, 0:n], ONESD, sq, start=(c == 0), stop=(c == KC - 1))
            mean = wp(3, n)
            rstd = wp(4, n)
            k.cp(mean, pm[:, 0:n], eng='act')
            k.tt(rstd, mean, mean, ALU.mult)
            k.tt(rstd, pq[:, 0:n], rstd, ALU.subtract)
            k.act(rstd, rstd, AF.Sqrt, bias=T(SMALLt[:, 0:1], ['SMALL0']))
            k.recip(rstd, rstd)
            for c in range(KC):
                d = wp(5 + c % 4, n)
                e_ = 'gp' if c % 2 else 'dve'
                k.tt(d, X32(c, lc, n), mean, ALU.subtract, eng=e_)
                k.tt(d, d, rstd, ALU.mult, eng=e_)
                k.ts(X32(c, lc, n), d, par(l, jg, c), ALU.mult, par(l, jb, c), ALU.add, eng=e_)
                k.act(XB(c, lc, n), d, AF.Identity, bias=par(l, jb, c), scale=par(l, jg, c))

        def resid_stage(wsrc_fn, src, tiles, first=True):
            for e in range(KC):
                W = wq.next(wsrc_fn(e))
                for (t0, lc, n, kind) in tiles:
                    pb = proj(W, lc, n, src=src)
                    if first:
                        k.stt(X32(e, lc, n), X32(e, lc, n), ALPHA, pb[:, 0:n], ALU.mult, ALU.add)
                    else:
                        k.tt(X32(e, lc, n), X32(e, lc, n), pb[:, 0:n], ALU.add)

        def layer_pass(l, pi, tiles):
            last_p_tile = max((i for i, t in enumerate(tiles) if t[3] == 'p'))
            is_last_pass = (pi == npass[0] - 1)
            has_s = any(t[3] == 's' for t in tiles)
            k.dma(T(W2A2t[0:64, :], ['W2A2']), rw_w2[l], q='pool')
            k.dma(T(W2A2t[64:128, :], ['W2A2']), rw_a2[l], q='pool')
            k.dma(T(G2t[:], ['G2']), rw_g2[l], q='pool')
            k.memset(T(WABt[:], ['WAB']), 0.0)
            for c in range(KC):
                for hh in range(2):
                    lo = hh * 64
                    k.dma(T(WABt[lo:lo + 64, 0, c, lo:lo + 64], ['WAB']), lru_wa[l, 2 * c + hh], q='pool')
                    k.dma(T(WABt[lo:lo + 64, 1, c, lo:lo + 64], ['WAB']), lru_wx[l, 2 * c + hh], q='pool')
            k.memset(T(SMALLt[:, 0:1], ['SMALL0']), LN_EPS)
            k.memset(T(SMALLt[:, 1:2], ['SMALL1']), GN_EPS)
            if has_s:
                load_sample_states(l)

            S.mark('M0 l%d p%d' % (l, pi))
            for ci in range(2):
                W = wq.next(wsrc(w_in, l, C_W + ci * 128))
                for ti, tile in enumerate(tiles):
                    (t0, lc, n, kind) = tile
                    psf = proj_shift(l, 24 + ci, W, tile, wp(0), wp(1), is_last_pass and ti == last_p_tile)
                    if ci == 0:
                        k.act(LW(0, lc, n)[0:64], psf[0:64], AF.Tanh)
                        k.cp(LW(0, lc, n)[64:128], psf[64:128], eng='act')
                    else:
                        k.act(LW(1, lc, n), psf, AF.Sigmoid)

            if stop_at == 'm0':
                raise Stop()
            S.mark('M1 l%d p%d' % (l, pi))
            S.barrier()
            rr['nbmod'] = 3
            iters = [(hp, ti, tile) for hp in range(KC) for ti, tile in enumerate(tiles)]
            Wd = {}

            def getW(hp):
                if hp not in Wd:
                    Wd[hp] = (wq.next(wsrc(w_in, l, C_R + hp * 128)), wq.next(wsrc(w_in, l, C_K + hp * 128)),
                              wq.next(wsrc(w_in, l, C_V + hp * 128)))
                return Wd[hp]

            def mkAB(i):
                hp, ti, tile = iters[i]
                ctx = {}
                return ctx, rwkv_AB_gen(l, hp, tile, getW(hp), is_last_pass and ti == last_p_tile, i % 2, ctx)
            ctx, gAB = mkAB(0)
            for _ in gAB:
                pass
            pend_post = None
            for i, (hp, ti, tile) in enumerate(iters):
                nctx, ngen = mkAB(i + 1) if i + 1 < len(iters) else (None, None)
                chain_ = itertools.chain(pend_post if pend_post is not None else [], ngen if ngen is not None else [])
                pend_post = None
                if tile[3] == 'p':
                    pend_post = rwkv_C(l, hp, tile, ctx, chain_)
                for _ in chain_:
                    pass
                ctx = nctx
                if is_last_pass and ti == len(tiles) - 1:
                    Hs = T(HSTt[:, l, hp, 0:64], [('HST', l, hp)])
                    pb = nbank()
                    k.tr(pb[0:64, 0:128], Hs, IDF)
                    ho = T(HOUTt[:], ['HOUT'])
                    k.cp(ho[0:64], pb[0:64, 0:128], eng='act')
                    for hh in range(2):
                        lo = hh * 64
                        k.dma(o_prw[l, 2 * hp + hh], ho[0:64, lo:lo + 64])
            if pend_post is not None:
                for _ in pend_post:
                    pass
            rr['nbmod'] = 8
            if has_s:
                sample_rwkv_state(l)
                stile = [t for t in tiles if t[3] == 's'][0]
                for hp in range(KC):
                    rwkv_post(l, hp, stile, T(YSt[:, hp, :], ['YS']), T(SVt[:, 2, hp, :], ['SV']), None, None, sample=True)
            S.barrier()
            if debug and l == 0 and pi == dbg_pi:
                k.dma(dbg['yb'], T(YBt[:], [('YB', c, lc) for c in range(KC) for lc in (0, 512, 1024)]), q='pool')

            if stop_at == 'm1':
                raise Stop()
            S.mark('M2 l%d p%d' % (l, pi))
            for e in range(KC):
                Wp = wq.next(wsrc(rw_proj, l, e * 128))
                Wg = wq.next(wsrc(w_in, l, C_GATE + e * 128))
                for (t0, lc, n, kind) in tiles:
                    po = proj(Wp, lc, n, src=YB)
                    pg = proj(Wg, lc, n)
                    g0 = wp(0, n)
                    k.act(g0, pg[:, 0:n], AF.Sigmoid, bias=par(l, 19, e))
                    k.tt(EB(e, lc, n), g0, po[:, 0:n], ALU.mult)

            if stop_at == 'm2':
                raise Stop()
            S.mark('M3 l%d p%d' % (l, pi))
            for c in range(0, KC, 2):
                Ws = [(wq.next(wsrc(w_in, l, C_U + cc * 128)), wq.next(wsrc(w_in, l, C_Y + cc * 128))) for cc in (c, c + 1)]
                for ti, tile in enumerate(tiles):
                    alive = [lru_gen(l, c + d_, tile, Ws[d_][0], Ws[d_][1], is_last_pass and ti == last_p_tile, 10 * d_) for d_ in range(2)]
                    while alive:
                        for g_ in list(alive):
                            try:
                                next(g_)
                            except StopIteration:
                                alive.remove(g_)

            if stop_at == 'm3':
                raise Stop()
            S.mark('M4 l%d p%d' % (l, pi))
            for e in range(KC):
                Wp = wq.next(wsrc(lru_proj, l, e * 128))
                Wg = wq.next(wsrc(w_in, l, C_GATE + D + e * 128))
                for (t0, lc, n, kind) in tiles:
                    po = proj(Wp, lc, n, src=YB)
                    pg = proj(Wg, lc, n)
                    g1 = wp(0, n)
                    k.act(g1, pg[:, 0:n], AF.Sigmoid, bias=par(l, 20, e))
                    k.tt(g1, g1, po[:, 0:n], ALU.mult)
                    k.tt(EB(e, lc, n), EB(e, lc, n), g1, ALU.add)

            S.mark('M5 l%d p%d' % (l, pi))
            resid_stage(lambda e: wsrc(w_out_mix, l, e * 128), EB, tiles)
            for tile in tiles:
                layernorm(l, 21, 22, tile)
            if debug and l == 0 and pi == dbg_pi:
                k.dma(dbg['x1'], T(X32t[:], [('X32', c, lc) for c in range(KC) for lc in (0, 512, 1024)]))

            if stop_at == 'm5':
                raise Stop()
            S.mark('AT l%d p%d' % (l, pi))
            for e in range(KC):
                W = wq.next(wsrc(xa_wq, l, e * 128))
                for (t0, lc, n, kind) in tiles:
                    pb = proj(W, lc, n)
                    if kind == 'p':
                        k.act(YB(e, lc, n), pb[:, 0:n], AF.Copy, scale=1.0 / 16.0)
                    else:
                        k.act(T(YSt[:, e, :], ['YS']), pb[:, 0:n], AF.Copy, scale=1.0 / 16.0)
            for tile in tiles:
                if tile[3] == 'p':
                    attn_prompt(l, tile)
                else:
                    attn_sample(l, tile)
            resid_stage(lambda e: wsrc(xa_wo, l, e * 128), EB, tiles)
            for tile in tiles:
                layernorm(l, 23, 24, tile)

            if stop_at == 'attn':
                raise Stop()
            S.mark('MLP l%d p%d' % (l, pi))
            for g in range(4):
                for j in range(KC):
                    W = wq.next(wsrc(mlp_up, l, g * D + j * 128))
                    for (t0, lc, n, kind) in tiles:
                        pb = proj(W, lc, n)
                        r_ = wp(j % 2, n)
                        k.act(r_, pb[:, 0:n], AF.Relu)
                        k.tt(EB(j, lc, n), r_, r_, ALU.mult, eng='gp' if j % 2 else 'dve')
                resid_stage(lambda e: wsrc(mlp_down, l, e * 128, r0=g * D), EB, tiles, first=(g == 0))
            for tile in tiles:
                layernorm(l, 25, 26, tile)

            if stop_at == 'mlp':
                raise Stop()
            S.mark('OUT l%d p%d' % (l, pi))
            if is_last_pass:
                def store_T(get_src, nchk, dst, npart):
                    tm = wpspan(8, 7).v(lambda a_: a_.rearrange("p a b -> p (a b)"))[0:npart, 0:nchk * 128]
                    for c0_ in range(0, nchk, 4):
                        pb = nbank()
                        cn = min(4, nchk - c0_)
                        for c in range(cn):
                            k.tr(pb[0:npart, c * 128:(c + 1) * 128], get_src(c0_ + c), IDF)
                        k.cp(tm[:, c0_ * 128:(c0_ + cn) * 128], pb[0:npart, 0:cn * 128], eng='act')
                    k.dma(dst, tm)
                shk = [('SHO', c) for c in range(26)]
                store_T(lambda c: T(SHOt[:, :, 0], shk), 1, o_psh[l].rearrange("(c p) -> c p", p=128), 26)
                store_T(lambda c: T(HOt[:, :, 0], ['HO']), 1, o_ph[l].rearrange("(c p) -> c p", p=128), 8)
                for j in range(3):
                    store_T(lambda c, j=j: T(CVOt[:, :, j, 0], ['CVO']), 1, o_pcv[l, j].rearrange("(c p) -> c p", p=128), 8)
                store_T(lambda c: T(SHOt[:, c, 1:1 + NS], [('SHO', c)]), 26, o_ssh[l], NS)
                store_T(lambda c: T(HOt[:, c, 1:1 + NS], ['HO']), 8, o_sh[l], NS)
                for j in range(3):
                    store_T(lambda c, j=j: T(CVOt[:, c, j, 1:1 + NS], ['CVO']), 8, o_scv[l, :, j, :], NS)

        def load_sample_states(l):
            def tload(src2d, ncols, dstfn):
                tm = wpspan(8, 7).v(lambda a: a.rearrange("p a b -> p (a b)"))[0:NS, 0:ncols]
                k.dma(tm, src2d)
                nch = ncols // 128
                for c0_ in range(0, nch, 26):
                    pb = nbank()
                    cn = min(26, nch - c0_)
                    for c in range(cn):
                        k.tr(pb[:, c * NS:(c + 1) * NS], tm[:, (c0_ + c) * 128:(c0_ + c + 1) * 128], IDF[0:NS, 0:NS])
                    dstfn(c0_, cn, pb)
            tload(ssh[l], NSH, lambda c0_, cn, pb: k.cp(T(SHSt[:, c0_:c0_ + cn, :], ['SHS']),
                                                        pb[:, 0:cn * NS].v(lambda a: a.rearrange("p (c b) -> p c b", b=NS)), eng='act'))
            tload(shh[l], D, lambda c0_, cn, pb: k.cp(T(HS0t[:, c0_:c0_ + cn, :], ['HS0']),
                                                      pb[:, 0:cn * NS].v(lambda a: a.rearrange("p (c b) -> p c b", b=NS)), eng='act'))
            for j in range(3):
                tload(scv[l, :, j, :], D, lambda c0_, cn, pb, j=j: k.cp(T(CSt[:, c0_:c0_ + cn, j, :], ['CS']),
                                                                        pb[:, 0:cn * NS].v(lambda a: a.rearrange("p (c b) -> p c b", b=NS)), eng='act'))

        ARB = [T(EBt[:, 5120 + i_ * 1032:5120 + (i_ + 1) * 1032].bitcast(F32), [('ar', i_, s_) for s_ in range(4)]) for i_ in range(3)]
        BLKP = [{nm: T(arena((P_ * 5 + i_) * 512, 512), [('blk', P_, nm)]) for i_, nm in enumerate(['R', 'K', 'B', 'A', 'V'])}
                for P_ in range(2)]

        def rwkv_AB_gen(l, hp, tile, W3, last_p, P, ctx):
            (t0, lc, n, kind) = tile
            dt = ARB[0]
            r = proj_shift(l, hp, W3[0], tile, wp(0), dt, last_p)
            yield
            kx = proj_shift(l, 8 + hp, W3[1], tile, wp(1), dt, last_p)
            yield
            vx = proj_shift(l, 16 + hp, W3[2], tile, wp(2), dt, last_p)
            yield
            sw = wp(5, n)
            asg = wp(17, n)
            kk = wp(18, n)
            t8 = wp(19, n)
            km = ARB[1][:, 0:n]
            cl = ARB[0][:, 0:n]
            en = ARB[2][:, 0:n]
            gg = slot(6, 2 * P, 2)[:, 0:n]
            bon = slot(10, 2 * P, 2)[:, 0:n]
            pz = nbank()
            k.mm(pz[:, 0:n], T(W2A2t[0:64, hp * 128:(hp + 1) * 128], ['W2A2']), LW(0, lc, n)[0:64])
            k.act(sw, pz[:, 0:n], AF.Sigmoid, bias=par(l, 4, hp))
            yield
            pz = nbank()
            k.mm(pz[:, 0:n], T(W2A2t[64:128, hp * 128:(hp + 1) * 128], ['W2A2']), LW(0, lc, n)[64:128])
            k.act(asg, pz[:, 0:n], AF.Sigmoid, bias=par(l, 5, hp))
            yield
            pz = nbank()
            k.mm(pz[:, 0:n], T(G2t[:, hp * 128:(hp + 1) * 128], ['G2']), LW(1, lc, n))
            k.cp(gg, pz[:, 0:n], eng='act')
            yield
            k.ts(kk, kx, par(l, 6, hp), ALU.mult)
            k.act(t8, kk, AF.Square)
            yield
            pn = nbank()
            k.mm(pn[:, 0:n], BONES, t8)
            k.ts(t8, pn[:, 0:n], 1e-24, ALU.max)
            yield
            k.act(t8, t8, AF.Ln)
            k.act(t8, t8, AF.Exp, scale=-0.5)
            yield
            k.tt(kk, kk, t8, ALU.mult)
            yield
            k.ts(km, asg, par(l, 7, hp), ALU.mult, par(l, 27, hp), ALU.add)
            k.tt(km, km, kx, ALU.mult)
            yield
            k.stt(t8, r, par(l, 8, hp), km, ALU.mult, ALU.mult)
            pbon = nbank()
            k.mm(pbon[:, 0:n], BONES, t8)
            k.tt(bon, pbon[:, 0:n], vx, ALU.mult)
            yield
            ctx['gg'], ctx['bon'] = gg, bon
            if kind == 's':
                k.cp(T(SVt[:, 0, hp, :], ['SV']), r, eng='act')
                k.cp(T(SVt[:, 1, hp, :], ['SV']), km, eng='act')
                k.cp(T(SVt[:, 2, hp, :], ['SV']), vx, eng='act')
                k.act(T(SVt[:, 3, hp, :], ['SV']), sw, AF.Exp, scale=-C0)
                k.ts(T(SVt[:, 4, hp, :], ['SV']), kk, -1.0, ALU.mult)
                k.tt(T(SVt[:, 5, hp, :], ['SV']), kk, asg, ALU.mult)
                k.cp(T(SGt[:, 0, hp, :], ['SG']), gg, eng='act')
                k.cp(T(SGt[:, 1, hp, :], ['SG']), bon, eng='act')
                return
            BL = BLKP[P]
            k.scan(cl, RST, sw, 0.0)
            yield
            ep = t8
            k.act(ep, cl, AF.Exp, scale=-C0)
            k.act(en, cl, AF.Exp, scale=C0)
            yield
            k.tt(sw, cl, sw, ALU.subtract)
            k.act(sw, sw, AF.Exp, scale=-C0)
            yield
            k.tt(asg, kk, asg, ALU.mult)
            nch = n // CH
            pc = T(PCt[:, P, 0:nch], [('PC', P)])
            k.cp(pc, ep.v(lambda a_: a_[:, CH - 1:n:CH]), eng='act')
            yield
            k.tt(BL['R'], r, ep, ALU.mult)
            yield
            k.tt(BL['K'], km, en, ALU.mult)
            yield
            k.tt(BL['B'], asg, en, ALU.mult)
            yield
            k.stt(BL['A'], kk, -1.0, sw, ALU.mult, ALU.mult)
            k.cp(BL['V'], vx, eng='act')
            ctx['pc'] = pc
            ctx['BL'] = BL
            yield

        def rwkv_C(l, hp, tile, ctx, nxt):
            (t0, lc, n, kind) = tile
            nch = n // CH
            BL, pc = ctx['BL'], ctx['pc']

            def adv(cnt):
                if nxt is not None:
                    for _ in range(cnt):
                        next(nxt, None)
            Y = wp(16, n)
            Hs = T(HSTt[:, l, hp, 0:64], [('HST', l, hp)])
            HB = T(HBFt[:], ['HBF'])
            k.cp(HB, Hs, eng='act')
            blk = lambda nm, c: BL[nm][:, c * CH:(c + 1) * CH]
            nq = nch // 4
            assert nq == 2
            alive = [pre_quad_gen(q, blk, QS[q]) for q in range(nq)]
            while alive:
                for g_ in list(alive):
                    try:
                        next(g_)
                    except StopIteration:
                        alive.remove(g_)
                    adv(1)
            Ybs = [psbank(3), psbank(7)]
            pend = []
            for c in range(nch):
                seq_chunk(c, c % 4, blk, QS[c // 4], Hs, HB, pc[:, c:c + 1], Ybs, pend)
                adv(2)
            pend.pop()()
            if nxt is not None:
                for _ in nxt:
                    pass
            for par_ in range(2):
                k.cp(Y.v(lambda a_: a_.rearrange("p (c two s) -> p c two s", two=2, s=CH)[:, :, par_, :]),
                     Ybs[par_][:, 0:n].v(lambda a_: a_.rearrange("p (c two s) -> p c two s", two=2, s=CH)[:, :, par_, :]),
                     eng='act' if par_ else 'dve')
            return rwkv_post_gen(l, hp, tile, Y, ctx['gg'], ctx['bon'], wp(18))

        def mkqs(b0, b1, b2, b3, b4):
            return {'VT': slot(b0, 0), 'KT': slot(b0, 1), 'BT': slot(b0, 2), 'AT': slot(b0, 3), 'TRS': slot(b0, 0, 4),
                    'ATrk': slot(b1, 0), 'ATrb': slot(b1, 1), 'ATR': slot(b1, 0, 2), 'ATak': slot(b1, 2), 'TT': slot(b1, 3),
                    'X': [slot(b2, 0), slot(b2, 2)], 'XT': [slot(b2, 1), slot(b2, 3)], 'XX': [slot(b2, 0, 2), slot(b2, 2, 2)],
                    'TTt': [slot(b3, 0), slot(b3, 1)], 'ZV': slot(b3, 2), 'Wh': slot(b3, 3),
                    'Uv': slot(b4, 0), 'WhT': slot(b4, 1), 'UW': slot(b4, 0, 2), 'GT': slot(b4, 2), 'N': slot(b4, 3), 'GN': slot(b4, 2, 2)}
        QS = [mkqs(3, 4, 7, 8, 14), mkqs(9, 11, 12, 13, 15)]
        rr['qb'] = 0

        def nqbank():
            b = rr['qb']
            rr['qb'] = (b + 1) % 3
            return psbank(4 + b)

        def hmm(out_t, lt, rh, start=True, stop=True):
            for h_ in range(2):
                lo = h_ * 64
                k.mm(out_t[lo:lo + 64], lt[lo:lo + 64], rh[lo:lo + 64], start=start, stop=stop)

        def ch(t, j):
            return t[:, j * CH:(j + 1) * CH]

        def bcm(mi, nrep):
            return T(MSKt[:, mi, :], ['MSK']).v(lambda a_: a_.unsqueeze(1).to_broadcast([128, nrep, 64]))

        def r3(t, nrep):
            return t.v(lambda a_: a_.rearrange("p (c n) -> p c n", c=nrep))

        def pre_quad_gen(q, blk, qs):
            cs = [4 * q + j for j in range(4)]
            bk = nqbank()
            bkb = bk.v(lambda a_: a_.bitcast(BF16))
            for si, nm in enumerate(('V', 'K', 'B', 'A')):
                for j, c in enumerate(cs):
                    for h_ in range(2):
                        lo = h_ * 64
                        k.tr(bkb[lo:lo + 64, si * 256 + j * 64:si * 256 + (j + 1) * 64], blk(nm, c)[lo:lo + 64], IDB[lo:lo + 64, lo:lo + 64])
            k.cp(qs['TRS'], bkb[:, 0:1024], eng='act')
            yield
            X, XT = qs['X'][0], qs['XT'][0]
            for lt, rh, mi, dst in (('B', 'A', 1, XT), ('A', 'B', 2, X), ('K', 'A', 1, qs['ATak'])):
                bk = nqbank()
                for j, c in enumerate(cs):
                    hmm(ch(bk, j), blk(lt, c), blk(rh, c))
                k.tt(r3(dst, 4), r3(bk[:, 0:256], 4), bcm(mi, 4), ALU.mult)
                yield
            bk = nqbank()
            for si, lt in enumerate(('K', 'B')):
                for j, c in enumerate(cs):
                    hmm(bk[:, si * 256 + j * 64:si * 256 + (j + 1) * 64], blk(lt, c), blk('R', c))
            k.tt(r3(qs['ATR'], 8), r3(bk, 8), bcm(0, 8), ALU.mult)
            yield
            bk = nqbank()
            for j in range(4):
                hmm(ch(bk, j), ch(qs['ATak'], j), ch(qs['VT'], j))
            k.cp(qs['ZV'], bk[:, 0:256], eng='act')
            yield
            TT = qs['TTt'][0]
            ids4 = T(IDSt[:], ['IDS']).v(lambda a_: a_.unsqueeze(1).to_broadcast([128, 4, 64]))
            k.tt(r3(TT, 4), r3(XT, 4), ids4, ALU.add)
            tpar = 0

            def tt_update(TT, Xf, dst):
                bk3 = nqbank()
                for j in range(4):
                    hmm(ch(bk3, j), ch(Xf, j), ch(TT, j))
                k.tt(dst, bk3[:, 0:256], TT, ALU.add)
                return dst
            for L in range(1, 6):
                p_ = L % 2
                bk = nqbank()
                for j in range(4):
                    hmm(ch(bk, j), ch(XT, j), ch(X, j))
                if L < 5:
                    for j in range(4):
                        hmm(bk[:, 256 + j * 64:256 + (j + 1) * 64], ch(X, j), ch(XT, j))
                    k.cp(qs['XX'][p_], bk, eng='act')
                else:
                    k.cp(qs['X'][p_], bk[:, 0:256], eng='act')
                if L >= 2:
                    tpar = 1 - tpar
                    TT = tt_update(TT, X, qs['TTt'][tpar])
                X, XT = qs['X'][p_], qs['XT'][p_]
                yield
            tt_update(TT, X, qs['TT'])
            yield
            TTf = qs['TT']
            bk = nqbank()
            for j in range(4):
                hmm(ch(bk, j), ch(TTf, j), ch(qs['ZV'], j))
            for j in range(4):
                hmm(bk[:, 256 + j * 64:256 + (j + 1) * 64], ch(qs['AT'], j), ch(TTf, j))
            k.cp(qs['UW'], bk, eng='act')
            bk = nqbank()
            for j in range(4):
                hmm(ch(bk, j), ch(TTf, j), ch(qs['AT'], j))
            k.cp(qs['Wh'], bk[:, 0:256], eng='dve')
            yield
            bk = nqbank()
            for j in range(4):
                hmm(ch(bk, j), ch(qs['Wh'], j), ch(qs['BT'], j))
            for j in range(4):
                o_ = bk[:, 256 + j * 64:256 + (j + 1) * 64]
                for h_ in range(2):
                    lo = h_ * 64
                    k.mm(o_[lo:lo + 64], ch(qs['KT'], j)[lo:lo + 64], ch(qs['VT'], j)[lo:lo + 64], start=True, stop=False)
                    k.mm(o_[lo:lo + 64], ch(qs['BT'], j)[lo:lo + 64], ch(qs['Uv'], j)[lo:lo + 64], start=False, stop=True)
            k.cp(qs['GN'], bk, eng='act')
            yield

        def seq_chunk(c, j, blk, qs, Hs, HB, pcc, Ybs, pend):
            R = blk('R', c)
            g = lambda nm: ch(qs[nm], j)
            hsp = T(HSPt[:], ['HSP'])
            k.tt(hsp, Hs, g('N'), ALU.add)
            k.ts(hsp, hsp, pcc, ALU.mult)
            bku = nqbank()
            hmm(bku[:, 0:64], g('WhT'), HB)
            ys = Ybs[c % 2][:, c * CH:(c + 1) * CH]
            for h_ in range(2):
                lo = h_ * 64
                k.mm(ys[lo:lo + 64], HB[lo:lo + 64], R[lo:lo + 64], start=True, stop=False)
                k.mm(ys[lo:lo + 64], g('VT')[lo:lo + 64], g('ATrk')[lo:lo + 64], start=False, stop=False)
            bk = nqbank()
            hmm(bk[:, 0:64], g('GT'), HB)
            if pend:
                pend.pop()()
            k.stt(HB, bk[:, 0:64], pcc, hsp, ALU.mult, ALU.add)
            k.stt(Hs, bk[:, 0:64], pcc, hsp, ALU.mult, ALU.add)
            U = T(ZUt[:, c % 2, :], [('U', c % 2)])
            k.tt(U, bku[:, 0:64], g('Uv'), ALU.add)

            def fin(ys=ys, U=U, atrb=g('ATrb')):
                for h_ in range(2):
                    lo = h_ * 64
                    k.mm(ys[lo:lo + 64], U[lo:lo + 64], atrb[lo:lo + 64], start=False, stop=True)
            pend.append(fin)

        def evac_b(dst, src, eng):
            if eng == 'act':
                k.cp(dst, src, eng='act')
            else:
                k.cp(dst, src, eng='dve')

        def chunk_pre(c, blk):
            R, K_, B_, A_, V_ = blk('R', c), blk('K', c), blk('B', c), blk('A', c), blk('V', c)
            pers = SMB[(c % 2) * NPERS:(c % 2 + 1) * NPERS]
            out = {}
            for i, (nm, src) in enumerate((('VT', V_), ('KT', K_), ('BT', B_))):
                sl = nslot()
                slb = sl.v(lambda a: a.bitcast(BF16)[:, 0:128])
                k.tr(slb, src, IDB)
                d = pers[i]
                k.cp(d, slb, eng='act')
                out[nm] = d
            for i, (nm, lt, rh, mask) in enumerate((('ATrk', K_, R, M_LE), ('ATrb', B_, R, M_LE), ('ATak', K_, A_, M_LT),
                                                    ('XT', B_, A_, M_LT), ('X', A_, B_, M_GT))):
                sl = nslot()
                k.mm(sl, lt, rh)
                d = pers[3 + i] if i < 3 else nsmb()
                k.tt(d, sl, mask, ALU.mult)
                out[nm] = d
            X, XT = out['X'], out['XT']
            TT = nsmb()
            k.tt(TT, XT, IDB, ALU.add, eng='pool')

            def tt_update(TT, Xf, final):
                sl3 = nslot()
                k.mm(sl3, Xf, TT)
                TTn = pers[6] if final else nsmb()
                k.tt(TTn, sl3, TT, ALU.add)
                return TTn
            for L in range(1, 6):
                sl = nslot()
                k.mm(sl, XT, X)
                Xn = nsmb()
                k.cp(Xn, sl, eng='act')
                XTn = None
                if L < 5:
                    sl2 = nslot()
                    k.mm(sl2, X, XT)
                    XTn = nsmb()
                    k.cp(XTn, sl2, eng='act')
                if L >= 2:
                    TT = tt_update(TT, X, False)
                X, XT = Xn, XTn
            TT = tt_update(TT, X, True)
            out['TT'] = TT
            return out

        def chunk_seq(c, blk, pre, Hs, HB, pcc, Y):
            R, A_ = blk('R', c), blk('A', c)
            sl = nslot()
            k.mm(sl, A_, HB, start=True, stop=False)
            k.mm(sl, pre['ATak'], pre['VT'], start=False, stop=True)
            Z = nsmb()
            k.cp(Z, sl, eng='act')
            sl = nslot()
            k.mm(sl, pre['TT'], Z)
            U = nsmb()
            k.cp(U, sl, eng='dve')
            sl = nslot()
            k.mm(sl, HB, R, start=True, stop=False)
            k.mm(sl, pre['VT'], pre['ATrk'], start=False, stop=False)
            k.mm(sl, U, pre['ATrb'], start=False, stop=True)
            for hh in range(2):
                lo = hh * 64
                k.cp(Y[lo:lo + 64, c * CH:(c + 1) * CH], sl[lo:lo + 64, lo:lo + 64], eng='act' if hh else 'dve')
            sl = nslot()
            k.mm(sl, pre['KT'], pre['VT'], start=True, stop=False)
            k.mm(sl, pre['BT'], U, start=False, stop=True)
            k.tt(Hs, Hs, sl, ALU.add)
            k.ts(Hs, Hs, pcc, ALU.mult)
            k.cp(HB, Hs, eng='act')

        def rwkv_post_gen(l, hp, tile, Y, gg, bon, tmp, sample=False):
            (t0, lc, n, kind) = tile
            if sample:
                gg = T(SGt[:, 0, hp, :], ['SG'])
                bon = T(SGt[:, 1, hp, :], ['SG'])
            ysq = tmp[:, 0:n]
            k.act(ysq, Y, AF.Square)
            yield
            pm = nbank()
            pq = nbank()
            k.mm(pm[:, 0:n], BONES64, Y)
            k.mm(pq[:, 0:n], BONES64, ysq)
            k.act(ysq, pm[:, 0:n], AF.Square)
            k.tt(ysq, pq[:, 0:n], ysq, ALU.subtract)
            k.tt(Y, Y, pm[:, 0:n], ALU.subtract)
            yield
            k.act(ysq, ysq, AF.Ln, bias=T(SMALLt[:, 1:2], ['SMALL1']))
            yield
            k.act(ysq, ysq, AF.Exp, scale=-0.5)
            yield
            k.tt(Y, Y, ysq, ALU.mult)
            yield
            k.ts(Y, Y, par(l, 9, hp), ALU.mult, par(l, 10, hp), ALU.add)
            k.tt(Y, Y, bon, ALU.add)
            yield
            k.tt(YB(hp, lc, n), Y, gg, ALU.mult)
            yield

        def rwkv_post(l, hp, tile, Y, vx, gg, bon, sample=False):
            for _ in rwkv_post_gen(l, hp, tile, Y, gg, bon, wp(18), sample=sample):
                pass

        SGt = sb("SG", [128, 2, 8, NS], F32)

        def sample_rwkv_state(l):
            for j in range(6):
                tm = wpspan(0, 2).v(lambda a: a.rearrange("p a b -> p (a b)"))[0:NS, 0:D]
                for half in range(2):
                    pb = nbank()
                    for h4 in range(4):
                        hp = half * 4 + h4
                        k.tr(pb[0:NS, h4 * 128:(h4 + 1) * 128], T(SVt[:, j, hp, :], ['SV']), IDF)
                    k.cp(tm[:, half * 512:(half + 1) * 512], pb[0:NS, :], eng='act')
                k.dma(T(scr[j].rearrange("(b f) -> b f", b=NS), [('scr', j)]), tm)
                k.dma(T(SV2t[:, j, :], ['SV2']), T(scr[j].rearrange("(p f) -> p f", p=128), [('scr', j)]))
            sv = lambda j: T(SV2t[:, j, :], ['SV2'])
            ysm = T(WPt[:, 19, 0:128], [('wp', 19, s_) for s_ in range(4)])
            for hh in range(2):
                for vh in range(2):
                    ST = wpspan(0, 4).v(lambda a: a.rearrange("p a b -> p (a b)")[:, 0:2048].rearrange("p (v k) -> p v k", k=64))
                    TM = wpspan(4, 4).v(lambda a: a.rearrange("p a b -> p (a b)")[:, 0:2048].rearrange("p (v k) -> p v k", k=64))
                    off = hh * 4096 + vh * 2048
                    k.dma(ST, srw[l, :, off:off + 2048].rearrange("p (v k) -> p v k", k=64))
                    kb = lambda j: sv(j).v(lambda a: a[:, hh * 64:(hh + 1) * 64].unsqueeze(1).to_broadcast([128, 32, 64]))
                    vs = sv(2).v(lambda a: a[:, hh * 64 + vh * 32:hh * 64 + vh * 32 + 32])
                    k.tt(TM, ST, kb(4), ALU.mult)
                    sa = T(SMALLt[:, 24:56], ['SMALLsa'])
                    k.red(sa, TM, ALU.add)
                    k.tt(ST, ST, kb(3), ALU.mult)
                    k.tt(TM, sa.v(lambda a: a.unsqueeze(2).to_broadcast([128, 32, 64])), kb(5), ALU.mult)
                    k.tt(ST, ST, TM, ALU.add)
                    k.tt(TM, vs.v(lambda a: a.unsqueeze(2).to_broadcast([128, 32, 64])), kb(1), ALU.mult)
                    k.tt(ST, ST, TM, ALU.add)
                    k.dma(o_srw[l, :, off:off + 2048].rearrange("p (v k) -> p v k", k=64), ST)
                    k.tt(TM, ST, kb(0), ALU.mult)
                    k.red(ysm[:, hh * 64 + vh * 32:hh * 64 + vh * 32 + 32], TM, ALU.add)
            k.dma(T(scr[6].rearrange("(p f) -> p f", p=128), [('scr', 6)]), ysm)
            tm = wpspan(0, 2).v(lambda a: a.rearrange("p a b -> p (a b)"))[0:NS, 0:D]
            k.dma(tm, T(scr[6].rearrange("(b f) -> b f", b=NS), [('scr', 6)]))
            pb = nbank()
            for hp in range(KC):
                k.tr(pb[:, hp * NS:(hp + 1) * NS], tm[:, hp * 128:(hp + 1) * 128], IDF[0:NS, 0:NS])
            k.cp(T(YSt[:], ['YS']), pb[:, 0:KC * NS].v(lambda a: a.rearrange("p (c b) -> p c b", b=NS)), eng='act')

        def lru_gen(l, c, tile, Wu, Wy, last_p, base):
            (t0, lc, n, kind) = tile
            pu = proj(Wu, lc, n)
            ue = wp(base + 0)
            yv = wp(base + 8, n)
            cw = lambda j: par(l, 11 + j, c)
            xc = wp(base + 1, n)
            if kind == 'p':
                k.cp(ue[:, 3:3 + n], pu[:, 0:n], eng='act')
                cvc = T(CVCt[:, l, c, :], [('CVC', l, c)])
                k.cp(ue[:, 0:3], cvc, eng='pool')
                k.cp(cvc, ue[:, n:n + 3], eng='pool')
                if last_p:
                    k.cp(T(CVOt[:, c, :, 0], ['CVO']), ue[:, n:n + 3], eng='pool')
                k.ts(xc, ue[:, 0:n], cw(0), ALU.mult, par(l, 15, c), ALU.add)
                for j in (1, 2, 3):
                    k.stt(xc, ue[:, j:j + n], cw(j), xc, ALU.mult, ALU.add)
            else:
                k.cp(ue[:, 0:n], pu[:, 0:n], eng='act')
                cs = lambda j: T(CSt[:, c, j, :], ['CS'])
                k.ts(xc, cs(0), cw(0), ALU.mult, par(l, 15, c), ALU.add)
                k.stt(xc, cs(1), cw(1), xc, ALU.mult, ALU.add)
                k.stt(xc, cs(2), cw(2), xc, ALU.mult, ALU.add)
                k.stt(xc, ue[:, 0:n], cw(3), xc, ALU.mult, ALU.add)
                k.cp(T(CVOt[:, c, 0, 1:1 + NS], ['CVO']), cs(1), eng='pool')
                k.cp(T(CVOt[:, c, 1, 1:1 + NS], ['CVO']), cs(2), eng='pool')
                k.cp(T(CVOt[:, c, 2, 1:1 + NS], ['CVO']), ue[:, 0:n], eng='pool')
            yield
            py = proj(Wy, lc, n)
            k.cp(yv, py[:, 0:n], eng='act')
            yield
            xcb = wp(base + 2, n).v(lambda a: a.bitcast(BF16)[:, 0:n])
            k.cp(xcb, xc, eng='act')
            pa = nbank()
            k.mm(pa[:, 0:n], T(WABt[:, 0, c, :], ['WAB']), xcb)
            px = nbank()
            k.mm(px[:, 0:n], T(WABt[:, 1, c, :], ['WAB']), xcb)
            gr = wp(base + 3, n)
            gi = wp(base + 4, n)
            k.act(gr, pa[:, 0:n], AF.Sigmoid, bias=par(l, 16, c))
            k.act(gi, px[:, 0:n], AF.Sigmoid, bias=par(l, 17, c))
            yield
            av = wp(base + 5, n)
            e2 = wp(base + 6, n)
            k.act(av, gr, AF.Exp, scale=par(l, 28, c))
            k.act(e2, gr, AF.Exp, scale=par(l, 29, c))
            k.ts(e2, e2, -1.0, ALU.mult, 1.0, ALU.add)
            k.ts(e2, e2, 0.0, ALU.max)
            yield
            k.act(e2, e2, AF.Sqrt)
            k.tt(gi, gi, xc, ALU.mult)
            k.tt(e2, e2, gi, ALU.mult)
            yield
            hh_ = wp(base + 7, n)
            hc = T(CARt[:, l, 26 + c:27 + c], [('CAR', l, 26 + c)])
            if kind == 'p':
                k.scan(hh_, av, e2, hc)
                k.cp(hc, hh_[:, n - 1:n], eng='pool')
                if last_p:
                    k.cp(T(HOt[:, c, 0:1], ['HO']), hh_[:, n - 1:n], eng='pool')
            else:
                k.tt(hh_, av, T(HS0t[:, c, :], ['HS0']), ALU.mult)
                k.tt(hh_, hh_, e2, ALU.add)
                k.cp(T(HOt[:, c, 1:1 + NS], ['HO']), hh_, eng='pool')
            yield
            t9 = wp(base + 9, n)
            k.tt(t9, yv, yv, ALU.mult, eng='pool')
            k.ts(t9, t9, 0.044715, ALU.mult, 1.0, ALU.add)
            yield
            k.tt(t9, t9, yv, ALU.mult)
            k.act(t9, t9, AF.Sigmoid, scale=1.5957691216057308)
            k.tt(t9, t9, yv, ALU.mult, eng='pool')
            yield
            k.tt(YB(c, lc, n), hh_, t9, ALU.mult)
            yield

        rr['b8'] = 0

        def nbank8():
            b_ = rr['b8']
            rr['b8'] = (b_ + 1) % 8
            return psbank(b_)

        def attn_prompt(l, tile):
            (t0, lc, n, kind) = tile
            KT = lambda e: T(KTt[:, l, e, :], [('KT', l)])
            nsub = n // 128

            def bufs(P):
                ex = wpspan(4 * P, 2).v(lambda a: a.rearrange("p a b -> p (a b)")[:, 0:1024].rearrange("p (h m) -> p h m", h=4))
                pbf = wp(4 * P + 2).v(lambda a: a.bitcast(BF16)[:, 0:1024].rearrange("p (h m) -> p h m", h=4))
                PTs = wp(4 * P + 3).v(lambda a: a.bitcast(BF16)[:, 0:1024])
                mx = T(SMALLt[:, 2:6], ['SMALLmx']) if P == 0 else T(SMALLt[:, 56:60], ['SMALLmxb'])
                sm = T(SMALLt[:, 6:10], ['SMALLsm']) if P == 0 else T(SMALLt[:, 60:64], ['SMALLsmb'])
                return ex, pbf, PTs, mx, sm

            def stageA(sub):
                P = sub % 2
                c0_ = lc + sub * 128
                ex, pbf, PTs, mx, sm = bufs(P)
                pbs = []
                for hpair in range(2):
                    pb = nbank8()
                    pbs.append(pb)
                    for h2 in range(2):
                        h = hpair * 2 + h2
                        for dc in range(2):
                            q = T(YBt[:, 2 * h + dc, c0_:c0_ + 128], [('YB', 2 * h + dc, lc)])
                            k.mm(pb[:, h2 * 256:(h2 + 1) * 256], q, KT(2 * h + dc), start=(dc == 0), stop=(dc == 1))
                for hpair in range(2):
                    k.red(mx[:, hpair * 2:hpair * 2 + 2], pbs[hpair].v(lambda a: a.rearrange("p (h m) -> p h m", h=2)), ALU.max)
                k.ts(mx, mx, -1.0, ALU.mult)
                for h in range(4):
                    k.act(ex[:, h, :], pbs[h // 2][:, (h % 2) * 256:(h % 2 + 1) * 256], AF.Exp, bias=mx[:, h:h + 1], accum=sm[:, h:h + 1])

            def stageB(sub):
                P = sub % 2
                c0_ = lc + sub * 128
                ex, pbf, PTs, mx, sm = bufs(P)
                k.recip(sm, sm)
                for h in range(4):
                    k.ts(pbf[:, h, :], ex[:, h, :], sm[:, h:h + 1], ALU.mult)
                ptb = nbank8()
                ptv = ptb.v(lambda a: a.bitcast(BF16))
                for h in range(4):
                    for mc in range(2):
                        j = h * 2 + mc
                        k.tr(ptv[:, j * 128:(j + 1) * 128], pbf[:, h, mc * 128:(mc + 1) * 128], IDB)
                k.cp(PTs, ptv[:, 0:1024], eng='act')
                for half in range(2):
                    po = nbank8()
                    for j in range(4):
                        ec = half * 4 + j
                        h = ec // 2
                        for mc in range(2):
                            k.mm(po[:, j * 128:(j + 1) * 128], T(VNt[:, l, mc, ec * 128:(ec + 1) * 128], [('VN', l)]),
                                 PTs[:, (h * 2 + mc) * 128:(h * 2 + mc + 1) * 128], start=(mc == 0), stop=(mc == 1))
                    dst = T(EBv[:, half * 4:half * 4 + 4, c0_:c0_ + 128], [('EB', c, lc) for c in range(half * 4, half * 4 + 4)])
                    k.cp(dst, po.v(lambda a: a.rearrange("p (j t) -> p j t", j=4)), eng='act' if half else 'dve')
            stageA(0)
            for sub in range(nsub):
                if sub + 1 < nsub:
                    stageA(sub + 1)
                stageB(sub)

        def attn_sample(l, tile):
            (t0, lc, n, kind) = tile
            pb = nbank()
            pb2 = nbank()
            for c in range(KC):
                tgt = pb if c < 4 else pb2
                k.tr(tgt[0:NS, (c % 4) * 128:(c % 4 + 1) * 128], T(YSt[:, c, :], ['YS']), IDF)
            tm = wpspan(0, 2).v(lambda a: a.rearrange("p a b -> p (a b)"))[0:NS, 0:D]
            k.cp(tm[:, 0:512], pb[0:NS, :], eng='act')
            k.cp(tm[:, 512:1024], pb2[0:NS, :], eng='act')
            k.dma(T(scr[7].rearrange("(b f) -> b f", b=NS), [('scr', 7)]), tm)
            SCT = wp(19, 128).v(lambda a: a.rearrange("p (m b h) -> p m b h", m=2, b=NS))
            for b in range(NS):
                qb = wpspan(2 + 2 * (b % 2), 2).v(lambda a: a.rearrange("p a b -> p (a b)")[:, 0:D])
                k.dma(qb, T(scr[7][b * D:(b + 1) * D].partition_broadcast(128), [('scr', 7)]))
                kb = wpspan(6 + 4 * (b % 2), 4).v(lambda a: a.rearrange("p a b -> p (a b)")[:, 0:2048].rearrange("p (m f) -> p m f", m=2))
                k.dma(kb, ck[l, b].rearrange("(m p) f -> p m f", p=128))
                for mc in range(2):
                    pr = wpspan(14, 2).v(lambda a: a.rearrange("p a b -> p (a b)")[:, 0:D])
                    k.tt(pr, kb[:, mc, :], qb, ALU.mult)
                    k.red(SCT[:, mc, b, :], pr.v(lambda a: a.rearrange("p (h d) -> p h d", h=4)), ALU.add)
            pbT = nbank()
            for mc in range(2):
                k.tr(pbT[0:64, mc * 128:(mc + 1) * 128], SCT[:, mc].v(lambda a: a.rearrange("p b h -> p (b h)")), IDF)
            sc = wp(16, 256)[0:64]
            k.cp(sc, pbT[0:64, 0:256], eng='act')
            mx = T(SMALLt[0:64, 10:11], ['SMALLmx2'])
            sm = T(SMALLt[0:64, 11:12], ['SMALLsm2'])
            k.red(mx, sc, ALU.max)
            k.ts(mx, mx, -1.0, ALU.mult)
            k.act(sc, sc, AF.Exp, bias=mx, accum=sm)
            k.recip(sm, sm)
            k.ts(sc, sc, sm, ALU.mult)
            pbP = nbank()
            for mc in range(2):
                k.tr(pbP[:, mc * 64:(mc + 1) * 64], sc[:, mc * 128:(mc + 1) * 128], IDF[0:64, 0:64])
            PTS = wp(17, 128).v(lambda a: a.rearrange("p (m j) -> p m j", m=2))
            k.cp(PTS, pbP[:, 0:128].v(lambda a: a.rearrange("p (m j) -> p m j", m=2)), eng='act')
            po = nbank()
            for b in range(NS):
                vb = wpspan(6 + 4 * (b % 2), 4).v(lambda a: a.rearrange("p a b -> p (a b)")[:, 0:2048].rearrange("p (m f) -> p m f", m=2))
                k.dma(vb, cv[l, b].rearrange("(m p) f -> p m f", p=128))
                for ec in range(KC):
                    h = ec // 2
                    for mc in range(2):
                        k.mm(po[:, ec * NS + b:ec * NS + b + 1], vb[:, mc, ec * 128:(ec + 1) * 128],
                             PTS[:, mc, b * 4 + h:b * 4 + h + 1], start=(mc == 0), stop=(mc == 1))
            dst = T(EBv[:, :, lc:lc + NS], [('EB', c, lc) for c in range(KC)])
            k.cp(dst, po[:, 0:KC * NS].v(lambda a: a.rearrange("p (c b) -> p c b", b=NS)), eng='act')

        S.dry = True
        try:
            program()
        except Stop:
            pass
        S.reset()
        rr['bank'] = 0
        rr['slot'] = 0
        rr['smb'] = 0
        rr['qb'] = 0
        rr['b8'] = 0
        rr['nbmod'] = 8
        wq.start_emit()
        try:
            program()
        except Stop:
            pass
        S.finish('sp')
        S.emit_all()
        nc._sched_stats = {q: len(v) for q, v in S.streams.items()}
        nc._sched_stats['waits'] = S.n_wait
        nc._marks = getattr(S, 'marks', [])
    return nc


_NC_CACHE = {}


def _get_nc(debug=False):
    if debug not in _NC_CACHE:
        _NC_CACHE[debug] = build_nc(debug)
    return _NC_CACHE[debug]


def make_in_maps(inp):
    f = lambda a: np.ascontiguousarray(np.asarray(a, dtype=np.float32))
    prow = np.zeros((2 * NRL, D), np.float32)
    for l in range(2):
        rows = []
        mu = f(inp['mu_shift'])[l]
        rows += [mu[0:1024], mu[1024:2048], mu[2048:3072], np.concatenate([mu[3072:3328], np.zeros(768, np.float32)])]
        rows += [f(inp['rw_w0'])[l], f(inp['rw_a0'])[l], f(inp['rw_k_k'])[l].reshape(-1), f(inp['rw_k_a'])[l].reshape(-1),
                 f(inp['rw_r_k'])[l].reshape(-1), f(inp['rw_lnx_g'])[l].reshape(-1), f(inp['rw_lnx_b'])[l].reshape(-1)]
        cw = f(inp['lru_conv_w'])[l]
        rows += [cw[0], cw[1], cw[2], cw[3], f(inp['lru_conv_b'])[l], f(inp['lru_ba'])[l], f(inp['lru_bx'])[l], f(inp['lru_lambda'])[l]]
        gb = f(inp['mix_gate_b'])[l]
        rows += [gb[0], gb[1]]
        rows += [f(inp[n])[l] for n in ('ln1_g', 'ln1_b', 'ln2_g', 'ln2_b', 'ln3_g', 'ln3_b')]
        assert len(rows) == NRL
        for j, r in enumerate(rows):
            prow[l * NRL + j] = r
    shared = {n: f(inp[n]) for n in ('w_in', 'rw_w2', 'rw_a2', 'rw_g2', 'rw_proj', 'lru_wa', 'lru_wx', 'lru_proj', 'w_out_mix',
                                     'xa_wq', 'xa_wk', 'xa_wv', 'xa_wo', 'mlp_up', 'mlp_down')}
    shared['prow'] = prow
    xp = f(inp['x_prompt'])
    xs = f(inp['x_sample']).reshape(128, D)
    mem = f(inp['mem_prompt'])
    ck = f(inp['cache_mem_k']).reshape(2, 128, 256, D)
    cv = f(inp['cache_mem_v']).reshape(2, 128, 256, D)
    srw = f(inp['state_rwkv']).reshape(2, 128, 16 * 64 * 64)
    ssh = f(inp['state_rwkv_shift'])
    shh = f(inp['state_lru_h'])
    scv = f(inp['state_lru_conv'])
    maps = []
    for b in range(8):
        sl = slice(b * NS, (b + 1) * NS)
        m = dict(shared)
        m['xp'] = xp[b]
        m['xs'] = np.ascontiguousarray(xs[sl])
        m['mem'] = mem[b]
        m['ck'] = np.ascontiguousarray(ck[:, sl])
        m['cv'] = np.ascontiguousarray(cv[:, sl])
        m['srw'] = np.ascontiguousarray(srw[:, sl]).reshape(2, 128, 8192)
        m['ssh'] = np.ascontiguousarray(ssh[:, sl])
        m['shh'] = np.ascontiguousarray(shh[:, sl])
        m['scv'] = np.ascontiguousarray(scv[:, sl])
        maps.append(m)
    return maps


def gather(results):
    R = results
    cat = lambda name, ax: np.concatenate([r[name] for r in R], axis=ax)
    y_p = np.stack([r['o_yp'] for r in R], 0)
    y_s = cat('o_ys', 0).reshape(128, 1, D)
    p_rw = np.stack([r['o_prw'] for r in R], 1)
    p_sh = np.stack([r['o_psh'] for r in R], 1)
    p_h = np.stack([r['o_ph'] for r in R], 1)
    p_cv = np.stack([r['o_pcv'] for r in R], 1)
    mk = np.stack([r['o_mk'] for r in R], 1).reshape(2, 8, 256, 4, 256)
    mv = np.stack([r['o_mv'] for r in R], 1).reshape(2, 8, 256, 4, 256)
    s_rw = np.concatenate([r['o_srw'].reshape(2, NS, 16, 64, 64) for r in R], axis=1)
    s_sh = cat('o_ssh', 1)
    s_h = cat('o_sh', 1)
    s_cv = cat('o_scv', 1)
    outs = (y_p, y_s, p_rw, p_sh, p_h, p_cv, mk, mv, s_rw, s_sh, s_h, s_cv)
    return tuple(np.ascontiguousarray(o, dtype=np.float32) for o in outs)


def kernel(**inputs):
    nc = _get_nc(False)
    maps = make_in_maps(inputs)
    res = run_bass_kernel_spmd(nc, maps, core_ids=list(range(8)))
    return gather(res.results)
```

```python
import math
import itertools
import numpy as np
from contextlib import ExitStack
import concourse.bass as bass
import concourse.mybir as mybir
from concourse.bass_utils import run_bass_kernel_spmd

F32 = mybir.dt.float32
BF16 = mybir.dt.bfloat16
AF = mybir.ActivationFunctionType
ALU = mybir.AluOpType
AX = mybir.AxisListType

COMPUTE = ('pe', 'act', 'dve', 'pool')
ALLQ = ('pe', 'act', 'dve', 'pool', 'sp')

D = 1024
KC = 8
SEQ = 2048
NS = 16
NT = 512
NIN = 7424
C_R, C_K, C_V, C_W, C_A, C_G = 0, 1024, 2048, 3072, 3136, 3200
NSH = 3328
C_U, C_Y, C_GATE = 3328, 4352, 5376
DEPTH = 2
ALPHA = float((2 * DEPTH) ** 0.25)
LN_EPS = 1e-5
GN_EPS = 64e-5
C0 = float(math.exp(-0.5))
NRL = 27
NPC = 30
CH = 64


class Sched:
    def __init__(self, nc, stack, n_dma_sems=40):
        self.nc = nc
        self.streams = {e: [] for e in ALLQ}
        self.sem = {e: stack.enter_context(nc.semaphore('s_' + e)) for e in COMPUTE}
        self.dsem = [stack.enter_context(nc.semaphore('d%d' % i)) for i in range(n_dma_sems)]
        self.reset()

    def reset(self):
        self.streams = {e: [] for e in ALLQ}
        self.cnt = {e: 0 for e in COMPUTE}
        self.dcnt = [0] * len(self.dsem)
        self.drr = 0
        self.seen = {q: {} for q in ALLQ}
        self.snap = {}
        self.lastw = {}
        self.readers = {}
        self.lastacc = {}
        self.n_wait = 0
        self.dry = False

    def _semh(self, key):
        return self.sem[key] if isinstance(key, str) else self.dsem[key]

    def _need(self, q, ev, waits):
        key, val = ev
        if self.seen[q].get(key, 0) >= val:
            return
        if key == q and q == 'pe':
            return
        if waits.get(key, 0) < val:
            waits[key] = val

    def _deps(self, q, reads, writes):
        waits = {}
        for k in reads:
            ev = self.lastw.get(k)
            if ev is not None:
                self._need(q, ev, waits)
        for k in writes:
            ev = self.lastw.get(k)
            if ev is not None:
                self._need(q, ev, waits)
            for ev in self.readers.get(k, ()):
                self._need(q, ev, waits)
        return waits

    def _apply_waits(self, q, waits):
        out = []
        for key, val in waits.items():
            if self.seen[q].get(key, 0) >= val:
                continue
            out.append((key, val))
            self.seen[q][key] = val
            sn = self.snap.get((key, val))
            if sn is not None:
                for e, v in zip(COMPUTE, sn):
                    if e == q:
                        continue
                    if self.seen[q].get(e, 0) < v:
                        self.seen[q][e] = v
        self.n_wait += len(out)
        return [(self._semh(k), v) for k, v in out]

    def _record(self, ev, reads, writes):
        for k in writes:
            self.lastw[k] = ev
            self.readers[k] = []
        for k in reads:
            self.readers.setdefault(k, []).append(ev)

    def op(self, q, fn, reads=(), writes=()):
        if self.dry:
            return None
        waits = self._deps(q, reads, writes)
        banks = set()
        for kk_ in reads:
            if isinstance(kk_, tuple) and kk_[0] == 'ps':
                banks.add(kk_[1])
        for kk_ in writes:
            if isinstance(kk_, tuple) and kk_[0] == 'ps':
                banks.add(kk_[1])
        for b_ in banks:
            ev0 = self.lastacc.get(b_)
            if ev0 is not None and ev0[0] != q:
                self._need(q, ev0, waits)
        wl = self._apply_waits(q, waits)
        self.cnt[q] += 1
        ev = (q, self.cnt[q])
        for b_ in banks:
            self.lastacc[b_] = ev
        self.snap[ev] = tuple(self.seen[q].get(e, 0) for e in COMPUTE)
        sem = self.sem[q]

        def emit(e, fn=fn, wl=wl, sem=sem):
            for s, v in wl:
                e.wait_ge(s, v)
            fn(e).then_inc(sem, 1)
        self.streams[q].append(emit)
        self._record(ev, reads, writes)
        return ev

    def dma(self, q, out, in_, reads=(), writes=(), **kw):
        if self.dry:
            return None
        waits = self._deps(q, reads, writes)
        i = self.drr
        self.drr = (self.drr + 1) % len(self.dsem)
        if self.dcnt[i] > 0:
            self._need(q, (i, 16 * self.dcnt[i]), waits)
        wl = self._apply_waits(q, waits)
        self.dcnt[i] += 1
        ev = (i, 16 * self.dcnt[i])
        self.snap[ev] = tuple(self.seen[q].get(e, 0) for e in COMPUTE)
        sem = self.dsem[i]

        def emit(e, out=out, in_=in_, wl=wl, sem=sem, kw=kw):
            for s, v in wl:
                e.wait_ge(s, v)
            e.dma_start(out=out, in_=in_, **kw).then_inc(sem, 16)
        self.streams[q].append(emit)
        self._record(ev, reads, writes)
        return ev

    def mark(self, name):
        if self.dry:
            return
        if not hasattr(self, 'marks'):
            self.marks = []
        self.marks.append((name, dict(self.cnt)))

    def barrier(self):
        if self.dry:
            return
        for q in ALLQ:
            waits = {}
            for e in COMPUTE:
                if self.cnt[e] and e != q:
                    self._need(q, (e, self.cnt[e]), waits)
            if q == 'sp':
                for i, c in enumerate(self.dcnt):
                    if c:
                        self._need(q, (i, 16 * c), waits)
            wl = self._apply_waits(q, waits)

            def emit(e, wl=wl):
                for s, v in wl:
                    e.wait_ge(s, v)
            self.streams[q].append(emit)

    def finish(self, q='sp'):
        waits = {}
        for i, c in enumerate(self.dcnt):
            if c:
                self._need(q, (i, 16 * c), waits)
        for e in COMPUTE:
            if self.cnt[e] and e != q:
                self._need(q, (e, self.cnt[e]), waits)
        wl = self._apply_waits(q, waits)

        def emit(e, wl=wl):
            for s, v in wl:
                e.wait_ge(s, v)
        self.streams[q].append(emit)

    def emit_all(self):
        nc = self.nc
        with nc.Block() as block:
            @block.tensor
            def _(e):
                for f in self.streams['pe']:
                    f(e)

            @block.scalar
            def _(e):
                for f in self.streams['act']:
                    f(e)

            @block.vector
            def _(e):
                for f in self.streams['dve']:
                    f(e)

            @block.gpsimd
            def _(e):
                for f in self.streams['pool']:
                    f(e)

            @block.sync
            def _(e):
                for f in self.streams['sp']:
                    f(e)


class T:
    __slots__ = ('ap', 'keys')

    def __init__(self, ap, keys):
        self.ap = ap
        self.keys = tuple(keys)

    def __getitem__(self, idx):
        return T(self.ap[idx], self.keys)

    def v(self, fn):
        return T(fn(self.ap), self.keys)

    def re(self, s, **kw):
        return T(self.ap.rearrange(s, **kw), self.keys)


def _ap(x):
    return x.ap if isinstance(x, T) else x


def _keys(*xs):
    out = []
    for x in xs:
        if isinstance(x, T):
            out.extend(x.keys)
    return out


class KB:
    def __init__(self, S):
        self.S = S

    def mm(self, out, lhsT, rhs, start=True, stop=True):
        o, l, r = _ap(out), _ap(lhsT), _ap(rhs)
        self.S.op('pe', lambda e: e.matmul(o, lhsT=l, rhs=r, start=start, stop=stop),
                  reads=_keys(lhsT, rhs), writes=_keys(out))

    def tr(self, out, in_, ident):
        o, i, d = _ap(out), _ap(in_), _ap(ident)
        self.S.op('pe', lambda e: e.transpose(out=o, in_=i, identity=d), reads=_keys(in_, ident), writes=_keys(out))

    def act(self, out, in_, func, bias=None, scale=None, accum=None):
        o, i = _ap(out), _ap(in_)
        kw = {}
        if bias is not None:
            kw['bias'] = _ap(bias)
        if scale is not None:
            kw['scale'] = _ap(scale)
        if accum is not None:
            kw['accum_out'] = _ap(accum)
        self.S.op('act', lambda e: e.activation(out=o, in_=i, func=func, **kw),
                  reads=_keys(in_, bias, scale), writes=_keys(out, accum))

    def tt(self, out, a, b, op, eng='dve'):
        if eng == 'pool':
            eng = 'dve'
        if eng == 'gp':
            eng = 'pool'
        o, x, y = _ap(out), _ap(a), _ap(b)
        self.S.op(eng, lambda e: e.tensor_tensor(out=o, in0=x, in1=y, op=op), reads=_keys(a, b), writes=_keys(out))

    def ts(self, out, a, s1, op0, s2=None, op1=None, eng='dve'):
        if eng == 'pool':
            eng = 'dve'
        if eng == 'gp':
            eng = 'pool'
        o, x, p, q = _ap(out), _ap(a), _ap(s1), _ap(s2)
        if op1 is None:
            self.S.op(eng, lambda e: e.tensor_scalar(out=o, in0=x, scalar1=p, scalar2=None, op0=op0),
                      reads=_keys(a, s1), writes=_keys(out))
        else:
            self.S.op(eng, lambda e: e.tensor_scalar(out=o, in0=x, scalar1=p, scalar2=q, op0=op0, op1=op1),
                      reads=_keys(a, s1, s2), writes=_keys(out))

    def stt(self, out, a, scalar, b, op0, op1):
        o, x, s, y = _ap(out), _ap(a), _ap(scalar), _ap(b)
        self.S.op('dve', lambda e: e.scalar_tensor_tensor(out=o, in0=x, scalar=s, in1=y, op0=op0, op1=op1),
                  reads=_keys(a, scalar, b), writes=_keys(out))

    def cp(self, out, in_, eng='dve'):
        o, i = _ap(out), _ap(in_)
        if eng == 'pool':
            eng = 'act'
        if eng == 'poolcast':
            eng = 'pool'
        if eng == 'act':
            self.S.op('act', lambda e: e.activation(out=o, in_=i, func=AF.Copy), reads=_keys(in_), writes=_keys(out))
        else:
            self.S.op(eng, lambda e: e.tensor_copy(out=o, in_=i), reads=_keys(in_), writes=_keys(out))

    def scan(self, out, d0, d1, init):
        o, x, y, z = _ap(out), _ap(d0), _ap(d1), _ap(init)
        self.S.op('dve', lambda e: e.tensor_tensor_scan(out=o, data0=x, data1=y, initial=z, op0=ALU.mult, op1=ALU.add),
                  reads=_keys(d0, d1, init), writes=_keys(out))

    def red(self, out, in_, op, axis=AX.X):
        o, i = _ap(out), _ap(in_)
        self.S.op('dve', lambda e: e.tensor_reduce(out=o, in_=i, axis=axis, op=op), reads=_keys(in_), writes=_keys(out))

    def recip(self, out, in_):
        o, i = _ap(out), _ap(in_)
        self.S.op('dve', lambda e: e.reciprocal(out=o, in_=i), reads=_keys(in_), writes=_keys(out))

    def memset(self, t, val, eng='pool'):
        o = _ap(t)
        self.S.op(eng, lambda e: e.memset(o, val), writes=_keys(t))

    def dma(self, out, in_, q='sp', **kw):
        self.S.dma(q, _ap(out), _ap(in_), reads=_keys(in_), writes=_keys(out), **kw)


class WQ:
    def __init__(self, k, stg, ring, la=3):
        self.k = k
        self.stg = stg
        self.ring = ring
        self.la = la
        self.specs = []
        self.collect = True
        self.i = 0
        self.issued = 0

    def start_emit(self):
        self.collect = False
        self.i = 0
        self.issued = 0

    def _issue(self, j):
        src = self.specs[j]
        dst = self.ring[j % len(self.ring)]
        self.k.dma(dst, src, q='pool')

    def next(self, src):
        if self.collect:
            self.specs.append(src)
            return self.ring[0]
        i = self.i
        self.i += 1
        lim = min(i + self.la, len(self.specs) - 1)
        while self.issued <= lim:
            self._issue(self.issued)
            self.issued += 1
        return self.ring[i % len(self.ring)]


class Stop(Exception):
    pass


def build_nc(debug=False, stop_at=None, short=False):
    nc = bass.Bass("TRN2", target_bir_lowering=False)

    def din(name, shape):
        return nc.dram_tensor(name, list(shape), F32, kind="ExternalInput").ap()

    def dout(name, shape):
        return nc.dram_tensor(name, list(shape), F32, kind="ExternalOutput").ap()

    xp = din("xp", [SEQ, D])
    xs = din("xs", [NS, D])
    mem = din("mem", [256, D])
    ck = din("ck", [2, NS, 256, D])
    cv = din("cv", [2, NS, 256, D])
    srw = din("srw", [2, 128, 8192])
    ssh = din("ssh", [2, NS, NSH])
    shh = din("shh", [2, NS, D])
    scv = din("scv", [2, NS, 3, D])
    prow = din("prow", [2 * NRL, D])
    w_in = din("w_in", [2, D, NIN])
    rw_w2 = din("rw_w2", [2, 64, D])
    rw_a2 = din("rw_a2", [2, 64, D])
    rw_g2 = din("rw_g2", [2, 128, D])
    rw_proj = din("rw_proj", [2, D, D])
    lru_wa = din("lru_wa", [2, 16, 64, 64])
    lru_wx = din("lru_wx", [2, 16, 64, 64])
    lru_proj = din("lru_proj", [2, D, D])
    w_out_mix = din("w_out_mix", [2, D, D])
    xa_wq = din("xa_wq", [2, D, D])
    xa_wk = din("xa_wk", [2, D, D])
    xa_wv = din("xa_wv", [2, D, D])
    xa_wo = din("xa_wo", [2, D, D])
    mlp_up = din("mlp_up", [2, D, 4 * D])
    mlp_down = din("mlp_down", [2, 4 * D, D])

    o_yp = dout("o_yp", [SEQ, D])
    o_ys = dout("o_ys", [NS, D])
    o_prw = dout("o_prw", [2, 16, 64, 64])
    o_psh = dout("o_psh", [2, NSH])
    o_ph = dout("o_ph", [2, D])
    o_pcv = dout("o_pcv", [2, 3, D])
    o_mk = dout("o_mk", [2, 256, D])
    o_mv = dout("o_mv", [2, 256, D])
    o_srw = dout("o_srw", [2, 128, 8192])
    o_ssh = dout("o_ssh", [2, NS, NSH])
    o_sh = dout("o_sh", [2, NS, D])
    o_scv = dout("o_scv", [2, NS, 3, D])
    scr = nc.dram_tensor("scr", [8, NS * D], F32, kind="Internal").ap()
    dbg = {}
    if debug:
        dbg['x1'] = dout("dbg_x1", [128, 8, 1040])
        dbg['yb'] = dout("dbg_yb", [128, 8, 1040])

    with ExitStack() as st:
        S = Sched(nc, st)
        k = KB(S)

        def sb(name, shape, dt):
            return st.enter_context(nc.sbuf_tensor(name, list(shape), dt))

        PP = 1040
        X32t = sb("X32", [128, KC, PP], F32)
        XBt = sb("XB", [128, KC, PP], BF16)
        YBt = sb("YB", [128, KC, PP], BF16)
        EBt = sb("EB", [128, KC * PP], BF16)
        LWt = sb("LW", [128, 2, PP], BF16)
        KTt = sb("KT", [128, 2, KC, 256], BF16)
        VNt = sb("VN", [128, 2, 2, D], BF16)
        NB = 20
        WPt = sb("WP", [128, NB, 516], F32)
        RINGt = sb("RING", [128, 10, KC, 128], BF16)
        PARt = sb("PAR", [128, KC, 2, NPC], F32)
        CONt = sb("CON", [128, 8, 128], F32)
        CONBt = sb("CONB", [128, 128], BF16)
        RSTt = sb("RST", [128, NT], F32)
        W2A2t = sb("W2A2", [128, D], BF16)
        G2t = sb("G2", [128, D], BF16)
        WABt = sb("WAB", [128, 2, KC, 128], BF16)
        HSTt = sb("HST", [128, 2, 8, 128], F32)
        HBFt = sb("HBF", [128, 64], BF16)
        HSPt = sb("HSP", [128, 64], F32)
        ZUt = sb("ZU", [128, 2, 64], BF16)
        PCt = sb("PCT", [128, 2, 8], F32)
        HOUTt = sb("HOUT", [64, 128], F32)
        MSKt = sb("MSK", [128, 3, 64], F32)
        IDSt = sb("IDS", [128, 64], BF16)
        CARt = sb("CAR", [128, 2, 40], F32)
        CVCt = sb("CVC", [128, 2, 8, 3], F32)
        SHOt = sb("SHO", [128, 26, 1 + NS], F32)
        HOt = sb("HO", [128, 8, 1 + NS], F32)
        CVOt = sb("CVO", [128, 8, 3, 1 + NS], F32)
        SHSt = sb("SHS", [128, 26, NS], F32)
        HS0t = sb("HS0", [128, 8, NS], F32)
        CSt = sb("CS", [128, 8, 3, NS], F32)
        SVt = sb("SV", [128, 6, 8, NS], F32)
        SV2t = sb("SV2", [128, 6, 128], F32)
        YSt = sb("YS", [128, 8, NS], F32)
        SMALLt = sb("SMALL", [128, 64], F32)

        ps = [st.enter_context(nc.psum_tensor("ps%d" % i, [128, 512], F32)) for i in range(8)]

        def psbank(b):
            return T(ps[b][:], [('ps', b, s) for s in range(4)])

        def psslot(b, s):
            return T(ps[b][:, s * 128:(s + 1) * 128], [('ps', b, s)])

        rr = {'bank': 0, 'slot': 0}

        def nbank():
            m_ = rr.get('nbmod', 8)
            b = rr['bank'] % m_
            rr['bank'] = (b + 1) % m_
            return psbank(b)

        def nslot():
            s = rr['slot']
            rr['slot'] = (s + 1) % 16
            return psslot(4 + s % 4, s // 4)

        def wp(i, n=None):
            t = T(WPt[:, i, :], [('wp', i, s_) for s_ in range(4)])
            return t if n is None else t[:, :n]

        def wpspan(i, cnt):
            return T(WPt[:, i:i + cnt, :], [('wp', j, s_) for j in range(i, i + cnt) for s_ in range(4)])

        def slot(i, s0, ns=1):
            return T(WPt[:, i, :].bitcast(BF16)[:, s0 * 256:(s0 + ns) * 256], [('wp', i, s_) for s_ in range(s0, s0 + ns)])

        X32 = lambda c, lc, n: T(X32t[:, c, lc:lc + n], [('X32', c, lc)])
        XB = lambda c, lc, n: T(XBt[:, c, lc:lc + n], [('XB', c, lc)])
        YB = lambda c, lc, n: T(YBt[:, c, lc:lc + n], [('YB', c, lc)])
        EBv = EBt[:].rearrange("p (c n) -> p c n", c=KC)
        EB = lambda c, lc, n: T(EBv[:, c, lc:lc + n], [('EB', c, lc)])
        LW = lambda i, lc, n: T(LWt[:, i, lc:lc + n], [('LW', i, lc)])
        CON = lambda i: T(CONt[:, i, :], [('CON', i)])
        IDF, BONES, BONES64, ONESD, M_LE, M_LT, M_GT = [CON(i) for i in range(7)]
        IDB = T(CONBt[:], ['IDB'])
        RST = T(RSTt[:], ['RST'])
        PAR = T(PARt[:], ['PAR'])

        def par(l, j, c, lo=0, hi=128):
            return T(PARt[lo:hi, c, l, j:j + 1], ['PAR'])

        RING = [T(RINGt[:, i], [('ring', i)]) for i in range(10)]
        wq = WQ(k, None, RING, la=6)

        def wsrc(w, l, c0, r0=0):
            return w[l, r0:r0 + D, c0:c0 + 128].rearrange("(kc p) n -> p kc n", p=128)

        def arena(off, n):
            return EBt[:, off:off + n]
        BLK = {}
        for i, nm in enumerate(['R', 'K', 'B', 'A', 'V']):
            BLK[nm] = T(arena(i * 512, 512), [('blk', nm)])

        npass = [2]
        dbg_pi = 0 if short else 1

        def program():
            k.memset(T(CONt[:], [('CON', i) for i in range(8)]), 0.0)
            k.memset(IDB, 0.0)
            S.op('pool', lambda e: e.affine_select(out=CONt[:, 0, :], in_=CONt[:, 0, :], compare_op=ALU.not_equal, fill=1.0,
                                                   base=0, pattern=[[-1, 128]], channel_multiplier=1),
                 reads=IDF.keys, writes=IDF.keys)
            S.op('pool', lambda e: e.affine_select(out=CONBt[:], in_=CONBt[:], compare_op=ALU.not_equal, fill=1.0,
                                                   base=0, pattern=[[-1, 128]], channel_multiplier=1),
                 reads=IDB.keys, writes=IDB.keys)
            for (lo, hi) in ((0, 64), (64, 128)):
                k.memset(T(CONt[lo:hi, 1, lo:hi], BONES.keys), 1.0)
                k.memset(T(CONt[lo:hi, 2, lo:hi], BONES64.keys), 1.0 / 64.0)
                for m in (4, 5, 6):
                    k.memset(T(CONt[lo:hi, m, lo:hi], CON(m).keys), 1.0)
            k.memset(ONESD, 1.0 / 1024.0)
            S.op('pool', lambda e: e.affine_select(out=CONt[:, 4, :], in_=CONt[:, 4, :], compare_op=ALU.is_ge, fill=0.0,
                                                   base=0, pattern=[[1, 128]], channel_multiplier=-1),
                 reads=M_LE.keys, writes=M_LE.keys)
            S.op('pool', lambda e: e.affine_select(out=CONt[:, 5, :], in_=CONt[:, 5, :], compare_op=ALU.is_gt, fill=0.0,
                                                   base=0, pattern=[[1, 128]], channel_multiplier=-1),
                 reads=M_LT.keys, writes=M_LT.keys)
            S.op('pool', lambda e: e.affine_select(out=CONt[:, 6, :], in_=CONt[:, 6, :], compare_op=ALU.is_gt, fill=0.0,
                                                   base=0, pattern=[[-1, 128]], channel_multiplier=1),
                 reads=M_GT.keys, writes=M_GT.keys)
            for mi, src_ in enumerate((M_LE, M_LT, M_GT)):
                k.tt(T(MSKt[:, mi, :], ['MSK']), src_[:, 0:64], src_[:, 64:128], ALU.add)
            k.tt(T(IDSt[:], ['IDS']), IDB[:, 0:64], IDB[:, 64:128], ALU.add)
            k.memset(RST, 1.0)
            k.memset(T(RSTt[:, 0:NT:CH], RST.keys), 0.0)
            k.memset(T(HSTt[:], ['HST']), 0.0)
            k.memset(T(CARt[:], ['CAR']), 0.0)
            k.memset(T(CVCt[:], ['CVC']), 0.0)
            S.barrier()

            PR = T(WPt[0:2 * NRL, 0:2, :].rearrange("p a b -> p (a b)")[:, 0:D], [('wp', 0, s_) for s_ in range(4)] + [('wp', 1, s_) for s_ in range(4)])
            k.dma(PR, prow)
            for c in range(KC):
                pb = nbank()
                k.tr(pb[:, 0:2 * NRL], PR[:, c * 128:(c + 1) * 128], IDF[0:2 * NRL, 0:2 * NRL])
                for l in range(2):
                    k.cp(T(PARt[:, c, l, 0:NRL], ['PAR']), pb[:, l * NRL:(l + 1) * NRL], eng='act')
            for l in range(2):
                pv = T(PARt[:, :, l, :], ['PAR'])
                k.ts(pv[:, :, 27], pv[:, :, 7], -1.0, ALU.mult, 1.0, ALU.add)
                k.act(pv[:, :, 28], pv[:, :, 18], AF.Exp, scale=-1.0)
                k.act(pv[:, :, 28], pv[:, :, 28], AF.Ln, bias=1.0)
                k.ts(pv[:, :, 29], pv[:, :, 28], -16.0, ALU.mult)
                k.ts(pv[:, :, 28], pv[:, :, 28], -8.0, ALU.mult)

            if stop_at == 'const':
                raise Stop()
            MT = T(WPt[:, 16:18, :].rearrange("p a b -> p (a b)")[:, 0:1024].bitcast(BF16).rearrange("p (c m) -> p c m", c=8),
                   [('wp', 16, s_) for s_ in range(4)] + [('wp', 17, s_) for s_ in range(4)])
            for mc in range(2):
                mt = wpspan(0, 2).v(lambda a: a.rearrange("p a b -> p (a b)")[:, 0:D])
                k.dma(mt, mem[mc * 128:(mc + 1) * 128, :])
                for half in range(2):
                    pb = nbank()
                    for j in range(4):
                        c = half * 4 + j
                        k.tr(pb[:, j * 128:(j + 1) * 128], mt[:, c * 128:(c + 1) * 128], IDF)
                    k.cp(MT[:, half * 4:half * 4 + 4, mc * 128:(mc + 1) * 128],
                         pb.v(lambda a: a.rearrange("p (j m) -> p j m", j=4)), eng='act')
            if stop_at == 'memT':
                raise Stop()
            for l in range(2):
                for which, wsrc_t, odram in ((0, xa_wk, o_mk), (1, xa_wv, o_mv)):
                    if stop_at == 'kv0' and (l, which) == (0, 1):
                        raise Stop()
                    NAT = wpspan(2, 4).v(lambda a: a.rearrange("p a b -> p (a b)")[:, 0:2048].rearrange("p (m e) -> p m e", m=2))
                    for e in range(KC):
                        W = wq.next(wsrc(wsrc_t, l, e * 128))
                        if which == 0:
                            pb = nbank()
                            for kc in range(KC):
                                k.mm(pb[:, 0:256], W[:, kc, :], MT[:, kc, :], start=(kc == 0), stop=(kc == KC - 1))
                            k.cp(T(KTt[:, l, e, :], [('KT', l)]), pb[:, 0:256], eng='act')
                        pb = nbank()
                        for mc in range(2):
                            for kc in range(KC):
                                k.mm(pb[:, mc * 128:(mc + 1) * 128], MT[:, kc, mc * 128:(mc + 1) * 128], W[:, kc, :],
                                     start=(kc == 0), stop=(kc == KC - 1))
                        k.cp(NAT[:, :, e * 128:(e + 1) * 128], pb[:, 0:256].v(lambda a: a.rearrange("p (m n) -> p m n", m=2)), eng='dve')
                        if which == 1:
                            k.cp(T(VNt[:, l, :, e * 128:(e + 1) * 128], [('VN', l)]),
                                 pb[:, 0:256].v(lambda a: a.rearrange("p (m n) -> p m n", m=2)), eng='act')
                    k.dma(odram[l].rearrange("(m p) e -> p m e", p=128), NAT)

            if stop_at == 'memkv':
                raise Stop()
            passes = [
                [(0, 0, NT, 'p'), (512, 512, NT, 'p')],
                [(1024, 0, NT, 'p'), (1536, 512, NT, 'p'), (None, 1024, NS, 's')],
            ]
            if short:
                passes = [[(0, 0, NT, 'p'), (512, 512, NT, 'p'), (None, 1024, NS, 's')]]
            npass[0] = len(passes)
            for pi, tiles in enumerate(passes):
                S.mark('LOADX p%d' % pi)
                load_x(tiles)
                if stop_at != 'loadx':
                    for l in range(2):
                        layer_pass(l, pi, tiles)
                S.mark('STOREY p%d' % pi)
                store_y(tiles)
            S.mark('END')

        def load_x(tiles):
            for (t0, lc, n, kind) in tiles:
                nblk = (n + 127) // 128
                for bi in range(nblk):
                    nb_ = min(128, n - bi * 128)
                    xin = wpspan(0, 2).v(lambda a: a.rearrange("p a b -> p (a b)")[:, 0:D])
                    src = xp[t0 + bi * 128:t0 + bi * 128 + nb_, :] if kind == 'p' else xs[:, :]
                    k.dma(xin[0:nb_, :], src)
                    for half in range(2):
                        pb = nbank()
                        for j in range(4):
                            c = half * 4 + j
                            k.tr(pb[:, j * 128:j * 128 + nb_], xin[0:nb_, c * 128:(c + 1) * 128], IDF[0:nb_, 0:nb_])
                        pv = pb.v(lambda a: a.rearrange("p (j m) -> p j m", j=4)[:, :, 0:nb_])
                        c0_ = half * 4
                        dst32 = T(X32t[:, c0_:c0_ + 4, lc + bi * 128:lc + bi * 128 + nb_], [('X32', c, lc) for c in range(c0_, c0_ + 4)])
                        dstb = T(XBt[:, c0_:c0_ + 4, lc + bi * 128:lc + bi * 128 + nb_], [('XB', c, lc) for c in range(c0_, c0_ + 4)])
                        k.cp(dst32, pv, eng='act')
                        k.cp(dstb, pv, eng='dve')

        def store_y(tiles):
            for (t0, lc, n, kind) in tiles:
                nblk = (n + 127) // 128
                for bi in range(nblk):
                    nb_ = min(128, n - bi * 128)
                    yo = wpspan(2, 2).v(lambda a: a.rearrange("p a b -> p (a b)")[:, 0:D])
                    for half in range(2):
                        pb = nbank()
                        for j in range(4):
                            c = half * 4 + j
                            src = T(X32t[:, c, lc + bi * 128:lc + bi * 128 + nb_], [('X32', c, lc)])
                            k.tr(pb[0:nb_, j * 128:(j + 1) * 128], src, IDF)
                        k.cp(yo[0:nb_, half * 512:(half + 1) * 512], pb[0:nb_, :], eng='act' if half == 0 else 'dve')
                    dst = o_yp[t0 + bi * 128:t0 + bi * 128 + nb_, :] if kind == 'p' else o_ys[:, :]
                    k.dma(dst, yo[0:nb_, :])

        def proj(W, lc, n, pb=None, src=XB):
            if pb is None:
                pb = nbank()
            for kc in range(KC):
                k.mm(pb[:, 0:n], W[:, kc, :], src(kc, lc, n), start=(kc == 0), stop=(kc == KC - 1))
            return pb

        def proj_shift(l, cid, W, tile, raw, dtmp, last_p):
            (t0, lc, n, kind) = tile
            pb = proj(W, lc, n)
            k.cp(raw[:, 1:n + 1], pb[:, 0:n], eng='act')
            mu = par(l, cid // 8 if cid < 24 else 3, cid % 8 if cid < 24 else cid - 24)
            car = T(CARt[:, l, cid:cid + 1], [('CAR', l, cid)])
            if kind == 'p':
                k.cp(raw[:, 0:1], car, eng='pool')
                prev = raw[:, 0:n]
                k.cp(car, raw[:, n:n + 1], eng='pool')
                if last_p:
                    k.cp(T(SHOt[:, cid, 0:1], [('SHO', cid)]), raw[:, n:n + 1], eng='pool')
            else:
                prev = T(SHSt[:, cid, :], ['SHS'])
                k.cp(T(SHOt[:, cid, 1:1 + NS], [('SHO', cid)]), raw[:, 1:n + 1], eng='pool')
            k.tt(dtmp[:, 0:n], prev, raw[:, 1:n + 1], ALU.subtract)
            k.stt(raw[:, 1:n + 1], dtmp[:, 0:n], mu, raw[:, 1:n + 1], ALU.mult, ALU.add)
            return raw[:, 1:n + 1]

        def layernorm(l, jg, jb, tile):
            (t0, lc, n, kind) = tile
            pm = nbank()
            pq = nbank()
            for c in range(KC):
                sq = wp(c % 3, n)
                k.act(sq, X32(c, lc, n), AF.Square)
                k.mm(pm[:, 0:n], ONESD, X32(c, lc, n), start=(c == 0), stop=(c == KC - 1))
                k.mm(pq[:, 0:n], ONESD, sq, start=(c == 0), stop=(c == KC - 1))
            mean = wp(3, n)
            rstd = wp(4, n)
            k.cp(mean, pm[:, 0:n], eng='act')
            k.tt(rstd, mean, mean, ALU.mult)
            k.tt(rstd, pq[:, 0:n], rstd, ALU.subtract)
            k.act(rstd, rstd, AF.Ln, bias=T(SMALLt[:, 0:1], ['SMALL0']))
            k.act(rstd, rstd, AF.Exp, scale=-0.5)
            for c in range(KC):
                d = wp(5 + c % 4, n)
                e_ = 'gp' if c % 2 else 'dve'
                k.tt(d, X32(c, lc, n), mean, ALU.subtract, eng=e_)
                k.tt(d, d, rstd, ALU.mult, eng=e_)
                k.ts(X32(c, lc, n), d, par(l, jg, c), ALU.mult, par(l, jb, c), ALU.add, eng=e_)
                k.act(XB(c, lc, n), d, AF.Identity, bias=par(l, jb, c), scale=par(l, jg, c))

        def resid_stage(wsrc_fn, src, tiles, first=True):
            for e in range(KC):
                W = wq.next(wsrc_fn(e))
                for (t0, lc, n, kind) in tiles:
                    pb = proj(W, lc, n, src=src)
                    if first:
                        k.stt(X32(e, lc, n), X32(e, lc, n), ALPHA, pb[:, 0:n], ALU.mult, ALU.add)
                    else:
                        k.tt(X32(e, lc, n), X32(e, lc, n), pb[:, 0:n], ALU.add)

        def layer_pass(l, pi, tiles):
            last_p_tile = max((i for i, t in enumerate(tiles) if t[3] == 'p'))
            is_last_pass = (pi == npass[0] - 1)
            has_s = any(t[3] == 's' for t in tiles)
            k.dma(T(W2A2t[0:64, :], ['W2A2']), rw_w2[l], q='pool')
            k.dma(T(W2A2t[64:128, :], ['W2A2']), rw_a2[l], q='pool')
            k.dma(T(G2t[:], ['G2']), rw_g2[l], q='pool')
            k.memset(T(WABt[:], ['WAB']), 0.0)
            for c in range(KC):
                for hh in range(2):
                    lo = hh * 64
                    k.dma(T(WABt[lo:lo + 64, 0, c, lo:lo + 64], ['WAB']), lru_wa[l, 2 * c + hh], q='pool')
                    k.dma(T(WABt[lo:lo + 64, 1, c, lo:lo + 64], ['WAB']), lru_wx[l, 2 * c + hh], q='pool')
            k.memset(T(SMALLt[:, 0:1], ['SMALL0']), LN_EPS)
            k.memset(T(SMALLt[:, 1:2], ['SMALL1']), GN_EPS)
            if has_s:
                load_sample_states(l)

            S.mark('M0 l%d p%d' % (l, pi))
            for ci in range(2):
                W = wq.next(wsrc(w_in, l, C_W + ci * 128))
                for ti, tile in enumerate(tiles):
                    (t0, lc, n, kind) = tile
                    psf = proj_shift(l, 24 + ci, W, tile, wp(0), wp(1), is_last_pass and ti == last_p_tile)
                    if ci == 0:
                        k.act(LW(0, lc, n)[0:64], psf[0:64], AF.Tanh)
                        k.cp(LW(0, lc, n)[64:128], psf[64:128], eng='act')
                    else:
                        k.act(LW(1, lc, n), psf, AF.Sigmoid)

            if stop_at == 'm0':
                raise Stop()
            S.mark('M1 l%d p%d' % (l, pi))
            S.barrier()
            rr['nbmod'] = 3
            iters = [(hp, ti, tile) for hp in range(KC) for ti, tile in enumerate(tiles)]
            Wd = {}

            def getW(hp):
                if hp not in Wd:
                    Wd[hp] = (wq.next(wsrc(w_in, l, C_R + hp * 128)), wq.next(wsrc(w_in, l, C_K + hp * 128)),
                              wq.next(wsrc(w_in, l, C_V + hp * 128)))
                return Wd[hp]

            def mkAB(i):
                hp, ti, tile = iters[i]
                ctx = {}
                return ctx, rwkv_AB_gen(l, hp, tile, getW(hp), is_last_pass and ti == last_p_tile, i % 2, ctx)
            ctx, gAB = mkAB(0)
            for _ in gAB:
                pass
            pend_post = None
            for i, (hp, ti, tile) in enumerate(iters):
                nctx, ngen = mkAB(i + 1) if i + 1 < len(iters) else (None, None)
                chain_ = itertools.chain(pend_post if pend_post is not None else [], ngen if ngen is not None else [])
                pend_post = None
                if tile[3] == 'p':
                    pend_post = rwkv_C(l, hp, tile, ctx, chain_)
                for _ in chain_:
                    pass
                ctx = nctx
                if is_last_pass and ti == len(tiles) - 1:
                    Hs = T(HSTt[:, l, hp, 0:64], [('HST', l, hp)])
                    pb = nbank()
                    k.tr(pb[0:64, 0:128], Hs, IDF)
                    ho = T(HOUTt[:], ['HOUT'])
                    k.cp(ho[0:64], pb[0:64, 0:128], eng='act')
                    for hh in range(2):
                        lo = hh * 64
                        k.dma(o_prw[l, 2 * hp + hh], ho[0:64, lo:lo + 64])
            if pend_post is not None:
                for _ in pend_post:
                    pass
            rr['nbmod'] = 8
            if has_s:
                sample_rwkv_state(l)
                stile = [t for t in tiles if t[3] == 's'][0]
                for hp in range(KC):
                    rwkv_post(l, hp, stile, T(YSt[:, hp, :], ['YS']), T(SVt[:, 2, hp, :], ['SV']), None, None, sample=True)
            S.barrier()
            if debug and l == 0 and pi == dbg_pi:
                k.dma(dbg['yb'], T(YBt[:], [('YB', c, lc) for c in range(KC) for lc in (0, 512, 1024)]), q='pool')

            if stop_at == 'm1':
                raise Stop()
            S.mark('M2 l%d p%d' % (l, pi))
            for e in range(KC):
                Wp = wq.next(wsrc(rw_proj, l, e * 128))
                Wg = wq.next(wsrc(w_in, l, C_GATE + e * 128))
                for (t0, lc, n, kind) in tiles:
                    po = proj(Wp, lc, n, src=YB)
                    pg = proj(Wg, lc, n)
                    g0 = wp(0, n)
                    k.act(g0, pg[:, 0:n], AF.Sigmoid, bias=par(l, 19, e))
                    k.tt(EB(e, lc, n), g0, po[:, 0:n], ALU.mult)

            if stop_at == 'm2':
                raise Stop()
            S.mark('M3 l%d p%d' % (l, pi))
            for c in range(0, KC, 2):
                Ws = [(wq.next(wsrc(w_in, l, C_U + cc * 128)), wq.next(wsrc(w_in, l, C_Y + cc * 128))) for cc in (c, c + 1)]
                for ti, tile in enumerate(tiles):
                    alive = [lru_gen(l, c + d_, tile, Ws[d_][0], Ws[d_][1], is_last_pass and ti == last_p_tile, 10 * d_) for d_ in range(2)]
                    while alive:
                        for g_ in list(alive):
                            try:
                                next(g_)
                            except StopIteration:
                                alive.remove(g_)

            if stop_at == 'm3':
                raise Stop()
            S.mark('M4 l%d p%d' % (l, pi))
            for e in range(KC):
                Wp = wq.next(wsrc(lru_proj, l, e * 128))
                Wg = wq.next(wsrc(w_in, l, C_GATE + D + e * 128))
                for (t0, lc, n, kind) in tiles:
                    po = proj(Wp, lc, n, src=YB)
                    pg = proj(Wg, lc, n)
                    g1 = wp(0, n)
                    k.act(g1, pg[:, 0:n], AF.Sigmoid, bias=par(l, 20, e))
                    k.tt(g1, g1, po[:, 0:n], ALU.mult)
                    k.tt(EB(e, lc, n), EB(e, lc, n), g1, ALU.add)

            S.mark('M5 l%d p%d' % (l, pi))
            resid_stage(lambda e: wsrc(w_out_mix, l, e * 128), EB, tiles)
            for tile in tiles:
                layernorm(l, 21, 22, tile)
            if debug and l == 0 and pi == dbg_pi:
                k.dma(dbg['x1'], T(X32t[:], [('X32', c, lc) for c in range(KC) for lc in (0, 512, 1024)]))

            if stop_at == 'm5':
                raise Stop()
            S.mark('AT l%d p%d' % (l, pi))
            for e in range(KC):
                W = wq.next(wsrc(xa_wq, l, e * 128))
                for (t0, lc, n, kind) in tiles:
                    pb = proj(W, lc, n)
                    if kind == 'p':
                        k.act(YB(e, lc, n), pb[:, 0:n], AF.Copy, scale=1.0 / 16.0)
                    else:
                        k.act(T(YSt[:, e, :], ['YS']), pb[:, 0:n], AF.Copy, scale=1.0 / 16.0)
            for tile in tiles:
                if tile[3] == 'p':
                    attn_prompt(l, tile)
                else:
                    attn_sample(l, tile)
            resid_stage(lambda e: wsrc(xa_wo, l, e * 128), EB, tiles)
            for tile in tiles:
                layernorm(l, 23, 24, tile)

            if stop_at == 'attn':
                raise Stop()
            S.mark('MLP l%d p%d' % (l, pi))
            for g in range(4):
                for j in range(KC):
                    W = wq.next(wsrc(mlp_up, l, g * D + j * 128))
                    for (t0, lc, n, kind) in tiles:
                        pb = proj(W, lc, n)
                        r_ = wp(j % 2, n)
                        k.act(r_, pb[:, 0:n], AF.Relu)
                        k.tt(EB(j, lc, n), r_, r_, ALU.mult, eng='gp' if j % 2 else 'dve')
                resid_stage(lambda e: wsrc(mlp_down, l, e * 128, r0=g * D), EB, tiles, first=(g == 0))
            for tile in tiles:
                layernorm(l, 25, 26, tile)

            if stop_at == 'mlp':
                raise Stop()
            S.mark('OUT l%d p%d' % (l, pi))
            if is_last_pass:
                def store_T(get_src, nchk, dst, npart):
                    tm = wpspan(8, 7).v(lambda a_: a_.rearrange("p a b -> p (a b)"))[0:npart, 0:nchk * 128]
                    for c0_ in range(0, nchk, 4):
                        pb = nbank()
                        cn = min(4, nchk - c0_)
                        for c in range(cn):
                            k.tr(pb[0:npart, c * 128:(c + 1) * 128], get_src(c0_ + c), IDF)
                        k.cp(tm[:, c0_ * 128:(c0_ + cn) * 128], pb[0:npart, 0:cn * 128], eng='act')
                    k.dma(dst, tm)
                shk = [('SHO', c) for c in range(26)]
                store_T(lambda c: T(SHOt[:, :, 0], shk), 1, o_psh[l].rearrange("(c p) -> c p", p=128), 26)
                store_T(lambda c: T(HOt[:, :, 0], ['HO']), 1, o_ph[l].rearrange("(c p) -> c p", p=128), 8)
                for j in range(3):
                    store_T(lambda c, j=j: T(CVOt[:, :, j, 0], ['CVO']), 1, o_pcv[l, j].rearrange("(c p) -> c p", p=128), 8)
                store_T(lambda c: T(SHOt[:, c, 1:1 + NS], [('SHO', c)]), 26, o_ssh[l], NS)
                store_T(lambda c: T(HOt[:, c, 1:1 + NS], ['HO']), 8, o_sh[l], NS)
                for j in range(3):
                    store_T(lambda c, j=j: T(CVOt[:, c, j, 1:1 + NS], ['CVO']), 8, o_scv[l, :, j, :], NS)

        def load_sample_states(l):
            def tload(src2d, ncols, dstfn):
                tm = wpspan(8, 7).v(lambda a: a.rearrange("p a b -> p (a b)"))[0:NS, 0:ncols]
                k.dma(tm, src2d)
                nch = ncols // 128
                for c0_ in range(0, nch, 26):
                    pb = nbank()
                    cn = min(26, nch - c0_)
                    for c in range(cn):
                        k.tr(pb[:, c * NS:(c + 1) * NS], tm[:, (c0_ + c) * 128:(c0_ + c + 1) * 128], IDF[0:NS, 0:NS])
                    dstfn(c0_, cn, pb)
            tload(ssh[l], NSH, lambda c0_, cn, pb: k.cp(T(SHSt[:, c0_:c0_ + cn, :], ['SHS']),
                                                        pb[:, 0:cn * NS].v(lambda a: a.rearrange("p (c b) -> p c b", b=NS)), eng='act'))
            tload(shh[l], D, lambda c0_, cn, pb: k.cp(T(HS0t[:, c0_:c0_ + cn, :], ['HS0']),
                                                      pb[:, 0:cn * NS].v(lambda a: a.rearrange("p (c b) -> p c b", b=NS)), eng='act'))
            for j in range(3):
                tload(scv[l, :, j, :], D, lambda c0_, cn, pb, j=j: k.cp(T(CSt[:, c0_:c0_ + cn, j, :], ['CS']),
                                                                        pb[:, 0:cn * NS].v(lambda a: a.rearrange("p (c b) -> p c b", b=NS)), eng='act'))

        ARB = [T(EBt[:, 5120 + i_ * 1032:5120 + (i_ + 1) * 1032].bitcast(F32), [('ar', i_, s_) for s_ in range(4)]) for i_ in range(3)]
        BLKP = [{nm: T(arena((P_ * 5 + i_) * 512, 512), [('blk', P_, nm)]) for i_, nm in enumerate(['R', 'K', 'B', 'A', 'V'])}
                for P_ in range(2)]

        def rwkv_AB_gen(l, hp, tile, W3, last_p, P, ctx):
            (t0, lc, n, kind) = tile
            dt = ARB[0]
            r = proj_shift(l, hp, W3[0], tile, wp(0), dt, last_p)
            yield
            kx = proj_shift(l, 8 + hp, W3[1], tile, wp(1), dt, last_p)
            yield
            vx = proj_shift(l, 16 + hp, W3[2], tile, wp(2), dt, last_p)
            yield
            sw = wp(5, n)
            asg = wp(17, n)
            kk = wp(18, n)
            t8 = wp(19, n)
            km = ARB[1][:, 0:n]
            cl = ARB[0][:, 0:n]
            en = ARB[2][:, 0:n]
            gg = slot(6, 2 * P, 2)[:, 0:n]
            bon = slot(10, 2 * P, 2)[:, 0:n]
            pz = nbank()
            k.mm(pz[:, 0:n], T(W2A2t[0:64, hp * 128:(hp + 1) * 128], ['W2A2']), LW(0, lc, n)[0:64])
            k.act(sw, pz[:, 0:n], AF.Sigmoid, bias=par(l, 4, hp))
            yield
            pz = nbank()
            k.mm(pz[:, 0:n], T(W2A2t[64:128, hp * 128:(hp + 1) * 128], ['W2A2']), LW(0, lc, n)[64:128])
            k.act(asg, pz[:, 0:n], AF.Sigmoid, bias=par(l, 5, hp))
            yield
            pz = nbank()
            k.mm(pz[:, 0:n], T(G2t[:, hp * 128:(hp + 1) * 128], ['G2']), LW(1, lc, n))
            k.cp(gg, pz[:, 0:n], eng='act')
            yield
            k.ts(kk, kx, par(l, 6, hp), ALU.mult)
            k.act(t8, kk, AF.Square)
            yield
            pn = nbank()
            k.mm(pn[:, 0:n], BONES, t8)
            k.ts(t8, pn[:, 0:n], 1e-24, ALU.max)
            yield
            k.act(t8, t8, AF.Ln)
            k.act(t8, t8, AF.Exp, scale=-0.5)
            yield
            k.tt(kk, kk, t8, ALU.mult)
            yield
            k.ts(km, asg, par(l, 7, hp), ALU.mult, par(l, 27, hp), ALU.add)
            k.tt(km, km, kx, ALU.mult)
            yield
            k.stt(t8, r, par(l, 8, hp), km, ALU.mult, ALU.mult)
            pbon = nbank()
            k.mm(pbon[:, 0:n], BONES, t8)
            k.tt(bon, pbon[:, 0:n], vx, ALU.mult)
            yield
            ctx['gg'], ctx['bon'] = gg, bon
            if kind == 's':
                k.cp(T(SVt[:, 0, hp, :], ['SV']), r, eng='act')
                k.cp(T(SVt[:, 1, hp, :], ['SV']), km, eng='act')
                k.cp(T(SVt[:, 2, hp, :], ['SV']), vx, eng='act')
                k.act(T(SVt[:, 3, hp, :], ['SV']), sw, AF.Exp, scale=-C0)
                k.ts(T(SVt[:, 4, hp, :], ['SV']), kk, -1.0, ALU.mult)
                k.tt(T(SVt[:, 5, hp, :], ['SV']), kk, asg, ALU.mult)
                k.cp(T(SGt[:, 0, hp, :], ['SG']), gg, eng='act')
                k.cp(T(SGt[:, 1, hp, :], ['SG']), bon, eng='act')
                return
            BL = BLKP[P]
            k.scan(cl, RST, sw, 0.0)
            yield
            ep = t8
            k.act(ep, cl, AF.Exp, scale=-C0)
            k.act(en, cl, AF.Exp, scale=C0)
            yield
            k.tt(sw, cl, sw, ALU.subtract)
            k.act(sw, sw, AF.Exp, scale=-C0)
            yield
            k.tt(asg, kk, asg, ALU.mult)
            nch = n // CH
            pc = T(PCt[:, P, 0:nch], [('PC', P)])
            k.cp(pc, ep.v(lambda a_: a_[:, CH - 1:n:CH]), eng='act')
            yield
            k.tt(BL['R'], r, ep, ALU.mult)
            yield
            k.tt(BL['K'], km, en, ALU.mult)
            yield
            k.tt(BL['B'], asg, en, ALU.mult)
            yield
            k.stt(BL['A'], kk, -1.0, sw, ALU.mult, ALU.mult)
            k.cp(BL['V'], vx, eng='act')
            ctx['pc'] = pc
            ctx['BL'] = BL
            yield

        def rwkv_C(l, hp, tile, ctx, nxt):
            (t0, lc, n, kind) = tile
            nch = n // CH
            BL, pc = ctx['BL'], ctx['pc']

            def adv(cnt):
                if nxt is not None:
                    for _ in range(cnt):
                        next(nxt, None)
            Y = wp(16, n)
            Hs = T(HSTt[:, l, hp, 0:64], [('HST', l, hp)])
            HB = T(HBFt[:], ['HBF'])
            k.cp(HB, Hs, eng='act')
            blk = lambda nm, c: BL[nm][:, c * CH:(c + 1) * CH]
            nq = nch // 4
            assert nq == 2
            alive = [pre_quad_gen(q, blk, QS[q]) for q in range(nq)]
            while alive:
                for g_ in list(alive):
                    try:
                        next(g_)
                    except StopIteration:
                        alive.remove(g_)
                    adv(1)
            Ybs = [psbank(3), psbank(7)]
            pend = []
            for c in range(nch):
                seq_chunk(c, c % 4, blk, QS[c // 4], Hs, HB, pc[:, c:c + 1], Ybs, pend)
                adv(2)
            pend.pop()()
            if nxt is not None:
                for _ in nxt:
                    pass
            for par_ in range(2):
                k.cp(Y.v(lambda a_: a_.rearrange("p (c two s) -> p c two s", two=2, s=CH)[:, :, par_, :]),
                     Ybs[par_][:, 0:n].v(lambda a_: a_.rearrange("p (c two s) -> p c two s", two=2, s=CH)[:, :, par_, :]),
                     eng='act' if par_ else 'dve')
            return rwkv_post_gen(l, hp, tile, Y, ctx['gg'], ctx['bon'], wp(18))

        def mkqs(b0, b1, b2, b3, b4):
            return {'VT': slot(b0, 0), 'KT': slot(b0, 1), 'BT': slot(b0, 2), 'AT': slot(b0, 3), 'TRS': slot(b0, 0, 4),
                    'ATrk': slot(b1, 0), 'ATrb': slot(b1, 1), 'ATR': slot(b1, 0, 2), 'ATak': slot(b1, 2), 'TT': slot(b1, 3),
                    'X': [slot(b2, 0), slot(b2, 2)], 'XT': [slot(b2, 1), slot(b2, 3)], 'XX': [slot(b2, 0, 2), slot(b2, 2, 2)],
                    'TTt': [slot(b3, 0), slot(b3, 1)], 'ZV': slot(b3, 2), 'Wh': slot(b3, 3),
                    'Uv': slot(b4, 0), 'WhT': slot(b4, 1), 'UW': slot(b4, 0, 2), 'GT': slot(b4, 2), 'N': slot(b4, 3), 'GN': slot(b4, 2, 2)}
        QS = [mkqs(3, 4, 7, 8, 14), mkqs(9, 11, 12, 13, 15)]
        rr['qb'] = 0

        def nqbank():
            b = rr['qb']
            rr['qb'] = (b + 1) % 3
            return psbank(4 + b)

        def hmm(out_t, lt, rh, start=True, stop=True):
            for h_ in range(2):
                lo = h_ * 64
                k.mm(out_t[lo:lo + 64], lt[lo:lo + 64], rh[lo:lo + 64], start=start, stop=stop)

        def ch(t, j):
            return t[:, j * CH:(j + 1) * CH]

        def bcm(mi, nrep):
            return T(MSKt[:, mi, :], ['MSK']).v(lambda a_: a_.unsqueeze(1).to_broadcast([128, nrep, 64]))

        def r3(t, nrep):
            return t.v(lambda a_: a_.rearrange("p (c n) -> p c n", c=nrep))

        def pre_quad_gen(q, blk, qs):
            cs = [4 * q + j for j in range(4)]
            bk = nqbank()
            bkb = bk.v(lambda a_: a_.bitcast(BF16))
            for si, nm in enumerate(('V', 'K', 'B', 'A')):
                for j, c in enumerate(cs):
                    for h_ in range(2):
                        lo = h_ * 64
                        k.tr(bkb[lo:lo + 64, si * 256 + j * 64:si * 256 + (j + 1) * 64], blk(nm, c)[lo:lo + 64], IDB[lo:lo + 64, lo:lo + 64])
            k.cp(qs['TRS'], bkb[:, 0:1024], eng='act')
            yield
            X, XT = qs['X'][0], qs['XT'][0]
            for lt, rh, mi, dst in (('B', 'A', 1, XT), ('A', 'B', 2, X), ('K', 'A', 1, qs['ATak'])):
                bk = nqbank()
                for j, c in enumerate(cs):
                    hmm(ch(bk, j), blk(lt, c), blk(rh, c))
                k.tt(r3(dst, 4), r3(bk[:, 0:256], 4), bcm(mi, 4), ALU.mult)
                yield
            bk = nqbank()
            for si, lt in enumerate(('K', 'B')):
                for j, c in enumerate(cs):
                    hmm(bk[:, si * 256 + j * 64:si * 256 + (j + 1) * 64], blk(lt, c), blk('R', c))
            k.tt(r3(qs['ATR'], 8), r3(bk, 8), bcm(0, 8), ALU.mult)
            yield
            bk = nqbank()
            for j in range(4):
                hmm(ch(bk, j), ch(qs['ATak'], j), ch(qs['VT'], j))
            k.cp(qs['ZV'], bk[:, 0:256], eng='act')
            yield
            TT = qs['TTt'][0]
            ids4 = T(IDSt[:], ['IDS']).v(lambda a_: a_.unsqueeze(1).to_broadcast([128, 4, 64]))
            k.tt(r3(TT, 4), r3(XT, 4), ids4, ALU.add)
            tpar = 0

            def tt_update(TT, Xf, dst):
                bk3 = nqbank()
                for j in range(4):
                    hmm(ch(bk3, j), ch(Xf, j), ch(TT, j))
                k.tt(dst, bk3[:, 0:256], TT, ALU.add)
                return dst
            for L in range(1, 6):
                p_ = L % 2
                bk = nqbank()
                for j in range(4):
                    hmm(ch(bk, j), ch(XT, j), ch(X, j))
                if L < 5:
                    for j in range(4):
                        hmm(bk[:, 256 + j * 64:256 + (j + 1) * 64], ch(X, j), ch(XT, j))
                    k.cp(qs['XX'][p_], bk, eng='act')
                else:
                    k.cp(qs['X'][p_], bk[:, 0:256], eng='act')
                if L >= 2:
                    tpar = 1 - tpar
                    TT = tt_update(TT, X, qs['TTt'][tpar])
                X, XT = qs['X'][p_], qs['XT'][p_]
                yield
            tt_update(TT, X, qs['TT'])
            yield
            TTf = qs['TT']
            bk = nqbank()
            for j in range(4):
                hmm(ch(bk, j), ch(TTf, j), ch(qs['ZV'], j))
            for j in range(4):
                hmm(bk[:, 256 + j * 64:256 + (j + 1) * 64], ch(qs['AT'], j), ch(TTf, j))
            k.cp(qs['UW'], bk, eng='act')
            bk = nqbank()
            for j in range(4):
                hmm(ch(bk, j), ch(TTf, j), ch(qs['AT'], j))
            k.cp(qs['Wh'], bk[:, 0:256], eng='dve')
            yield
            bk = nqbank()
            for j in range(4):
                hmm(ch(bk, j), ch(qs['Wh'], j), ch(qs['BT'], j))
            for j in range(4):
                o_ = bk[:, 256 + j * 64:256 + (j + 1) * 64]
                for h_ in range(2):
                    lo = h_ * 64
                    k.mm(o_[lo:lo + 64], ch(qs['KT'], j)[lo:lo + 64], ch(qs['VT'], j)[lo:lo + 64], start=True, stop=False)
                    k.mm(o_[lo:lo + 64], ch(qs['BT'], j)[lo:lo + 64], ch(qs['Uv'], j)[lo:lo + 64], start=False, stop=True)
            k.cp(qs['GN'], bk, eng='act')
            yield

        def seq_chunk(c, j, blk, qs, Hs, HB, pcc, Ybs, pend):
            R = blk('R', c)
            g = lambda nm: ch(qs[nm], j)
            hsp = T(HSPt[:], ['HSP'])
            k.tt(hsp, Hs, g('N'), ALU.add)
            k.ts(hsp, hsp, pcc, ALU.mult)
            bku = nqbank()
            hmm(bku[:, 0:64], g('WhT'), HB)
            ys = Ybs[c % 2][:, c * CH:(c + 1) * CH]
            for h_ in range(2):
                lo = h_ * 64
                k.mm(ys[lo:lo + 64], HB[lo:lo + 64], R[lo:lo + 64], start=True, stop=False)
                k.mm(ys[lo:lo + 64], g('VT')[lo:lo + 64], g('ATrk')[lo:lo + 64], start=False, stop=False)
            bk = nqbank()
            hmm(bk[:, 0:64], g('GT'), HB)
            if pend:
                pend.pop()()
            k.stt(HB, bk[:, 0:64], pcc, hsp, ALU.mult, ALU.add)
            k.stt(Hs, bk[:, 0:64], pcc, hsp, ALU.mult, ALU.add)
            U = T(ZUt[:, c % 2, :], [('U', c % 2)])
            k.tt(U, bku[:, 0:64], g('Uv'), ALU.add)

            def fin(ys=ys, U=U, atrb=g('ATrb')):
                for h_ in range(2):
                    lo = h_ * 64
                    k.mm(ys[lo:lo + 64], U[lo:lo + 64], atrb[lo:lo + 64], start=False, stop=True)
            pend.append(fin)

        def evac_b(dst, src, eng):
            if eng == 'act':
                k.cp(dst, src, eng='act')
            else:
                k.cp(dst, src, eng='dve')

        def chunk_pre(c, blk):
            R, K_, B_, A_, V_ = blk('R', c), blk('K', c), blk('B', c), blk('A', c), blk('V', c)
            pers = SMB[(c % 2) * NPERS:(c % 2 + 1) * NPERS]
            out = {}
            for i, (nm, src) in enumerate((('VT', V_), ('KT', K_), ('BT', B_))):
                sl = nslot()
                slb = sl.v(lambda a: a.bitcast(BF16)[:, 0:128])
                k.tr(slb, src, IDB)
                d = pers[i]
                k.cp(d, slb, eng='act')
                out[nm] = d
            for i, (nm, lt, rh, mask) in enumerate((('ATrk', K_, R, M_LE), ('ATrb', B_, R, M_LE), ('ATak', K_, A_, M_LT),
                                                    ('XT', B_, A_, M_LT), ('X', A_, B_, M_GT))):
                sl = nslot()
                k.mm(sl, lt, rh)
                d = pers[3 + i] if i < 3 else nsmb()
                k.tt(d, sl, mask, ALU.mult)
                out[nm] = d
            X, XT = out['X'], out['XT']
            TT = nsmb()
            k.tt(TT, XT, IDB, ALU.add, eng='pool')

            def tt_update(TT, Xf, final):
                sl3 = nslot()
                k.mm(sl3, Xf, TT)
                TTn = pers[6] if final else nsmb()
                k.tt(TTn, sl3, TT, ALU.add)
                return TTn
            for L in range(1, 6):
                sl = nslot()
                k.mm(sl, XT, X)
                Xn = nsmb()
                k.cp(Xn, sl, eng='act')
                XTn = None
                if L < 5:
                    sl2 = nslot()
                    k.mm(sl2, X, XT)
                    XTn = nsmb()
                    k.cp(XTn, sl2, eng='act')
                if L >= 2:
                    TT = tt_update(TT, X, False)
                X, XT = Xn, XTn
            TT = tt_update(TT, X, True)
            out['TT'] = TT
            return out

        def chunk_seq(c, blk, pre, Hs, HB, pcc, Y):
            R, A_ = blk('R', c), blk('A', c)
            sl = nslot()
            k.mm(sl, A_, HB, start=True, stop=False)
            k.mm(sl, pre['ATak'], pre['VT'], start=False, stop=True)
            Z = nsmb()
            k.cp(Z, sl, eng='act')
            sl = nslot()
            k.mm(sl, pre['TT'], Z)
            U = nsmb()
            k.cp(U, sl, eng='dve')
            sl = nslot()
            k.mm(sl, HB, R, start=True, stop=False)
            k.mm(sl, pre['VT'], pre['ATrk'], start=False, stop=False)
            k.mm(sl, U, pre['ATrb'], start=False, stop=True)
            for hh in range(2):
                lo = hh * 64
                k.cp(Y[lo:lo + 64, c * CH:(c + 1) * CH], sl[lo:lo + 64, lo:lo + 64], eng='act' if hh else 'dve')
            sl = nslot()
            k.mm(sl, pre['KT'], pre['VT'], start=True, stop=False)
            k.mm(sl, pre['BT'], U, start=False, stop=True)
            k.tt(Hs, Hs, sl, ALU.add)
            k.ts(Hs, Hs, pcc, ALU.mult)
            k.cp(HB, Hs, eng='act')

        def rwkv_post_gen(l, hp, tile, Y, gg, bon, tmp, sample=False):
            (t0, lc, n, kind) = tile
            if sample:
                gg = T(SGt[:, 0, hp, :], ['SG'])
                bon = T(SGt[:, 1, hp, :], ['SG'])
            ysq = tmp[:, 0:n]
            k.act(ysq, Y, AF.Square)
            yield
            pm = nbank()
            pq = nbank()
            k.mm(pm[:, 0:n], BONES64, Y)
            k.mm(pq[:, 0:n], BONES64, ysq)
            k.act(ysq, pm[:, 0:n], AF.Square)
            k.tt(ysq, pq[:, 0:n], ysq, ALU.subtract)
            k.tt(Y, Y, pm[:, 0:n], ALU.subtract)
            yield
            k.act(ysq, ysq, AF.Ln, bias=T(SMALLt[:, 1:2], ['SMALL1']))
            yield
            k.act(ysq, ysq, AF.Exp, scale=-0.5)
            yield
            k.tt(Y, Y, ysq, ALU.mult)
            yield
            k.ts(Y, Y, par(l, 9, hp), ALU.mult, par(l, 10, hp), ALU.add)
            k.tt(Y, Y, bon, ALU.add)
            yield
            k.tt(YB(hp, lc, n), Y, gg, ALU.mult)
            yield

        def rwkv_post(l, hp, tile, Y, vx, gg, bon, sample=False):
            for _ in rwkv_post_gen(l, hp, tile, Y, gg, bon, wp(18), sample=sample):
                pass

        SGt = sb("SG", [128, 2, 8, NS], F32)

        def sample_rwkv_state(l):
            for j in range(6):
                tm = wpspan(0, 2).v(lambda a: a.rearrange("p a b -> p (a b)"))[0:NS, 0:D]
                for half in range(2):
                    pb = nbank()
                    for h4 in range(4):
                        hp = half * 4 + h4
                        k.tr(pb[0:NS, h4 * 128:(h4 + 1) * 128], T(SVt[:, j, hp, :], ['SV']), IDF)
                    k.cp(tm[:, half * 512:(half + 1) * 512], pb[0:NS, :], eng='act')
                k.dma(T(scr[j].rearrange("(b f) -> b f", b=NS), [('scr', j)]), tm)
                k.dma(T(SV2t[:, j, :], ['SV2']), T(scr[j].rearrange("(p f) -> p f", p=128), [('scr', j)]))
            sv = lambda j: T(SV2t[:, j, :], ['SV2'])
            ysm = T(WPt[:, 19, 0:128], [('wp', 19, s_) for s_ in range(4)])
            for hh in range(2):
                for vh in range(2):
                    ST = wpspan(0, 4).v(lambda a: a.rearrange("p a b -> p (a b)")[:, 0:2048].rearrange("p (v k) -> p v k", k=64))
                    TM = wpspan(4, 4).v(lambda a: a.rearrange("p a b -> p (a b)")[:, 0:2048].rearrange("p (v k) -> p v k", k=64))
                    off = hh * 4096 + vh * 2048
                    k.dma(ST, srw[l, :, off:off + 2048].rearrange("p (v k) -> p v k", k=64))
                    kb = lambda j: sv(j).v(lambda a: a[:, hh * 64:(hh + 1) * 64].unsqueeze(1).to_broadcast([128, 32, 64]))
                    vs = sv(2).v(lambda a: a[:, hh * 64 + vh * 32:hh * 64 + vh * 32 + 32])
                    k.tt(TM, ST, kb(4), ALU.mult)
                    sa = T(SMALLt[:, 24:56], ['SMALLsa'])
                    k.red(sa, TM, ALU.add)
                    k.tt(ST, ST, kb(3), ALU.mult)
                    k.tt(TM, sa.v(lambda a: a.unsqueeze(2).to_broadcast([128, 32, 64])), kb(5), ALU.mult)
                    k.tt(ST, ST, TM, ALU.add)
                    k.tt(TM, vs.v(lambda a: a.unsqueeze(2).to_broadcast([128, 32, 64])), kb(1), ALU.mult)
                    k.tt(ST, ST, TM, ALU.add)
                    k.dma(o_srw[l, :, off:off + 2048].rearrange("p (v k) -> p v k", k=64), ST)
                    k.tt(TM, ST, kb(0), ALU.mult)
                    k.red(ysm[:, hh * 64 + vh * 32:hh * 64 + vh * 32 + 32], TM, ALU.add)
            k.dma(T(scr[6].rearrange("(p f) -> p f", p=128), [('scr', 6)]), ysm)
            tm = wpspan(0, 2).v(lambda a: a.rearrange("p a b -> p (a b)"))[0:NS, 0:D]
            k.dma(tm, T(scr[6].rearrange("(b f) -> b f", b=NS), [('scr', 6)]))
            pb = nbank()
            for hp in range(KC):
                k.tr(pb[:, hp * NS:(hp + 1) * NS], tm[:, hp * 128:(hp + 1) * 128], IDF[0:NS, 0:NS])
            k.cp(T(YSt[:], ['YS']), pb[:, 0:KC * NS].v(lambda a: a.rearrange("p (c b) -> p c b", b=NS)), eng='act')

        def lru_gen(l, c, tile, Wu, Wy, last_p, base):
            (t0, lc, n, kind) = tile
            pu = proj(Wu, lc, n)
            ue = wp(base + 0)
            yv = wp(base + 8, n)
            cw = lambda j: par(l, 11 + j, c)
            xc = wp(base + 1, n)
            if kind == 'p':
                k.cp(ue[:, 3:3 + n], pu[:, 0:n], eng='act')
                cvc = T(CVCt[:, l, c, :], [('CVC', l, c)])
                k.cp(ue[:, 0:3], cvc, eng='pool')
                k.cp(cvc, ue[:, n:n + 3], eng='pool')
                if last_p:
                    k.cp(T(CVOt[:, c, :, 0], ['CVO']), ue[:, n:n + 3], eng='pool')
                k.ts(xc, ue[:, 0:n], cw(0), ALU.mult, par(l, 15, c), ALU.add)
                for j in (1, 2, 3):
                    k.stt(xc, ue[:, j:j + n], cw(j), xc, ALU.mult, ALU.add)
            else:
                k.cp(ue[:, 0:n], pu[:, 0:n], eng='act')
                cs = lambda j: T(CSt[:, c, j, :], ['CS'])
                k.ts(xc, cs(0), cw(0), ALU.mult, par(l, 15, c), ALU.add)
                k.stt(xc, cs(1), cw(1), xc, ALU.mult, ALU.add)
                k.stt(xc, cs(2), cw(2), xc, ALU.mult, ALU.add)
                k.stt(xc, ue[:, 0:n], cw(3), xc, ALU.mult, ALU.add)
                k.cp(T(CVOt[:, c, 0, 1:1 + NS], ['CVO']), cs(1), eng='pool')
                k.cp(T(CVOt[:, c, 1, 1:1 + NS], ['CVO']), cs(2), eng='pool')
                k.cp(T(CVOt[:, c, 2, 1:1 + NS], ['CVO']), ue[:, 0:n], eng='pool')
            yield
            py = proj(Wy, lc, n)
            k.cp(yv, py[:, 0:n], eng='act')
            yield
            xcb = wp(base + 2, n).v(lambda a: a.bitcast(BF16)[:, 0:n])
            k.cp(xcb, xc, eng='act')
            pa = nbank()
            k.mm(pa[:, 0:n], T(WABt[:, 0, c, :], ['WAB']), xcb)
            px = nbank()
            k.mm(px[:, 0:n], T(WABt[:, 1, c, :], ['WAB']), xcb)
            gr = wp(base + 3, n)
            gi = wp(base + 4, n)
            k.act(gr, pa[:, 0:n], AF.Sigmoid, bias=par(l, 16, c))
            k.act(gi, px[:, 0:n], AF.Sigmoid, bias=par(l, 17, c))
            yield
            av = wp(base + 5, n)
            e2 = wp(base + 6, n)
            k.act(av, gr, AF.Exp, scale=par(l, 28, c))
            k.act(e2, gr, AF.Exp, scale=par(l, 29, c))
            k.ts(e2, e2, -1.0, ALU.mult, 1.0, ALU.add)
            k.ts(e2, e2, 0.0, ALU.max)
            yield
            k.act(e2, e2, AF.Sqrt)
            k.tt(gi, gi, xc, ALU.mult)
            k.tt(e2, e2, gi, ALU.mult)
            yield
            hh_ = wp(base + 7, n)
            hc = T(CARt[:, l, 26 + c:27 + c], [('CAR', l, 26 + c)])
            if kind == 'p':
                k.scan(hh_, av, e2, hc)
                k.cp(hc, hh_[:, n - 1:n], eng='pool')
                if last_p:
                    k.cp(T(HOt[:, c, 0:1], ['HO']), hh_[:, n - 1:n], eng='pool')
            else:
                k.tt(hh_, av, T(HS0t[:, c, :], ['HS0']), ALU.mult)
                k.tt(hh_, hh_, e2, ALU.add)
                k.cp(T(HOt[:, c, 1:1 + NS], ['HO']), hh_, eng='pool')
            yield
            t9 = wp(base + 9, n)
            k.tt(t9, yv, yv, ALU.mult, eng='pool')
            k.ts(t9, t9, 0.044715, ALU.mult, 1.0, ALU.add)
            yield
            k.tt(t9, t9, yv, ALU.mult)
            k.act(t9, t9, AF.Sigmoid, scale=1.5957691216057308)
            k.tt(t9, t9, yv, ALU.mult, eng='pool')
            yield
            k.tt(YB(c, lc, n), hh_, t9, ALU.mult)
            yield

        rr['b8'] = 0

        def nbank8():
            b_ = rr['b8']
            rr['b8'] = (b_ + 1) % 8
            return psbank(b_)

        def attn_prompt(l, tile):
            (t0, lc, n, kind) = tile
            KT = lambda e: T(KTt[:, l, e, :], [('KT', l)])
            nsub = n // 128

            def bufs(P):
                ex = wpspan(4 * P, 2).v(lambda a: a.rearrange("p a b -> p (a b)")[:, 0:1024].rearrange("p (h m) -> p h m", h=4))
                pbf = wp(4 * P + 2).v(lambda a: a.bitcast(BF16)[:, 0:1024].rearrange("p (h m) -> p h m", h=4))
                PTs = wp(4 * P + 3).v(lambda a: a.bitcast(BF16)[:, 0:1024])
                mx = T(SMALLt[:, 2:6], ['SMALLmx']) if P == 0 else T(SMALLt[:, 56:60], ['SMALLmxb'])
                sm = T(SMALLt[:, 6:10], ['SMALLsm']) if P == 0 else T(SMALLt[:, 60:64], ['SMALLsmb'])
                return ex, pbf, PTs, mx, sm

            def stageA(sub):
                P = sub % 2
                c0_ = lc + sub * 128
                ex, pbf, PTs, mx, sm = bufs(P)
                pbs = []
                for hpair in range(2):
                    pb = nbank8()
                    pbs.append(pb)
                    for h2 in range(2):
                        h = hpair * 2 + h2
                        for dc in range(2):
                            q = T(YBt[:, 2 * h + dc, c0_:c0_ + 128], [('YB', 2 * h + dc, lc)])
                            k.mm(pb[:, h2 * 256:(h2 + 1) * 256], q, KT(2 * h + dc), start=(dc == 0), stop=(dc == 1))
                for hpair in range(2):
                    k.red(mx[:, hpair * 2:hpair * 2 + 2], pbs[hpair].v(lambda a: a.rearrange("p (h m) -> p h m", h=2)), ALU.max)
                k.ts(mx, mx, -1.0, ALU.mult)
                for h in range(4):
                    k.act(ex[:, h, :], pbs[h // 2][:, (h % 2) * 256:(h % 2 + 1) * 256], AF.Exp, bias=mx[:, h:h + 1], accum=sm[:, h:h + 1])

            def stageB(sub):
                P = sub % 2
                c0_ = lc + sub * 128
                ex, pbf, PTs, mx, sm = bufs(P)
                k.recip(sm, sm)
                for h in range(4):
                    k.ts(pbf[:, h, :], ex[:, h, :], sm[:, h:h + 1], ALU.mult)
                ptb = nbank8()
                ptv = ptb.v(lambda a: a.bitcast(BF16))
                for h in range(4):
                    for mc in range(2):
                        j = h * 2 + mc
                        k.tr(ptv[:, j * 128:(j + 1) * 128], pbf[:, h, mc * 128:(mc + 1) * 128], IDB)
                k.cp(PTs, ptv[:, 0:1024], eng='act')
                for half in range(2):
                    po = nbank8()
                    for j in range(4):
                        ec = half * 4 + j
                        h = ec // 2
                        for mc in range(2):
                            k.mm(po[:, j * 128:(j + 1) * 128], T(VNt[:, l, mc, ec * 128:(ec + 1) * 128], [('VN', l)]),
                                 PTs[:, (h * 2 + mc) * 128:(h * 2 + mc + 1) * 128], start=(mc == 0), stop=(mc == 1))
                    dst = T(EBv[:, half * 4:half * 4 + 4, c0_:c0_ + 128], [('EB', c, lc) for c in range(half * 4, half * 4 + 4)])
                    k.cp(dst, po.v(lambda a: a.rearrange("p (j t) -> p j t", j=4)), eng='act' if half else 'dve')
            stageA(0)
            for sub in range(nsub):
                if sub + 1 < nsub:
                    stageA(sub + 1)
                stageB(sub)

        def attn_sample(l, tile):
            (t0, lc, n, kind) = tile
            pb = nbank()
            pb2 = nbank()
            for c in range(KC):
                tgt = pb if c < 4 else pb2
                k.tr(tgt[0:NS, (c % 4) * 128:(c % 4 + 1) * 128], T(YSt[:, c, :], ['YS']), IDF)
            tm = wpspan(0, 2).v(lambda a: a.rearrange("p a b -> p (a b)"))[0:NS, 0:D]
            k.cp(tm[:, 0:512], pb[0:NS, :], eng='act')
            k.cp(tm[:, 512:1024], pb2[0:NS, :], eng='act')
            k.dma(T(scr[7].rearrange("(b f) -> b f", b=NS), [('scr', 7)]), tm)
            SCT = wp(19, 128).v(lambda a: a.rearrange("p (m b h) -> p m b h", m=2, b=NS))
            for b in range(NS):
                qb = wpspan(2 + 2 * (b % 2), 2).v(lambda a: a.rearrange("p a b -> p (a b)")[:, 0:D])
                k.dma(qb, T(scr[7][b * D:(b + 1) * D].partition_broadcast(128), [('scr', 7)]))
                kb = wpspan(6 + 4 * (b % 2), 4).v(lambda a: a.rearrange("p a b -> p (a b)")[:, 0:2048].rearrange("p (m f) -> p m f", m=2))
                k.dma(kb, ck[l, b].rearrange("(m p) f -> p m f", p=128))
                for mc in range(2):
                    pr = wpspan(14, 2).v(lambda a: a.rearrange("p a b -> p (a b)")[:, 0:D])
                    k.tt(pr, kb[:, mc, :], qb, ALU.mult)
                    k.red(SCT[:, mc, b, :], pr.v(lambda a: a.rearrange("p (h d) -> p h d", h=4)), ALU.add)
            pbT = nbank()
            for mc in range(2):
                k.tr(pbT[0:64, mc * 128:(mc + 1) * 128], SCT[:, mc].v(lambda a: a.rearrange("p b h -> p (b h)")), IDF)
            sc = wp(16, 256)[0:64]
            k.cp(sc, pbT[0:64, 0:256], eng='act')
            mx = T(SMALLt[0:64, 10:11], ['SMALLmx2'])
            sm = T(SMALLt[0:64, 11:12], ['SMALLsm2'])
            k.red(mx, sc, ALU.max)
            k.ts(mx, mx, -1.0, ALU.mult)
            k.act(sc, sc, AF.Exp, bias=mx, accum=sm)
            k.recip(sm, sm)
            k.ts(sc, sc, sm, ALU.mult)
            pbP = nbank()
            for mc in range(2):
                k.tr(pbP[:, mc * 64:(mc + 1) * 64], sc[:, mc * 128:(mc + 1) * 128], IDF[0:64, 0:64])
            PTS = wp(17, 128).v(lambda a: a.rearrange("p (m j) -> p m j", m=2))
            k.cp(PTS, pbP[:, 0:128].v(lambda a: a.rearrange("p (m j) -> p m j", m=2)), eng='act')
            po = nbank()
            for b in range(NS):
                vb = wpspan(6 + 4 * (b % 2), 4).v(lambda a: a.rearrange("p a b -> p (a b)")[:, 0:2048].rearrange("p (m f) -> p m f", m=2))
                k.dma(vb, cv[l, b].rearrange("(m p) f -> p m f", p=128))
                for ec in range(KC):
                    h = ec // 2
                    for mc in range(2):
                        k.mm(po[:, ec * NS + b:ec * NS + b + 1], vb[:, mc, ec * 128:(ec + 1) * 128],
                             PTS[:, mc, b * 4 + h:b * 4 + h + 1], start=(mc == 0), stop=(mc == 1))
            dst = T(EBv[:, :, lc:lc + NS], [('EB', c, lc) for c in range(KC)])
            k.cp(dst, po[:, 0:KC * NS].v(lambda a: a.rearrange("p (c b) -> p c b", b=NS)), eng='act')

        S.dry = True
        try:
            program()
        except Stop:
            pass
        S.reset()
        rr['bank'] = 0
        rr['slot'] = 0
        rr['smb'] = 0
        rr['qb'] = 0
        rr['b8'] = 0
        rr['nbmod'] = 8
        wq.start_emit()
        try:
            program()
        except Stop:
            pass
        S.finish('sp')
        S.emit_all()
        nc._sched_stats = {q: len(v) for q, v in S.streams.items()}
        nc._sched_stats['waits'] = S.n_wait
        nc._marks = getattr(S, 'marks', [])
    return nc


_NC_CACHE = {}


def _get_nc(debug=False):
    if debug not in _NC_CACHE:
        _NC_CACHE[debug] = build_nc(debug)
    return _NC_CACHE[debug]


def make_in_maps(inp):
    f = lambda a: np.ascontiguousarray(np.asarray(a, dtype=np.float32))
    prow = np.zeros((2 * NRL, D), np.float32)
    for l in range(2):
        rows = []
        mu = f(inp['mu_shift'])[l]
        rows += [mu[0:1024], mu[1024:2048], mu[2048:3072], np.concatenate([mu[3072:3328], np.zeros(768, np.float32)])]
        rows += [f(inp['rw_w0'])[l], f(inp['rw_a0'])[l], f(inp['rw_k_k'])[l].reshape(-1), f(inp['rw_k_a'])[l].reshape(-1),
                 f(inp['rw_r_k'])[l].reshape(-1), f(inp['rw_lnx_g'])[l].reshape(-1), f(inp['rw_lnx_b'])[l].reshape(-1)]
        cw = f(inp['lru_conv_w'])[l]
        rows += [cw[0], cw[1], cw[2], cw[3], f(inp['lru_conv_b'])[l], f(inp['lru_ba'])[l], f(inp['lru_bx'])[l], f(inp['lru_lambda'])[l]]
        gb = f(inp['mix_gate_b'])[l]
        rows += [gb[0], gb[1]]
        rows += [f(inp[n])[l] for n in ('ln1_g', 'ln1_b', 'ln2_g', 'ln2_b', 'ln3_g', 'ln3_b')]
        assert len(rows) == NRL
        for j, r in enumerate(rows):
            prow[l * NRL + j] = r
    shared = {n: f(inp[n]) for n in ('w_in', 'rw_w2', 'rw_a2', 'rw_g2', 'rw_proj', 'lru_wa', 'lru_wx', 'lru_proj', 'w_out_mix',
                                     'xa_wq', 'xa_wk', 'xa_wv', 'xa_wo', 'mlp_up', 'mlp_down')}
    shared['prow'] = prow
    xp = f(inp['x_prompt'])
    xs = f(inp['x_sample']).reshape(128, D)
    mem = f(inp['mem_prompt'])
    ck = f(inp['cache_mem_k']).reshape(2, 128, 256, D)
    cv = f(inp['cache_mem_v']).reshape(2, 128, 256, D)
    srw = f(inp['state_rwkv']).reshape(2, 128, 16 * 64 * 64)
    ssh = f(inp['state_rwkv_shift'])
    shh = f(inp['state_lru_h'])
    scv = f(inp['state_lru_conv'])
    maps = []
    for b in range(8):
        sl = slice(b * NS, (b + 1) * NS)
        m = dict(shared)
        m['xp'] = xp[b]
        m['xs'] = np.ascontiguousarray(xs[sl])
        m['mem'] = mem[b]
        m['ck'] = np.ascontiguousarray(ck[:, sl])
        m['cv'] = np.ascontiguousarray(cv[:, sl])
        m['srw'] = np.ascontiguousarray(srw[:, sl]).reshape(2, 128, 8192)
        m['ssh'] = np.ascontiguousarray(ssh[:, sl])
        m['shh'] = np.ascontiguousarray(shh[:, sl])
        m['scv'] = np.ascontiguousarray(scv[:, sl])
        maps.append(m)
    return maps


def gather(results):
    R = results
    cat = lambda name, ax: np.concatenate([r[name] for r in R], axis=ax)
    y_p = np.stack([r['o_yp'] for r in R], 0)
    y_s = cat('o_ys', 0).reshape(128, 1, D)
    p_rw = np.stack([r['o_prw'] for r in R], 1)
    p_sh = np.stack([r['o_psh'] for r in R], 1)
    p_h = np.stack([r['o_ph'] for r in R], 1)
    p_cv = np.stack([r['o_pcv'] for r in R], 1)
    mk = np.stack([r['o_mk'] for r in R], 1).reshape(2, 8, 256, 4, 256)
    mv = np.stack([r['o_mv'] for r in R], 1).reshape(2, 8, 256, 4, 256)
    s_rw = np.concatenate([r['o_srw'].reshape(2, NS, 16, 64, 64) for r in R], axis=1)
    s_sh = cat('o_ssh', 1)
    s_h = cat('o_sh', 1)
    s_cv = cat('o_scv', 1)
    outs = (y_p, y_s, p_rw, p_sh, p_h, p_cv, mk, mv, s_rw, s_sh, s_h, s_cv)
    return tuple(np.ascontiguousarray(o, dtype=np.float32) for o in outs)


def kernel(**inputs):
    nc = _get_nc(False)
    maps = make_in_maps(inputs)
    res = run_bass_kernel_spmd(nc, maps, core_ids=list(range(8)))
    return gather(res.results)
```
